# Optimizing a Trainium2 kernel written in Bass

```python
import math
import jax, jax.numpy as jnp
from jax import lax
import numpy as np


D_MODEL = 1024
BATCH = 2
SEQ = 8192
DEPTH = 1

MEM_LEN = 256
RMS_EPS = 1e-6
RW_HEADS = 8
RW_HEAD_DIM = 64
RW_WIDTH = RW_HEADS * RW_HEAD_DIM
DECAY_LORA = 64
AAA_LORA = 64
LNX_EPS = 64e-5
L2_EPS = 1e-12
SW_HEADS = 8
SW_KV_HEADS = 2
SW_GROUP = SW_HEADS // SW_KV_HEADS
SW_HEAD_DIM = 64
SW_WIDTH = SW_HEADS * SW_HEAD_DIM
SW_KV_WIDTH = SW_KV_HEADS * SW_HEAD_DIM
WINDOW = 128
BLOCK = 128
ROPE_THETA = 10000.0
NEG_INF = -1e30
X_HEADS = 4
X_HEAD_DIM = 128
X_WIDTH = X_HEADS * X_HEAD_DIM
N_BRANCHES = 3

IN_SPLITS = (RW_WIDTH, RW_WIDTH, RW_WIDTH, DECAY_LORA, AAA_LORA, RW_WIDTH,
             SW_WIDTH, SW_KV_WIDTH, SW_KV_WIDTH, SW_WIDTH,
             X_WIDTH, X_WIDTH,
             N_BRANCHES * D_MODEL)
IN_WIDTH = sum(IN_SPLITS)

kernel_name = 'hybrid_rwkv7_swa_sink_memxattn_gated'


def rms_norm(x, g):
    xf = x.astype(jnp.float32)
    y = xf * lax.rsqrt(jnp.mean(xf * xf, axis=-1, keepdims=True) + RMS_EPS)
    return (y * g.astype(jnp.float32)).astype(x.dtype)


def token_shift_mix(u, mu):
    prev = jnp.pad(u, ((0, 0), (1, 0), (0, 0)))[:, :-1]
    return u + (prev - u) * mu


def rope(x, pos):
    d = x.shape[-1]
    half = d // 2
    inv = ROPE_THETA ** (-jnp.arange(half, dtype=jnp.float32) / half)
    ang = pos.astype(jnp.float32)[..., None] * inv
    cos = jnp.cos(ang)[:, :, None, :]
    sin = jnp.sin(ang)[:, :, None, :]
    xf = x.astype(jnp.float32)
    x1, x2 = xf[..., :half], xf[..., half:]
    return jnp.concatenate([x1 * cos - x2 * sin, x2 * cos + x1 * sin], axis=-1).astype(x.dtype)


def rwkv7_scan(r, w, k, v, a, b):
    def step(S, inp):
        r_t, w_t, k_t, v_t, a_t, b_t = inp
        sa = jnp.einsum('bhvk,bhk->bhv', S, a_t)
        S = S * w_t[:, :, None, :] + sa[..., None] * b_t[:, :, None, :] + v_t[..., None] * k_t[:, :, None, :]
        y = jnp.einsum('bhvk,bhk->bhv', S, r_t)
        return S, y
    bsz, _, h, n = r.shape
    xs = tuple(jnp.moveaxis(t, 1, 0) for t in (r, w, k, v, a, b))
    s0 = jnp.zeros((bsz, h, n, n), jnp.float32)
    _, y = lax.scan(step, s0, xs)
    return jnp.moveaxis(y, 0, 1)


def rwkv7_mix(p_r, p_k, p_v, p_w, p_a, mu_rkv, mu_wa, w0, w2, a0, a2, k_k, k_a, r_k, lnx_g, lnx_b):
    f32 = jnp.float32
    bsz, t, _ = p_r.shape
    r = token_shift_mix(p_r, mu_rkv[0]).astype(f32)
    k = token_shift_mix(p_k, mu_rkv[1]).astype(f32)
    v = token_shift_mix(p_v, mu_rkv[2]).astype(f32)
    lw = token_shift_mix(p_w, mu_wa[0]).astype(f32)
    la = token_shift_mix(p_a, mu_wa[1]).astype(f32)
    w_log = -jax.nn.softplus(-(w0.astype(f32) + jnp.tanh(lw) @ w2.astype(f32))) - 0.5
    decay = jnp.exp(-jnp.exp(w_log))
    a = jax.nn.sigmoid(a0.astype(f32) + la @ a2.astype(f32))
    heads = lambda u: u.reshape(bsz, t, RW_HEADS, RW_HEAD_DIM)
    kk = heads(k * k_k.astype(f32))
    kk = kk / jnp.maximum(jnp.linalg.norm(kk, axis=-1, keepdims=True), L2_EPS)
    k = k * (1.0 + (a - 1.0) * k_a.astype(f32))
    rh, kh, vh, ah = heads(r), heads(k), heads(v), heads(a)
    y = rwkv7_scan(rh, heads(decay), kh, vh, -kk, kk * ah)
    mean = jnp.mean(y, axis=-1, keepdims=True)
    var = jnp.mean(jnp.square(y - mean), axis=-1, keepdims=True)
    y = ((y - mean) * lax.rsqrt(var + LNX_EPS)).reshape(bsz, t, RW_WIDTH) * lnx_g.astype(f32) + lnx_b.astype(f32)
    bonus = jnp.sum(rh * kh * r_k.astype(f32), axis=-1, keepdims=True) * vh
    return y + bonus.reshape(bsz, t, RW_WIDTH)


def sliding_window_sink_attn(q, k, v, pos, q_g, k_g, sinks):
    bsz, t, _ = q.shape
    nb = t // BLOCK
    q = rope(rms_norm(q.reshape(bsz, t, SW_HEADS, SW_HEAD_DIM), q_g), pos)
    k = rope(rms_norm(k.reshape(bsz, t, SW_KV_HEADS, SW_HEAD_DIM), k_g), pos)
    v = v.reshape(bsz, t, SW_KV_HEADS, SW_HEAD_DIM)
    qb = q.reshape(bsz, nb, BLOCK, SW_KV_HEADS, SW_GROUP, SW_HEAD_DIM)
    def band(u):
        ub = u.reshape(bsz, nb, BLOCK, SW_KV_HEADS, SW_HEAD_DIM)
        prev = jnp.pad(ub, ((0, 0), (1, 0), (0, 0), (0, 0), (0, 0)))[:, :-1]
        return jnp.concatenate([prev, ub], axis=2)
    kband, vband = band(k), band(v)
    scale = SW_HEAD_DIM ** -0.5
    s = jnp.einsum('bnqhgd,bnkhd->bnhgqk', qb, kband).astype(jnp.float32) * scale
    qi = jnp.arange(BLOCK)[:, None]
    kj = jnp.arange(2 * BLOCK)[None, :]
    rel = qi - kj + BLOCK
    allowed = (rel >= 0) & (rel < WINDOW)
    not_pad = (jnp.arange(nb)[:, None, None] > 0) | (kj >= BLOCK)[None]
    mask = allowed[None] & not_pad
    s = jnp.where(mask[None, :, None, None], s, NEG_INF)
    sink = jnp.broadcast_to(sinks.astype(jnp.float32).reshape(SW_KV_HEADS, SW_GROUP)[None, None, :, :, None, None],
                            s.shape[:-1] + (1,))
    p = jax.nn.softmax(jnp.concatenate([s, sink], axis=-1), axis=-1)[..., :-1]
    o = jnp.einsum('bnhgqk,bnkhd->bnqhgd', p.astype(v.dtype), vband)
    return o.reshape(bsz, t, SW_WIDTH)


def memory_cross_attn(q, mem, mem_norm_g, w_mem_kv, xq_g, xk_g):
    bsz, t, _ = q.shape
    m = mem.shape[1]
    q = rms_norm(q.reshape(bsz, t, X_HEADS, X_HEAD_DIM), xq_g)
    kv = rms_norm(mem, mem_norm_g) @ w_mem_kv
    km, vm = kv[..., :X_WIDTH], kv[..., X_WIDTH:]
    km = rms_norm(km.reshape(bsz, m, X_HEADS, X_HEAD_DIM), xk_g)
    vm = vm.reshape(bsz, m, X_HEADS, X_HEAD_DIM)
    s = jnp.einsum('bthd,bmhd->bhtm', q, km).astype(jnp.float32) * (X_HEAD_DIM ** -0.5)
    p = jax.nn.softmax(s, axis=-1)
    o = jnp.einsum('bhtm,bmhd->bthd', p.astype(vm.dtype), vm)
    return o.reshape(bsz, t, X_WIDTH)


def setup_inputs(seed: int = 0) -> dict:
    key = jax.random.key(seed)
    ks = iter(jax.random.split(key, 40))
    f32 = jnp.float32
    def nrm(shape, scale):
        return jax.random.normal(next(ks), shape, f32) * scale
    L = DEPTH
    x = jax.random.normal(next(ks), (BATCH, SEQ, D_MODEL), f32)
    mem = jax.random.normal(next(ks), (BATCH, MEM_LEN, D_MODEL), f32)
    offset = jax.random.randint(next(ks), (BATCH, 1), 0, 4096, dtype=jnp.int32)
    positions = jnp.arange(SEQ, dtype=jnp.int32)[None, :] + offset
    return {
        'x': x,
        'mem': mem,
        'positions': positions,
        'norm_g': 1.0 + nrm((L, D_MODEL), 0.05),
        'mem_norm_g': 1.0 + nrm((L, D_MODEL), 0.05),
        'w_in': nrm((L, D_MODEL, IN_WIDTH), D_MODEL ** -0.5),
        'mu_rkv': jax.random.uniform(next(ks), (L, 3, RW_WIDTH), f32),
        'mu_wa': jax.random.uniform(next(ks), (L, 2, DECAY_LORA), f32),
        'w0': jnp.linspace(-6.0, -1.0, RW_WIDTH, dtype=f32)[None, :] + nrm((L, RW_WIDTH), 0.1),
        'w2': nrm((L, DECAY_LORA, RW_WIDTH), 0.5 * DECAY_LORA ** -0.5),
        'a0': nrm((L, RW_WIDTH), 0.1),
        'a2': nrm((L, AAA_LORA, RW_WIDTH), 0.5 * AAA_LORA ** -0.5),
        'k_k': 0.85 + nrm((L, RW_WIDTH), 0.05),
        'k_a': 1.0 + nrm((L, RW_WIDTH), 0.05),
        'r_k': nrm((L, RW_HEADS, RW_HEAD_DIM), 0.1),
        'lnx_g': 1.0 + nrm((L, RW_WIDTH), 0.05),
        'lnx_b': nrm((L, RW_WIDTH), 0.01),
        'q_norm_g': 1.0 + nrm((L, SW_HEAD_DIM), 0.05),
        'k_norm_g': 1.0 + nrm((L, SW_HEAD_DIM), 0.05),
        'sinks': nrm((L, SW_HEADS), 1.0),
        'xq_norm_g': 1.0 + nrm((L, X_HEAD_DIM), 0.05),
        'xk_norm_g': 1.0 + nrm((L, X_HEAD_DIM), 0.05),
        'w_mem_kv': nrm((L, D_MODEL, 2 * X_WIDTH), D_MODEL ** -0.5),
        'w_proj_a': nrm((L, RW_WIDTH, D_MODEL), RW_WIDTH ** -0.5),
        'w_proj_b': nrm((L, SW_WIDTH, D_MODEL), SW_WIDTH ** -0.5),
        'w_proj_c': nrm((L, X_WIDTH, D_MODEL), X_WIDTH ** -0.5),
        'w_out': nrm((L, D_MODEL, D_MODEL), D_MODEL ** -0.5),
    }


def reference(x, mem, positions, norm_g, mem_norm_g, w_in, mu_rkv, mu_wa, w0, w2, a0, a2, k_k, k_a, r_k,
              lnx_g, lnx_b, q_norm_g, k_norm_g, sinks, xq_norm_g, xk_norm_g, w_mem_kv,
              w_proj_a, w_proj_b, w_proj_c, w_out):
    bsz, t, _ = x.shape
    split_points = [int(i) for i in np.cumsum(IN_SPLITS)[:-1]]
    for l in range(DEPTH):
        h = rms_norm(x, norm_g[l])
        p = h @ w_in[l]
        (p_r, p_k, p_v, p_w, p_a, z_a,
         p_q, p_sk, p_sv, z_b,
         p_xq, z_c, p_gate) = jnp.split(p, split_points, axis=-1)
        y_a = rwkv7_mix(p_r, p_k, p_v, p_w, p_a, mu_rkv[l], mu_wa[l], w0[l], w2[l], a0[l], a2[l],
                        k_k[l], k_a[l], r_k[l], lnx_g[l], lnx_b[l]).astype(x.dtype) * jax.nn.silu(z_a)
        y_b = sliding_window_sink_attn(p_q, p_sk, p_sv, positions, q_norm_g[l], k_norm_g[l],
                                       sinks[l]) * jax.nn.silu(z_b)
        y_c = memory_cross_attn(p_xq, mem, mem_norm_g[l], w_mem_kv[l], xq_norm_g[l],
                                xk_norm_g[l]) * jax.nn.silu(z_c)
        gates = jax.nn.sigmoid(p_gate.astype(jnp.float32)).reshape(bsz, t, N_BRANCHES, D_MODEL)
        merged = (gates[:, :, 0] * (y_a @ w_proj_a[l]) + gates[:, :, 1] * (y_b @ w_proj_b[l])
                  + gates[:, :, 2] * (y_c @ w_proj_c[l]))
        x = x + (merged.astype(x.dtype) @ w_out[l]).astype(x.dtype)
    return x
```

```python
from contextlib import ExitStack
import numpy as np
import ml_dtypes
import concourse.bass as bass
import concourse.mybir as mybir
from concourse.bass_utils import run_bass_kernel_spmd

F32 = mybir.dt.float32
BF16 = mybir.dt.bfloat16
I32 = mybir.dt.int32
AF = mybir.ActivationFunctionType
OP = mybir.AluOpType

NCORES = 8
D = 1024
T = 8192
OWN = 2048
TG = 256
NT4 = TG // 128
NG = T // TG
G_OWN0 = NG - OWN // TG
RMS_EPS = 1e-6
LNX_EPS = 64e-5
C0 = -float(np.exp(-0.5))
PI = float(np.pi)
FILL_MIN, FILL_CAP, FILL_SCALE = 300.0, 24, 1.3
FILL_DENSITY = 1.3
SCHED_BUCKET = 150.0
FILL_ALL = False
Y_PREFIX_6 = True
PREFIX_W_ACC = True
OWN_W_ACC = False
POOL_RECIP = False
PE_COLD_GHZ = 1.2
PE_WARM_GHZ = 1.8

PCN = {}
def _pcdef():
    o = 0
    for name, n in [("mu_r", 4), ("mu_k", 4), ("mu_l", 1), ("w0", 4), ("a0", 4), ("k_k", 4), ("k_a", 4), ("r_k", 4),
                    ("lnx_g", 4), ("lnx_b", 4), ("qg", 1), ("kg", 1), ("sink", 4), ("xqg", 1), ("xkg", 1), ("invf", 1),
                    ("fm", 1)]:
        PCN[name] = (o, n)
        o += n
    return o
NPC = _pcdef()

CMN = {}
def _cmdef():
    o = 0
    for name, n in [("ident", 128), ("blk64", 128), ("ones", 128), ("msu", 256), ("miu", 256), ("msl", 256),
                    ("mcu", 512), ("mpl", 512), ("rot", 128), ("d0", 64)]:
        CMN[name] = (o, n)
        o += n
    return o
NCM = _cmdef()


def _consts():
    cm = np.zeros((128, NCM), np.float32)
    def put(name, a):
        o, n = CMN[name]
        assert a.shape == (128, n), (name, a.shape)
        cm[:, o:o + n] = a
    i = np.arange(128)
    r, c = i[:, None], i[None, :]
    same = (r // 64) == (c // 64)
    put("ident", (r == c).astype(np.float32))
    put("blk64", same.astype(np.float32))
    put("ones", np.ones((128, 128), np.float32))
    put("msu", np.tile(((r < c) & same).astype(np.float32), (1, 2)))
    put("miu", np.tile(((r <= c) & same).astype(np.float32), (1, 2)))
    put("msl", np.tile(((r > c) & same).astype(np.float32), (1, 2)))
    put("mcu", np.tile((c >= r).astype(np.float32), (1, 4)))
    put("mpl", np.tile((c < r).astype(np.float32), (1, 4)))
    rot = np.zeros((128, 128), np.float32)
    for cc in range(128):
        if cc % 64 < 32:
            rot[cc + 32, cc] = -1.0
        else:
            rot[cc - 32, cc] = 1.0
    put("rot", rot)
    put("d0", ((r % 64) == np.arange(64)[None, :]).astype(np.float32)[:, :64])
    cf = np.zeros((128, 2 * TG), np.float32)
    rm = np.ones(TG, np.float32)
    rm[::64] = 0.0
    cf[:, 0:TG] = rm[None, :]
    cf[:, TG:2 * TG] = ((np.arange(TG) % 64) + 1).astype(np.float32)[None, :]
    return cm.astype(ml_dtypes.bfloat16), cf


class Sched:
    def __init__(self, nc, stack):
        self.nc, self.stack = nc, stack
        self.names = ["pe", "act", "dve", "pool", "sp"]
        self.prog = {e: [] for e in self.names}
        self.cnt = {e: 0 for e in self.names}
        self.seg = 12000
        self.sems = {e: [] for e in self.names}
        self.seen = {e: {} for e in self.names}
        self.lastw, self.readers = {}, {}
        self.ND = 16
        self.dnext_pool = 0
        self.barrier_names = set()
        self.barrier_done = set()
        self.convsem = stack.enter_context(nc.semaphore("convsem"))
        self.nconv = 0
        self.dsems = [stack.enter_context(nc.semaphore(f"dsem{i}")) for i in range(self.ND)]
        self.dcnt = [0] * self.ND
        self.dlast = [None] * self.ND
        self.dnext = 0
        self.out_tokens = []
        self.rec = None
        self.filler = None
        self.fill_on = False
        self.cur_tset = "A"
        self.tagcost = {}
        self.curtag = ['?']
        self.nfill = 0
        self.fill_ns = 0.0
        self.etime = {e: 0.0 for e in self.names}

    def rec_start(self):
        self.rec = []

    def rec_stop(self):
        r, self.rec = self.rec, None
        return r

    def play(self, lst):
        for it in lst:
            if it[0] == "op":
                self.op(*it[1:5])
            elif it[0] == "conv":
                self._conv_dma(it[1], it[2])
            elif it[0] == "fill":
                for _ in range(it[1]):
                    self.prog["pe"].append(([], self.filler, None, 0))
                self.nfill += it[1]
            else:
                self.dma(*it[1:5])

    def schedule(self, lst):
        n = len(lst)
        lastw, readers = {}, {}
        preds = [set() for _ in range(n)]
        eng, cost, lat = [None] * n, [0.0] * n, [0.0] * n
        tset = [None] * n
        for i, it in enumerate(lst):
            if it[0] == "op":
                _, e, fn, ins, outs, c = it
                ins = [a for a in ins if a is not None and not isinstance(a, (int, float))]
                fs = 1
                for d_ in outs[0].shape[1:]:
                    fs *= int(d_)
                if c is None:
                    if e == "pe":
                        c = max(64, fs) / (PE_WARM_GHZ if self.fill_on else PE_COLD_GHZ) + 8
                    elif e == "act":
                        c = 200 + fs / 1.2
                    elif e == "dve":
                        c = 70 + fs * 1.04
                    else:
                        c = 600 + fs
                eng[i], cost[i], lat[i] = e, c, c + (200 if e == "pe" else 60)
                tset[i] = getattr(fn, "_tset", None)
                tg_ = getattr(fn, "_tag", "?")
                acc_ = self.tagcost.setdefault((tg_, e), [0, 0.0])
                acc_[0] += 1
                acc_[1] += c
            elif it[0] == "conv":
                ins, outs = [], []
                eng[i], cost[i], lat[i] = "pool", 1000.0, 1000.0
            else:
                _, q, out, in_, _io = it
                ins, outs = [in_], [out]
                eng[i], cost[i], lat[i] = q, 80.0, 2500.0
            for a in ins:
                k = self.key(a)
                if k in lastw:
                    preds[i].add(lastw[k])
            for a in outs:
                k = self.key(a)
                if k in lastw:
                    preds[i].add(lastw[k])
                preds[i].update(readers.get(k, ()))
            for a in ins:
                readers.setdefault(self.key(a), set()).add(i)
            for a in outs:
                k = self.key(a)
                lastw[k] = i
                readers[k] = set()
            preds[i].discard(i)
        succs = [[] for _ in range(n)]
        npred = [len(p_) for p_ in preds]
        for i in range(n):
            for p_ in preds[i]:
                succs[p_].append(i)
        blev = [0.0] * n
        for i in range(n - 1, -1, -1):
            m_ = 0.0
            for s_ in succs[i]:
                if blev[s_] > m_:
                    m_ = blev[s_]
            blev[i] = lat[i] + m_
        etime = dict(self.etime)
        t0 = max(etime.values())
        for e in etime:
            etime[e] = max(etime[e], t0 - 2000.0)
        fin = [0.0] * n
        ready = [0.0] * n
        avail = [i for i in range(n) if npred[i] == 0]
        order = []
        while avail:
            best, bi = None, None
            for i in avail:
                st = max(etime[eng[i]], ready[i])
                if tset[i] is not None and self.cur_tset not in tset[i]:
                    st += 1300.0
                key_ = (st // SCHED_BUCKET, -blev[i], i) if SCHED_BUCKET else (st, i)
                if best is None or key_ < best:
                    best, bi = key_, i
            avail.remove(bi)
            st = max(etime[eng[bi]], ready[bi])
            if tset[bi] is not None and self.cur_tset not in tset[bi]:
                self.cur_tset = tset[bi][0]
            if eng[bi] == "pe" and self.filler is not None and self.fill_cap > 0 and self.fill_on:
                gap = (st - etime["pe"]) * self.fill_scale
                if gap > self.fill_min:
                    nf = min(self.fill_cap, int(gap * FILL_DENSITY / 512.0 + 0.5))
                    if nf > 0:
                        order.append(("fill", nf))
            etime[eng[bi]] = st + cost[bi]
            fin[bi] = st + lat[bi]
            order.append(bi)
            for s_ in succs[bi]:
                r = fin[bi] + (120.0 if eng[s_] != eng[bi] else 0.0)
                if r > ready[s_]:
                    ready[s_] = r
                npred[s_] -= 1
                if npred[s_] == 0:
                    avail.append(s_)
        if getattr(self, "dbgwin", 0) > 0:
            self.dbgwin -= 1
            bus = {}
            for i in range(n):
                bus[eng[i]] = bus.get(eng[i], 0.0) + cost[i]
            print("window n=%d span=%.1fus busy=%s" % (n, (max(etime.values()) - t0) / 1000, {k: round(v / 1000, 1) for k, v in bus.items()}))
        self.etime = etime
        return [x if isinstance(x, tuple) else lst[x] for x in order]

    @staticmethod
    def merge(a, b):
        out, i, j = [], 0, 0
        na, nb = len(a), len(b)
        while i < na or j < nb:
            if j >= nb or (i < na and i * nb <= j * na):
                out.append(a[i]); i += 1
            else:
                out.append(b[j]); j += 1
        return out

    def _semfor(self, e, k):
        si = (k - 1) // self.seg
        while len(self.sems[e]) <= si:
            self.sems[e].append(self.stack.enter_context(self.nc.semaphore(f"s_{e}_{len(self.sems[e])}")))
        return self.sems[e][si], (k - 1) % self.seg + 1, si

    @staticmethod
    def key(ap):
        return ap.tensor.name

    def _waits(self, e, toks):
        best = {}
        for t in toks:
            if t is None:
                continue
            if t[0] == "c":
                _, f, k = t
                if f == e and e == "pe":
                    continue
                sem, val, si = self._semfor(f, k)
                kk = ("c", f)
                cur = best.get(kk)
                if cur is None or (si, val) > (cur[0], cur[1]):
                    best[kk] = (si, val, sem)
            else:
                _, i, val = t
                kk = ("d", i)
                cur = best.get(kk)
                if cur is None or val > cur[1]:
                    best[kk] = (0, val, self.dsems[i])
        wl = []
        for kk, (si, val, sem) in best.items():
            if self.seen[e].get(kk, (-1, 0)) >= (si, val):
                continue
            self.seen[e][kk] = (si, val)
            wl.append((sem, val))
        return wl

    def _deps(self, ins, outs):
        toks = []
        for a in ins:
            toks.append(self.lastw.get(self.key(a)))
        for a in outs:
            k = self.key(a)
            toks.append(self.lastw.get(k))
            toks.extend(self.readers.get(k, {}).values())
        return toks

    def _commit(self, tok, rid, ins, outs):
        for a in ins:
            self.readers.setdefault(self.key(a), {})[rid] = tok
        for a in outs:
            k = self.key(a)
            self.lastw[k] = tok
            self.readers[k] = {}

    def op(self, e, fn, ins, outs, cost=None):
        if self.rec is not None:
            try:
                fn._tag = self.curtag[0]
            except Exception:
                pass
            self.rec.append(("op", e, fn, ins, outs, cost))
            return
        ins = [a for a in ins if a is not None and not isinstance(a, (int, float))]
        wl = self._waits(e, self._deps(ins, outs))
        k = self.cnt[e] + 1
        self.cnt[e] = k
        sem, _, _ = self._semfor(e, k)
        self.prog[e].append((wl, fn, sem, 1))
        self._commit(("c", e, k), e, ins, outs)

    def dma(self, q, out, in_, is_output=False):
        if self.rec is not None:
            self.rec.append(("dma", q, out, in_, is_output))
            return
        half = self.ND // 2
        if q == "pool":
            i = half + self.dnext_pool
            self.dnext_pool = (self.dnext_pool + 1) % (self.ND - half)
        else:
            i = self.dnext
            self.dnext = (self.dnext + 1) % half
        toks = self._deps([in_], [out]) + [self.dlast[i]]
        wl = self._waits(q, toks)
        if self.key(in_) in self.barrier_names and q not in self.barrier_done:
            wl = wl + [(self.convsem, 16 * self.nconv)]
            self.barrier_done.add(q)
        self.dcnt[i] += 1
        tok = ("d", i, 16 * self.dcnt[i])
        self.dlast[i] = tok
        self.prog[q].append((wl, lambda eng, o=out, a=in_: eng.dma_start(out=o, in_=a), self.dsems[i], 16))
        self._commit(tok, ("d", i), [in_], [out])
        if is_output:
            self.out_tokens.append(tok)

    def conv_dma(self, out, in_):
        if self.rec is not None:
            self.rec.append(("conv", out, in_))
            return
        self._conv_dma(out, in_)

    def _conv_dma(self, out, in_):
        self.nconv += 1
        self.prog["pool"].append(([], lambda eng, o=out, a=in_: eng.dma_start(out=o, in_=a), self.convsem, 16))

    def finish(self):
        wl = self._waits("sp", self.out_tokens + [t for t in self.dlast if t is not None])
        self.prog["sp"].append((wl, None, None, 0))

    def emit(self):
        nc = self.nc
        engs = {"pe": "tensor", "act": "scalar", "dve": "vector", "pool": "gpsimd", "sp": "sync"}
        with nc.Block() as block:
            for e, bn in engs.items():
                prog = self.prog[e]

                def body(eng, prog=prog):
                    for wl, fn, sem, inc in prog:
                        for ws, wv in wl:
                            eng.wait_ge(ws, wv)
                        if fn is not None:
                            ins_ = fn(eng)
                            if sem is not None:
                                ins_.then_inc(sem, inc)
                getattr(block, bn)(body)


def build_program(n_groups=NG, g_own0=G_OWN0, dbg=False):
    G_OWN0 = g_own0
    nc = bass.Bass("TRN2", target_bir_lowering=False)
    st = ExitStack()
    with st:
        S = Sched(nc, st)

        def dram(name, shape, dt, kind="ExternalInput"):
            return nc.dram_tensor(name, list(shape), dt, kind=kind).ap()
        xw = dram("xw", [T, D], F32)
        xown = dram("xown", [OWN, D], F32)
        posd = dram("pos", [1, OWN + 128], I32)
        memd = dram("mem", [256, D], F32)
        w_in = dram("w_in", [D, 7552], F32)
        w_skd = dram("w_skd", [D, 256], F32)
        w_mkv = dram("w_mkv", [D, 1024], F32)
        w_pa = dram("w_pa", [512, D], F32)
        w_pb = dram("w_pb", [512, D], F32)
        w_pc = dram("w_pc", [512, D], F32)
        w_o = dram("w_o", [D, D], F32)
        w2a2d = dram("w2a2", [128, 512], F32)
        pcd = dram("pc", [128, NPC], F32)
        growd = dram("g_row", [1, D], F32)
        gmrowd = dram("gm_row", [1, D], F32)
        muvd = dram("muv_row", [1, 512], F32)
        cmd = dram("cm", [128, NCM], BF16)
        cfd = dram("cf", [128, 2 * TG], F32)
        yout = dram("y", [OWN, D], F32, kind="ExternalOutput")

        _n = [0]

        _sbytes = [0]
        _sblist = []

        def sb(shape, dt, name=None):
            _n[0] += 1
            _sbytes[0] += int(np.prod(shape[1:])) * (2 if dt == BF16 else 4)
            _sblist.append((int(np.prod(shape[1:])) * (2 if dt == BF16 else 4), name))
            return st.enter_context(nc.sbuf_tensor("sb_" + (name or f"t{_n[0]}"), list(shape), dt))

        class Ring:
            def __init__(self, n, shape, dt, name):
                self.t = [sb(shape, dt, f"{name}{i}") for i in range(n)]
                self.i = 0

            def get(self):
                t = self.t[self.i % len(self.t)]
                self.i += 1
                return t

        psum = [st.enter_context(nc.psum_tensor(f"ps{i}", [128, 512], F32)) for i in range(8)]
        CUR = ["Y"]
        PSB = {"X": [6, 7], "Y": [0, 1, 2, 3, 4], "Y0": [0, 1, 2], "Y1": [3, 4, 5]}
        PSI = {"X": 0, "Y": 0, "Y0": 0, "Y1": 0}
        PSY_BANK = psum[5]
        JUNK = psum[7]
        S.fill_min, S.fill_cap, S.fill_scale = FILL_MIN, FILL_CAP, FILL_SCALE

        XOWN = [False]
        YOWN = [False]

        def PS():
            c = CUR[0]
            if c == "X" and XOWN[0] and FILL_CAP > 0:
                return psum[6]
            banks = PSB[c]
            if c == "Y" and not YOWN[0] and Y_PREFIX_6:
                banks = [0, 1, 2, 3, 4, 5]
            t = psum[banks[PSI[c] % len(banks)]]
            PSI[c] += 1
            return t

        class SRing:
            def __init__(self, nx, ny, shape, dt, name):
                self.r = {"X": Ring(nx, shape, dt, name + "x") if nx else None,
                          "Y": Ring(ny, shape, dt, name + "y") if ny else None}

            def get(self):
                return self.r[CUR[0][0]].get()

        PEC = {}
        TAG = S.curtag
        TAG[0] = "setup"

        def _pec(out):
            fs = 1
            for d_ in out.shape[1:]:
                fs *= int(d_)
            k = TAG[0]
            c = PEC.setdefault(k, [0, 0])
            c[0] += 1
            c[1] += max(64, fs)

        def mm(out, lhsT, rhs, start=True, stop=True):
            _pec(out)
            S.op("pe", lambda e: e.matmul(out, lhsT, rhs, start=start, stop=stop), [lhsT, rhs], [out])

        def tr(out, in_, ident):
            _pec(out)
            S.op("pe", lambda e: e.transpose(out, in_, ident), [in_, ident], [out])

        TSET = {AF.Exp: ("A", "B"), AF.Tanh: ("A",), AF.Ln: ("B",), AF.Sin: ("C",)}

        def act(out, in_, func, bias=None, scale=None, accum=None):
            kw = {}
            if bias is not None:
                kw["bias"] = bias
            if scale is not None:
                kw["scale"] = scale
            if accum is not None:
                kw["accum_out"] = accum
            outs = [out] + ([accum] if accum is not None else [])
            fn_ = lambda e: e.activation(out, in_, func, **kw)
            fn_._tset = TSET.get(func)
            S.op("act", fn_, [in_, bias, scale], outs)

        def tt(eng, out, a, b, op):
            S.op(eng, lambda e: e.tensor_tensor(out, a, b, op), [a, b], [out])

        def ts(eng, out, a, s1, s2, op0, op1=None):
            if op1 is None:
                S.op(eng, lambda e: e.tensor_scalar(out, a, s1, None, op0), [a, s1], [out])
            else:
                S.op(eng, lambda e: e.tensor_scalar(out, a, s1, s2, op0, op1), [a, s1, s2], [out])

        def stt(out, a, s, b, op0, op1):
            S.op("dve", lambda e: e.scalar_tensor_tensor(out, a, s, b, op0, op1), [a, s, b], [out])

        def cp(eng, out, in_):
            if eng == "act":
                act(out, in_, AF.Copy)
            else:
                S.op(eng, lambda e: e.tensor_copy(out, in_), [in_], [out])

        def memset(eng, out, val):
            S.op(eng, lambda e: e.memset(out, val), [], [out])

        _rr = [0]

        def evac_eng():
            _rr[0] += 1
            return "act" if _rr[0] % 2 else "dve"

        cm = sb([128, NCM], BF16, "cm")
        cf = sb([128, 2 * TG], F32, "cf")
        pc = sb([128, NPC], F32, "pc")
        grow = sb([128, D], F32, "grow")
        Ybf = sb([128, 4, TG], BF16, "Ybf")
        Ysq = sb([128, 4, TG], BF16, "Ysq")
        muv = Ybf[:].bitcast(F32).rearrange("p j t -> p (j t)")
        omuv = Ysq[:].bitcast(F32).rearrange("p j t -> p (j t)")
        S.dma("sp", cm[:], cmd)
        S.dma("sp", cf[:], cfd)
        S.dma("sp", pc[:], pcd)
        S.dma("sp", grow[:], gmrowd.partition_broadcast(128))
        S.dma("sp", muv, muvd.partition_broadcast(128))

        def C(name):
            o, n = CMN[name]
            return cm[:, o:o + n]

        def PCc(name, j=0):
            o, n = PCN[name]
            return pc[:, o + j:o + j + 1]
        ident = C("ident")
        S.filler = lambda e: e.matmul(JUNK[:, 0:512], ident, cm[:, 0:512], start=True, stop=True)
        blk64 = C("blk64")
        ones = C("ones")
        rmask = cf[:, 0:TG]
        idx1 = cf[:, TG:2 * TG]

        def rsq(out, in_):
            act(out, in_, AF.Ln)
            act(out, out, AF.Exp, scale=-0.5)

        ts("dve", omuv, muv, -1.0, 1.0, OP.mult, OP.add)
        esink = sb([128, 4], F32, "esink")
        o_s, _ = PCN["sink"]
        act(esink[:], pc[:, o_s:o_s + 4], AF.Exp)
        o_ka, _ = PCN["k_a"]
        hka = sb([128, 4], F32, "hka")
        nhka = sb([128, 4], F32, "nhka")
        ts("dve", hka[:], pc[:, o_ka:o_ka + 4], 0.5, None, OP.mult)
        ts("dve", nhka[:], pc[:, o_ka:o_ka + 4], -0.5, None, OP.mult)
        hw0 = sb([128, 4], F32, "hw0")
        ha0 = sb([128, 4], F32, "ha0")
        ts("dve", hw0[:], pc[:, PCN["w0"][0]:PCN["w0"][0] + 4], 0.5, None, OP.mult)
        ts("dve", ha0[:], pc[:, PCN["a0"][0]:PCN["a0"][0] + 4], 0.5, None, OP.mult)
        mhalf = sb([128, 1], F32, "mhalf")
        memset("pool", mhalf[:], -0.5)
        mone = sb([128, 1], F32, "mone")
        memset("pool", mone[:], -1.0)

        w_in_v = w_in.rearrange("(kt p) c -> p kt c", p=128)
        Wk = sb([128, 8, 640], BF16, "Wk")
        S.dma("pool", Wk[:, :, 0:512], w_in_v[:, :, 512:1024])
        S.dma("pool", Wk[:, :, 512:640], w_in_v[:, :, 1536:1664])
        Wv1 = sb([128, 8, 512], BF16, "Wv1")
        Wv2 = sb([128, 8, 512], BF16, "Wv2")
        xin = SRing(2, 1, [128, D], F32, "xin")
        for ct in range(4):
            stg = xin.get()
            stg3 = stg[:].rearrange("p (k c) -> p k c", k=8)
            S.dma("sp", stg3, w_in_v[:, :, 1024 + ct * 128:1024 + (ct + 1) * 128])
            for kt in range(8):
                tt("dve", Wv1[:, kt, ct * 128:(ct + 1) * 128], stg3[:, kt, :], omuv[:, ct * 128:(ct + 1) * 128], OP.mult)
                tt("dve", Wv2[:, kt, ct * 128:(ct + 1) * 128], stg3[:, kt, :], muv[:, ct * 128:(ct + 1) * 128], OP.mult)
        w2a2 = sb([128, 512], BF16, "w2a2")
        S.dma("pool", w2a2[:], w2a2d)
        wp_v = [w.rearrange("(kt p) c -> p kt c", p=128) for w in (w_pa, w_pb, w_pc)]
        w_o_v = w_o.rearrange("(kt p) c -> p kt c", p=128)
        wring = SRing(2, 6, [128, 8, 128], BF16, "wstream")
        BFV = {}
        CONV = []
        for nm_, src_, rows_, cols_ in (("wbf_in", w_in, D, 7552), ("wbf_skd", w_skd, D, 256), ("wbf_pa", w_pa, 512, D),
                                       ("wbf_pb", w_pb, 512, D), ("wbf_pc", w_pc, 512, D), ("wbf_o", w_o, D, D)):
            nkt, ntl = rows_ // 128, cols_ // 128
            scr = nc.dram_tensor(nm_, [ntl, 128, nkt, 128], BF16, kind="Internal").ap()
            S.barrier_names.add(nm_)
            TCH = 15
            for kt_ in range(nkt):
                for t0_ in range(0, ntl, TCH):
                    nt_ = min(TCH, ntl - t0_)
                    src_ap = src_[kt_ * 128:(kt_ + 1) * 128, t0_ * 128:(t0_ + nt_) * 128].rearrange("p (t c) -> p t c", c=128)
                    dst_ap = scr[t0_:t0_ + nt_, :, kt_, :].rearrange("t p c -> p t c")
                    CONV.append((dst_ap, src_ap))
            BFV[src_.tensor.name] = scr

        wpring = Ring(3, [128, 4, 128], BF16, "wpstream")

        def wtile(src_view, c0, nk=8):
            t_ = wpring.get() if nk == 4 else wring.get()
            bv = BFV.get(src_view.tensor.name)
            if bv is not None:
                S.dma("sp", t_[:, 0:nk, :], bv[c0 // 128])
            else:
                S.dma("pool", t_[:, 0:nk, :], src_view[:, :, c0:c0 + 128])
            return t_

        xnr = Ring(2, [128, D], BF16, "xn")
        col = Ring(8, [128, 1], F32, "col")
        fT = SRing(10, 12, [128, TG], F32, "fT")
        bT = SRing(4, 8, [128, TG], BF16, "bT")
        eT = Ring(2, [128, 512], BF16, "eT")

        def norm_transpose(src_dram_rows, dst, dcol0):
            xt = xin.get()
            S.dma("sp", xt[:], src_dram_rows)
            ss = col.get()
            xn = xnr.get()
            act(xn[:], xt[:], AF.Square, accum=ss[:])
            rs = col.get()
            ts("dve", rs[:], ss[:], 1.0 / D, RMS_EPS, OP.mult, OP.add)
            rstd = col.get()
            S.op("pool", lambda e, o=rstd[:], a=rs[:], b=mhalf[:]: e.tensor_tensor(o, a, b, OP.pow), [rs[:], mhalf[:]], [rstd[:]], cost=1500.0)
            stt(xn[:], xt[:], rstd[:], grow[:], OP.mult, OP.mult)
            ps = PS()
            psb = ps[:].bitcast(BF16).rearrange("p (k t) -> p k t", k=8)
            for kt in range(8):
                tr(psb[:, kt, :], xn[:, kt * 128:(kt + 1) * 128], ident)
            cp("act", dst[:, :, dcol0:dcol0 + 128], psb)

        MG = sb([128, 8, TG], BF16, "MG")
        memT = MG
        for mt in range(2):
            norm_transpose(memd[mt * 128:(mt + 1) * 128, :], memT, mt * 128)
        S.dma("sp", grow[:], growd.partition_broadcast(128))
        KmT = sb([128, 4, 256], BF16, "KmT")
        Vmem = sb([128, 2, 512], BF16, "Vmem")
        w_mkv_v = w_mkv.rearrange("(kt p) c -> p kt c", p=128)
        for hd in range(4):
            wt = wtile(w_mkv_v, hd * 128)
            ps = PS()
            for kt in range(8):
                mm(ps[:, 0:256], wt[:, kt, :], memT[:, kt, :], start=(kt == 0), stop=(kt == 7))
            sq = bT.get()
            act(sq[:, 0:256], ps[:, 0:256], AF.Square)
            ps2 = PS()
            mm(ps2[:, 0:256], ones, sq[:, 0:256])
            ms = fT.get()
            ts("dve", ms[:, 0:256], ps2[:, 0:256], 1.0 / 128, RMS_EPS, OP.mult, OP.add)
            rn = fT.get()
            rsq(rn[:, 0:256], ms[:, 0:256])
            stt(KmT[:, hd, :], ps[:, 0:256], PCc("xkg"), rn[:, 0:256], OP.mult, OP.mult)
        for ct in range(4):
            wt = wtile(w_mkv_v, 512 + ct * 128)
            for mt in range(2):
                ps = PS()
                for kt in range(8):
                    mm(ps[:, 0:128], memT[:, kt, mt * 128:(mt + 1) * 128], wt[:, kt, :], start=(kt == 0), stop=(kt == 7))
                cp(evac_eng(), Vmem[:, mt, ct * 128:(ct + 1) * 128], ps[:, 0:128])

        NCG = max(0, G_OWN0 - 1)
        CPG = (len(CONV) + NCG - 1) // NCG if NCG else 0
        if not NCG:
            for o_, i_ in CONV:
                S.conv_dma(o_, i_)

        NCH = TG // 64
        hTb = [sb([128, 8, TG + 1], BF16, f"hT{i}") for i in range(2)]
        memset("pool", hTb[0][:], 0.0)
        memset("pool", hTb[1][:], 0.0)
        Uk = [sb([128, TG + 1], F32, f"Uk{j}") for j in range(4)]
        Ul = sb([128, TG + 1], F32, "Ul")
        Ur = [sb([128, TG + 1], F32, f"Ur{j}") for j in range(4)]
        for u in Uk + [Ul] + Ur:
            memset("pool", u[:], 0.0)
        lt = sb([128, TG], BF16, "lt")
        def H2(name, dt=BF16):
            return [sb([128, 2, TG], dt, f"{name}{hf}") for hf in range(2)]
        AtT, BtT, KtT, RtT, KhT, BhT = H2("AtT"), H2("BtT"), H2("KtT"), H2("RtT"), H2("KhT"), H2("BhT")
        KP, RX, VT = H2("KP"), H2("RX"), H2("VT")
        gam = [sb([128, 2, NCH], F32, f"gam{hf}") for hf in range(2)]
        TOKA = [[sb([128, 512], BF16, f"TOKA{hf}_{i}") for i in range(NT4)] for hf in range(2)]
        TOKK = [[sb([128, 256], BF16, f"TOKK{hf}_{i}") for i in range(NT4)] for hf in range(2)]
        Vtok = [[sb([128, 256], BF16, f"Vtok{hf}_{i}") for i in range(NT4)] for hf in range(2)]
        N_bt = [[sb([128, 4, 128], BF16, f"Nb{t}_{i}") for i in range(2)] for t in range(NT4)]
        Z_bt = [[sb([128, 4, 128], BF16, f"Zb{t}_{i}") for i in range(2)] for t in range(NT4)]
        W_bt = [[sb([128, 4, 128], BF16, f"Wb{t}_{i}") for i in range(2)] for t in range(NT4)]
        Makt = [sb([128, 4, 128], BF16, f"Mak{t}") for t in range(NT4)]
        Mrbt = [sb([128, 4, 128], BF16, f"Mrb{t}") for t in range(NT4)]
        Mrkt = [sb([128, 4, 128], BF16, f"Mrk{t}") for t in range(NT4)]
        Pm = sb([128, 2, 64], BF16, "Pm")
        Qsb = sb([128, 2, 64], F32, "Qsb")
        Stz = [[sb([128, 2, 2, 64], BF16, f"Stz{hf}_{i}") for i in range(2)] for hf in range(2)]
        for hf in range(2):
            for i in range(2):
                memset("pool", Stz[hf][i][:], 0.0)
        RG = sb([128, 2, 128], BF16, "RG")
        kx_t, sg_t, aa_t, Cs_t, eNC_t, eCp_t, eCL_t, kkn_t, bb_t = [sb([128, TG], F32, f"pre{i}") for i in range(9)]
        YA = sb([128, 4, TG], BF16, "YA")
        YB = sb([128, 4, TG], BF16, "YB")
        YC = sb([128, 4, TG], BF16, "YC")
        Qr = sb([128, 4, TG], BF16, "Qr")
        Kr = sb([128, 2, 128 + TG], BF16, "Kr")
        NV = NT4 + 1
        Vs = [sb([128, 128], BF16, f"Vs{i}") for i in range(NV)]
        for v_ in Vs:
            memset("pool", v_[:], 0.0)
        memset("pool", Kr[:], 0.0)
        cosT = sb([128, TG], F32, "cosT")
        sinT = sb([128, TG], F32, "sinT")
        posi = sb([128, TG], I32, "posi")
        kfi = sb([128, TG], I32, "kfi")
        PEX = {(w_, p_): sb([128, 512], BF16, f"pex{w_}{p_}") for w_ in "cp" for p_ in range(2)}
        cidx = [0, 0]
        w_skd_v = w_skd.rearrange("(kt p) c -> p kt c", p=128)

        def proj(hT, wt_tile, wcols):
            ps = PS()
            for kt in range(8):
                mm(ps[:, 0:TG], wt_tile[:, kt, wcols], hT[:, kt, 1:TG + 1], start=(kt == 0), stop=(kt == 7))
            return ps[:, 0:TG]

        def silu2(ps):
            th = bT.get()
            act(th[:], ps, AF.Tanh, scale=0.5)
            sz = bT.get()
            stt(sz[:], th[:], 1.0, ps, OP.add, OP.mult)
            return sz

        def shift_mix(ps, U, mu_ap, out):
            cp("dve", U[:, 0:1], U[:, TG:TG + 1])
            cp("act", U[:, 1:TG + 1], ps)
            d = fT.get()
            tt("dve", d[:], U[:, 0:TG], U[:, 1:TG + 1], OP.subtract)
            stt(out, d[:], mu_ap, U[:, 1:TG + 1], OP.mult, OP.add)

        def XC(g):
            CUR[0] = "X"
            XOWN[0] = FILL_ALL or g >= G_OWN0
            TAG[0] = "XC" + ("o" if g >= G_OWN0 else "p")
            hT, hTp = hTb[g % 2], hTb[(g - 1) % 2]
            if g > 0:
                cp("dve", hT[:, :, 0:1], hTp[:, :, TG:TG + 1])
            for t4 in range(NT4):
                r0 = g * TG + t4 * 128
                norm_transpose(xw[r0:r0 + 128, :], hT, 1 + t4 * 128)
            if g < NCG:
                for o_, i_ in CONV[g * CPG:(g + 1) * CPG]:
                    S.conv_dma(o_, i_)
            ps = proj(hT, Wk, slice(512, 640))
            lmix = fT.get()
            shift_mix(ps, Ul, PCc("mu_l"), lmix[:])
            act(lt[0:64, :], lmix[0:64, :], AF.Tanh)
            cp("dve", lt[64:128, :], lmix[64:128, :])

        def XH(g, hf):
            CUR[0] = "X"
            XOWN[0] = FILL_ALL or g >= G_OWN0
            TAG[0] = "XH" + ("o" if g >= G_OWN0 else "p")
            own = g >= G_OWN0
            hT = hTb[g % 2]
            for t4 in range(NT4):
                ps = PS()
                vs_ = slice(256 * hf, 256 * hf + 256)
                for kt in range(8):
                    mm(ps[:, 0:256], hT[:, kt, 1 + t4 * 128:1 + (t4 + 1) * 128], Wv1[:, kt, vs_], start=(kt == 0), stop=False)
                for kt in range(8):
                    mm(ps[:, 0:256], hT[:, kt, t4 * 128:(t4 + 1) * 128], Wv2[:, kt, vs_], start=False, stop=(kt == 7))
                cp(evac_eng(), Vtok[hf][t4][:], ps[:, 0:256])
            for jl in range(2):
                j = 2 * hf + jl
                ps = proj(hT, Wk, slice(j * 128, (j + 1) * 128))
                kx = kx_t
                shift_mix(ps, Uk[j], PCc("mu_k", j), kx[:])
                if own or g == G_OWN0 - 1:
                    wt = wtile(w_in_v, j * 128)
                    ps = proj(hT, wt, slice(0, 128))
                    shift_mix(ps, Ur[j], PCc("mu_r", j), RX[hf][:, jl, :])
                psw = PS()
                mm(psw[:, 0:TG], w2a2[0:64, j * 128:(j + 1) * 128], lt[0:64, :])
                thw = sg_t
                act(thw[:], psw[:, 0:TG], AF.Tanh, bias=hw0[:, j:j + 1], scale=0.5)
                psa = PS()
                mm(psa[:, 0:TG], w2a2[64:128, j * 128:(j + 1) * 128], lt[64:128, :])
                tha = aa_t
                act(tha[:], psa[:, 0:TG], AF.Tanh, bias=ha0[:, j:j + 1], scale=0.5)
                Cs = Cs_t
                S.op("dve", lambda e, o=Cs[:], m=rmask, s_=thw[:]: e.tensor_tensor_scan(o, m, s_, 0.0, OP.mult, OP.add),
                     [rmask, thw[:]], [Cs[:]], cost=600.0)
                tt("dve", Cs[:], Cs[:], idx1, OP.add)
                Cp = fT.get()
                stt(Cp[:], thw[:], -1.0, Cs[:], OP.mult, OP.add)
                CL = fT.get()
                Cs3 = Cs[:].rearrange("p (c l) -> p c l", l=64)
                tt("dve", CL[:].rearrange("p (c l) -> p c l", l=64), Cs3[:, :, 63:64].broadcast_to([128, NCH, 64]),
                   Cs3, OP.subtract)
                HC0 = 0.5 * C0
                act(gam[hf][:, jl:jl + 1, :].rearrange("p o c -> p c o"), Cs3[:, :, 63:64], AF.Exp, scale=HC0)
                eNC, eCp, eCL = eNC_t, eCp_t, eCL_t
                act(eNC[:], Cs[:], AF.Exp, scale=-HC0)
                act(eCp[:], Cp[:], AF.Exp, scale=HC0, bias=-HC0)
                act(eCL[:], CL[:], AF.Exp, scale=HC0)
                sq = bT.get()
                act(sq[:], kx[:], AF.Square, scale=PCc("k_k", j))
                pss = PS()
                mm(pss[:, 0:TG], blk64, sq[:])
                mx = fT.get()
                ts("dve", mx[:], pss[:, 0:TG], 1e-24, None, OP.max)
                rn2 = fT.get()
                if POOL_RECIP:
                    S.op("pool", lambda e, o=rn2[:], a=mx[:], b=mone[:, 0:1].broadcast_to([128, TG]): e.tensor_tensor(o, a, b, OP.pow),
                         [mx[:], mone[:]], [rn2[:]], cost=7000.0)
                else:
                    S.op("dve", lambda e, o=rn2[:], i=mx[:]: e.reciprocal(o, i), [mx[:]], [rn2[:]], cost=70 + 8 * TG)
                kkr = kkn_t
                stt(kkr[:], kx[:], PCc("k_k", j), rn2[:], OP.mult, OP.mult)
                t1 = fT.get()
                ts("dve", t1[:], tha[:], hka[:, j:j + 1], nhka[:, j:j + 1], OP.mult, OP.add)
                kp = KP[hf][:, jl, :]
                stt(kp, t1[:], 1.0, kx[:], OP.add, OP.mult)
                bb = bb_t
                stt(bb[:], tha[:], 1.0, kx[:], OP.add, OP.mult)
                tt("dve", KtT[hf][:, jl, :], kp, eNC[:], OP.mult)
                stt(BtT[hf][:, jl, :], bb[:], PCc("k_k", j), eNC[:], OP.mult, OP.mult)
                stt(AtT[hf][:, jl, :], kkr[:], -0.5, eCp[:], OP.mult, OP.mult)
                tt("dve", KhT[hf][:, jl, :], kp, eCL[:], OP.mult)
                stt(BhT[hf][:, jl, :], bb[:], PCc("k_k", j), eCL[:], OP.mult, OP.mult)
                if own:
                    eC = fT.get()
                    act(eC[:], Cs[:], AF.Exp, scale=0.5 * C0)
                    tt("dve", RtT[hf][:, jl, :], RX[hf][:, jl, :], eC[:], OP.mult)
            for t4 in range(NT4):
                cs = slice(t4 * 128, (t4 + 1) * 128)
                ps = PS()
                psb = ps[:].bitcast(BF16)
                for jl in range(2):
                    tr(psb[:, jl * 128:(jl + 1) * 128], AtT[hf][:, jl, cs], ident)
                    tr(psb[:, 256 + jl * 128:256 + (jl + 1) * 128], BhT[hf][:, jl, cs], ident)
                    tr(psb[:, 512 + jl * 128:512 + (jl + 1) * 128], KhT[hf][:, jl, cs], ident)
                    if own:
                        tr(psb[:, 768 + jl * 128:768 + (jl + 1) * 128], Vtok[hf][t4][:, jl * 128:(jl + 1) * 128], ident)
                cp("act", TOKA[hf][t4][:], psb[:, 0:512])
                cp("act", TOKK[hf][t4][:], psb[:, 512:768])
                if own:
                    cp("act", VT[hf][:, :, cs], psb[:, 768:1024].rearrange("p (j t) -> p j t", j=2))

        def YH(g, hf):
            CUR[0] = "Y"
            YOWN[0] = g >= G_OWN0
            TAG[0] = "YH" + ("o" if g >= G_OWN0 else "p")
            own = g >= G_OWN0
            hT = hTb[g % 2]
            for t4 in range(NT4):
                CUR[0] = f"Y{t4}"
                N_b, Z_b, W_b, Mak, Mrb, Mrk = N_bt[t4], Z_bt[t4], W_bt[t4], Makt[t4], Mrbt[t4], Mrkt[t4]
                cs = slice(t4 * 128, (t4 + 1) * 128)

                def par_mm(lhs, rhs_):
                    banks = []
                    for par in range(2):
                        ps = PS()
                        pv = ps[:, 0:256].rearrange("p (j t) -> p j t", j=2)
                        pr = slice(par * 64, par * 64 + 64)
                        for jl in range(2):
                            mm(pv[:, jl, :], lhs[hf][pr, jl, cs], rhs_[hf][pr, jl, cs])
                        banks.append(pv)
                    return banks

                def evac_par(banks, dst, mask):
                    d4 = dst[:].rearrange("p (j par) t -> p par j t", par=2)
                    for par in range(2):
                        tt("dve", d4[:, par], banks[par], mask.rearrange("p (j t) -> p j t", j=2), OP.mult)

                N0, Z0, W0 = N_b[0], Z_b[0], W_b[0]
                evac_par(par_mm(BtT, AtT), N0, C("msu"))
                evac_par(par_mm(KtT, AtT), Mak, C("msu"))
                evac_par(par_mm(AtT, BtT), Z0, C("msl"))
                if own:
                    evac_par(par_mm(BtT, RtT), Mrb, C("miu"))
                    evac_par(par_mm(KtT, RtT), Mrk, C("miu"))
                if own:
                    cp("dve", W0[:, :, 0:64], TOKA[hf][t4][:, 0:256].rearrange("p (h k) -> p h k", h=4))
                ps = PS()
                pv = ps[:, 0:256].rearrange("p (h v) -> p h v", h=4)
                for hl in range(4):
                    mm(pv[:, hl, :], Mak[:, hl, :], Vtok[hf][t4][:, hl * 64:(hl + 1) * 64])
                cp("act", W0[:, :, 64:128], pv)
                for lv in range(6):
                    Nc, Zc, Wc = N_b[lv % 2], Z_b[lv % 2], W_b[lv % 2]
                    Nn, Zn, Wn = N_b[(lv + 1) % 2], Z_b[(lv + 1) % 2], W_b[(lv + 1) % 2]
                    ps = PS()
                    if own:
                        pv = ps[:].rearrange("p (h t) -> p h t", h=4)
                        if OWN_W_ACC:
                            for hl in range(4):
                                mm(pv[:, hl, :], ident, Wc[:, hl, :], start=True, stop=False)
                                mm(pv[:, hl, :], Nc[:, hl, :], Wc[:, hl, :], start=False, stop=True)
                            cp("act", Wn[:], pv)
                        else:
                            for hl in range(4):
                                mm(pv[:, hl, :], Nc[:, hl, :], Wc[:, hl, :])
                            tt("dve", Wn[:], pv, Wc[:], OP.add)
                    else:
                        pv = ps[:, 0:256].rearrange("p (h k) -> p h k", h=4)
                        Wc_ap = (TOKA[hf][t4][:, 256:512].rearrange("p (h k) -> p h k", h=4) if lv == 0
                                 else Wc[:, :, 0:64])
                        if PREFIX_W_ACC:
                            for hl in range(4):
                                mm(pv[:, hl, :], ident, Wc_ap[:, hl, :], start=True, stop=False)
                                mm(pv[:, hl, :], Zc[:, hl, :], Wc_ap[:, hl, :], start=False, stop=True)
                            cp("act", Wn[:, :, 0:64], pv)
                        else:
                            for hl in range(4):
                                mm(pv[:, hl, :], Zc[:, hl, :], Wc_ap[:, hl, :])
                            tt("dve", Wn[:, :, 0:64], pv, Wc_ap, OP.add)
                    if lv < 5:
                        ps = PS()
                        pv = ps[:].rearrange("p (h t) -> p h t", h=4)
                        for hl in range(4):
                            mm(pv[:, hl, :], Nc[:, hl, :], Zc[:, hl, :])
                        cp("act", Zn[:], pv)
                        ps = PS()
                        pv = ps[:].rearrange("p (h t) -> p h t", h=4)
                        for hl in range(4):
                            mm(pv[:, hl, :], Zc[:, hl, :], Nc[:, hl, :])
                        cp("act", Nn[:], pv)
            CUR[0] = "Y"
            for t4 in range(NT4):
                N_b, Z_b, W_b, Mak, Mrb, Mrk = N_bt[t4], Z_bt[t4], W_bt[t4], Makt[t4], Mrbt[t4], Mrkt[t4]
                cs = slice(t4 * 128, (t4 + 1) * 128)
                Wf = W_b[0]
                if own:
                    ps = PS()
                    pv = ps[:, 0:256].rearrange("p (j t) -> p j t", j=2)
                    for hl in range(4):
                        jl, par = hl // 2, hl % 2
                        mm(pv[par * 64:par * 64 + 64, jl, :], Wf[:, hl, 0:64], Mrb[:, hl, :])
                    tt("dve", RG[:], pv, RtT[hf][:, :, cs], OP.add)
                    pY = PSY_BANK[:, 0:256].rearrange("p (j t) -> p j t", j=2)
                for c in range(2):
                    cr = slice(c * 64, c * 64 + 64)
                    Sc, Sn = Stz[hf][cidx[hf] % 2], Stz[hf][(cidx[hf] + 1) % 2]
                    chunk_in_group = t4 * 2 + c
                    psP = PS()
                    pP = psP[:, 0:128].rearrange("p (j k) -> p j k", j=2)
                    psQ = PS()
                    pQ = psQ[:, 0:128].rearrange("p (j k) -> p j k", j=2)
                    for hl in range(4):
                        jl, par = hl // 2, hl % 2
                        pr = slice(par * 64, par * 64 + 64)
                        if own:
                            Bh_tok = TOKA[hf][t4][cr, 256 + hl * 64:256 + (hl + 1) * 64]
                            mm(pP[pr, jl, :], Wf[cr, hl, 0:64], Bh_tok)
                            mm(pQ[pr, jl, :], Bh_tok, Wf[cr, hl, 64:128], start=True, stop=False)
                        else:
                            mm(pP[pr, jl, :], TOKA[hf][t4][cr, hl * 64:(hl + 1) * 64], Wf[cr, hl, 0:64])
                            mm(pQ[pr, jl, :], Wf[cr, hl, 0:64], Wf[cr, hl, 64:128], start=True, stop=False)
                        mm(pQ[pr, jl, :], TOKK[hf][t4][cr, hl * 64:(hl + 1) * 64], Vtok[hf][t4][cr, hl * 64:(hl + 1) * 64],
                           start=False, stop=True)
                    for jl in range(2):
                        stt(Pm[:, jl, :], C("d0"), gam[hf][:, jl, chunk_in_group:chunk_in_group + 1], pP[:, jl, :],
                            OP.mult, OP.add)
                    cp("act", Qsb[:], pQ)
                    if own:
                        oc = slice(c * 64, c * 64 + 64)
                        for hl in range(4):
                            jl, par = hl // 2, hl % 2
                            pr = slice(par * 64, par * 64 + 64)
                            mm(pY[pr, jl, oc], Sc[:, par, jl, :], RG[:, jl, oc], start=True, stop=False)
                            mm(pY[pr, jl, oc], Wf[:, hl, 64:128], Mrb[:, hl, oc], start=False, stop=False)
                            mm(pY[pr, jl, oc], Vtok[hf][t4][:, hl * 64:(hl + 1) * 64], Mrk[:, hl, oc], start=False, stop=True)
                    psS = PS()
                    pS = psS[:, 0:128].rearrange("p (j k) -> p j k", j=2)
                    for hl in range(4):
                        jl, par = hl // 2, hl % 2
                        mm(pS[par * 64:par * 64 + 64, jl, :], Pm[:, jl, :], Sc[:, par, jl, :])
                    for par in range(2):
                        pr = slice(par * 64, par * 64 + 64)
                        tt("dve", Sn[pr, par], pS[pr], Qsb[pr], OP.add)
                    cidx[hf] += 1
                if own:
                    cp("act", Ybf[:, 2 * hf:2 * hf + 2, cs], pY)
                    act(Ysq[:, 2 * hf:2 * hf + 2, cs], pY, AF.Square)
            if not own:
                return
            for jl in range(2):
                j = 2 * hf + jl
                ps1 = PS()
                mm(ps1[:, 0:TG], blk64, Ybf[:, j, :])
                ps2 = PS()
                mm(ps2[:, 0:TG], blk64, Ysq[:, j, :])
                mean = fT.get()
                act(mean[:], ps1[:, 0:TG], AF.Copy, scale=1.0 / 64)
                msq = fT.get()
                tt("dve", msq[:], mean[:], mean[:], OP.mult)
                var = fT.get()
                stt(var[:], ps2[:, 0:TG], 1.0 / 64, msq[:], OP.mult, OP.subtract)
                ve = fT.get()
                ts("dve", ve[:], var[:], LNX_EPS, None, OP.add)
                rstd = fT.get()
                rsq(rstd[:], ve[:])
                yc = fT.get()
                tt("dve", yc[:], Ybf[:, j, :], mean[:], OP.subtract)
                yn = fT.get()
                tt("dve", yn[:], yc[:], rstd[:], OP.mult)
                yg = fT.get()
                ts("dve", yg[:], yn[:], PCc("lnx_g", j), PCc("lnx_b", j), OP.mult, OP.add)
                rk = bT.get()
                stt(rk[:], RX[hf][:, jl, :], PCc("r_k", j), KP[hf][:, jl, :], OP.mult, OP.mult)
                psb_ = PS()
                mm(psb_[:, 0:TG], blk64, rk[:])
                bv = fT.get()
                tt("dve", bv[:], psb_[:, 0:TG], VT[hf][:, jl, :], OP.mult)
                yb = fT.get()
                tt("dve", yb[:], bv[:], yg[:], OP.add)
                wt = wtile(w_in_v, 1664 + j * 128)
                ps = proj(hT, wt, slice(0, 128))
                sz = silu2(ps)
                stt(YA[:, j, :], yb[:], 0.5, sz[:], OP.mult, OP.mult)

        def YE(g):
            CUR[0] = "Y"
            YOWN[0] = g >= G_OWN0
            TAG[0] = "YE" + ("o" if g >= G_OWN0 else "p")
            if g < G_OWN0 - 1:
                return
            own = g >= G_OWN0
            og = g - G_OWN0
            hT = hTb[g % 2]
            if own:
                cp("dve", Kr[:, :, 0:128], Kr[:, :, TG:TG + 128])
                S.dma("sp", posi[:], posd[:, 128 + og * TG:128 + (og + 1) * TG].partition_broadcast(128))
            else:
                memset("pool", posi[:], 0)
                S.dma("sp", posi[:, TG - 128:TG], posd[:, 0:128].partition_broadcast(128))
            posf = fT.get()
            cp("dve", posf[:], posi[:])
            ang = fT.get()
            ts("dve", ang[:], posf[:], PCc("invf"), None, OP.mult)
            ts("dve", kfi[:], ang[:], 1.0 / (2 * PI), None, OP.mult)
            kff = fT.get()
            cp("dve", kff[:], kfi[:])
            rr = fT.get()
            stt(rr[:], kff[:], -2 * PI, ang[:], OP.mult, OP.add)
            wa = fT.get()
            ts("dve", wa[:], rr[:], -PI, 2 * PI, OP.is_lt, OP.mult)
            wb = fT.get()
            ts("dve", wb[:], rr[:], PI, -2 * PI, OP.is_gt, OP.mult)
            rw0 = fT.get()
            tt("dve", rw0[:], rr[:], wa[:], OP.add)
            rw = fT.get()
            tt("dve", rw[:], rw0[:], wb[:], OP.add)
            yc_ = fT.get()
            ts("dve", yc_[:], rw[:], PI / 2, None, OP.add)
            wc = fT.get()
            ts("dve", wc[:], yc_[:], PI, -2 * PI, OP.is_gt, OP.mult)
            rc = fT.get()
            tt("dve", rc[:], yc_[:], wc[:], OP.add)
            act(sinT[:], rw[:], AF.Sin)
            act(cosT[:], rc[:], AF.Sin)

            def head_norm_rope(ps, g_ap, dst, nparts_scale, do_rope=True):
                raw = fT.get()
                cp("act", raw[:], ps)
                sq = bT.get()
                act(sq[:], ps, AF.Square)
                pss = PS()
                mm(pss[:, 0:TG], blk64 if nparts_scale == 64 else ones, sq[:])
                ms = fT.get()
                ts("dve", ms[:], pss[:, 0:TG], 1.0 / nparts_scale, RMS_EPS, OP.mult, OP.add)
                rn = fT.get()
                rsq(rn[:], ms[:])
                if not do_rope:
                    stt(dst, raw[:], g_ap, rn[:], OP.mult, OP.mult)
                    return
                qn = bT.get()
                stt(qn[:], raw[:], g_ap, rn[:], OP.mult, OP.mult)
                psr = PS()
                mm(psr[:, 0:TG], C("rot"), qn[:])
                t1 = fT.get()
                tt("dve", t1[:], qn[:], cosT[:], OP.mult)
                t2 = fT.get()
                tt("dve", t2[:], psr[:, 0:TG], sinT[:], OP.mult)
                tt("dve", dst, t1[:], t2[:], OP.add)

            for kvh in range(2):
                wt = wtile(w_skd_v, kvh * 128)
                ps = proj(hT, wt, slice(0, 128))
                head_norm_rope(ps, PCc("kg"), Kr[:, kvh, 128:128 + TG], 64)
            wt = wtile(w_in_v, 2816)
            for t4 in range(NT4):
                if not own and t4 < NT4 - 1:
                    continue
                ps = PS()
                for kt in range(8):
                    mm(ps[:, 0:128], hT[:, kt, 1 + t4 * 128:1 + (t4 + 1) * 128], wt[:, kt, :], start=(kt == 0), stop=(kt == 7))
                vi = (0 if not own else 1 + og * NT4 + t4) % NV
                cp(evac_eng(), Vs[vi][:], ps[:, 0:128])
            if not own:
                return
            for j in range(4):
                wt = wtile(w_in_v, 2176 + j * 128)
                ps = proj(hT, wt, slice(0, 128))
                head_norm_rope(ps, PCc("qg"), Qr[:, j, :], 64)
            for t4 in range(NT4):
                blk = og * NT4 + t4
                qs = slice(t4 * 128, (t4 + 1) * 128)
                kc = slice(128 + t4 * 128, 128 + (t4 + 1) * 128)
                kp_ = slice(t4 * 128, (t4 + 1) * 128)
                Vc, Vp = Vs[(1 + blk) % NV], Vs[blk % NV]
                for par in range(2):
                    pr = slice(par * 64, par * 64 + 64)
                    for which, ksl, msk in (("c", kc, C("mcu")), ("p", kp_, C("mpl"))):
                        ps = PS()
                        pv = ps[:].rearrange("p (j t) -> p j t", j=4)
                        for j in range(4):
                            mm(pv[:, j, :], Kr[pr, j // 2, ksl], Qr[pr, j, qs])
                        et = eT.get()
                        act(et[:], ps[:], AF.Exp, scale=0.125)
                        dst = PEX[(which, par)]
                        if which == "p" and blk == 0:
                            stt(dst[:], et[:], PCc("fm"), msk, OP.mult, OP.mult)
                        else:
                            tt("dve", dst[:], et[:], msk, OP.mult)
                pso = PS()
                po = pso[:].rearrange("p (j t) -> p j t", j=4)
                psd = PS()
                pd = psd[:].rearrange("p (j t) -> p j t", j=4)
                for j in range(4):
                    kvh = j // 2
                    for par in range(2):
                        pr = slice(par * 64, par * 64 + 64)
                        pc_ = PEX[("c", par)][:, j * 128:(j + 1) * 128]
                        pp_ = PEX[("p", par)][:, j * 128:(j + 1) * 128]
                        mm(po[pr, j, :], Vc[:, kvh * 64:(kvh + 1) * 64], pc_, start=True, stop=False)
                        mm(po[pr, j, :], Vp[:, kvh * 64:(kvh + 1) * 64], pp_, start=False, stop=True)
                        mm(pd[pr, j, :], ones[:, 0:64], pc_, start=True, stop=False)
                        mm(pd[pr, j, :], ones[:, 0:64], pp_, start=False, stop=True)
                den = [fT.get(), fT.get()]
                for j in range(4):
                    ts("dve", den[j // 2][:, (j % 2) * 128:(j % 2 + 1) * 128], pd[:, j, :], esink[:, j:j + 1], None, OP.add)
                for hf2 in range(2):
                    rden = fT.get()
                    S.op("dve", lambda e, o=rden[:], i=den[hf2][:]: e.reciprocal(o, i), [den[hf2][:]], [rden[:]])
                    tt("dve", YB[:, 2 * hf2:2 * hf2 + 2, qs], po[:, 2 * hf2:2 * hf2 + 2, :],
                       rden[:].rearrange("p (j t) -> p j t", j=2), OP.mult)
            for j in range(4):
                wt = wtile(w_in_v, 2944 + j * 128)
                ps = proj(hT, wt, slice(0, 128))
                sz = silu2(ps)
                stt(YB[:, j, :], YB[:, j, :], 0.5, sz[:], OP.mult, OP.mult)
            for hd in range(4):
                wt = wtile(w_in_v, 3456 + hd * 128)
                ps = proj(hT, wt, slice(0, 128))
                qx = bT.get()
                head_norm_rope(ps, PCc("xqg"), qx[:], 128, do_rope=False)
                pes = []
                for mt in range(2):
                    pss = PS()
                    mm(pss[:, 0:TG], KmT[:, hd, mt * 128:(mt + 1) * 128], qx[:])
                    pe_ = bT.get()
                    act(pe_[:], pss[:, 0:TG], AF.Exp, scale=float(128 ** -0.5))
                    pes.append(pe_)
                pso = PS()
                psd = PS()
                for mt in range(2):
                    mm(pso[:, 0:TG], Vmem[:, mt, hd * 128:(hd + 1) * 128], pes[mt][:], start=(mt == 0), stop=(mt == 1))
                for mt in range(2):
                    mm(psd[:, 0:TG], ones, pes[mt][:], start=(mt == 0), stop=(mt == 1))
                rden = fT.get()
                S.op("dve", lambda e, o=rden[:], i=psd[:, 0:TG]: e.reciprocal(o, i), [psd[:, 0:TG]], [rden[:]])
                oc_ = fT.get()
                tt("dve", oc_[:], pso[:, 0:TG], rden[:], OP.mult)
                wt = wtile(w_in_v, 3968 + hd * 128)
                ps = proj(hT, wt, slice(0, 128))
                sz = silu2(ps)
                stt(YC[:, hd, :], oc_[:], 0.5, sz[:], OP.mult, OP.mult)
            for dt_ in range(8):
                acc = None
                for br, Yb in enumerate((YA, YB, YC)):
                    wt = wtile(w_in_v, 4480 + br * 1024 + dt_ * 128)
                    psg = proj(hT, wt, slice(0, 128))
                    gt = bT.get()
                    act(gt[:], psg, AF.Tanh, scale=0.5)
                    wpt = wtile(wp_v[br], dt_ * 128, nk=4)
                    psp = PS()
                    for kt in range(4):
                        mm(psp[:, 0:TG], wpt[:, kt, :], Yb[:, kt, :], start=(kt == 0), stop=(kt == 3))
                    tmp = fT.get()
                    stt(tmp[:], gt[:], 1.0, psp[:, 0:TG], OP.add, OP.mult)
                    if acc is None:
                        acc = tmp
                    elif br == 2:
                        tt("dve", MG[:, dt_, :], acc[:], tmp[:], OP.add)
                    else:
                        acc2 = fT.get()
                        tt("dve", acc2[:], acc[:], tmp[:], OP.add)
                        acc = acc2
            for t4 in range(NT4):
                xr = xin.get()
                r0 = og * TG + t4 * 128
                S.dma("sp", xr[:], xown[r0:r0 + 128, :])
                for ct in range(8):
                    wot = wtile(w_o_v, ct * 128)
                    ps = PS()
                    for kt in range(8):
                        mm(ps[:, 0:128], MG[:, kt, t4 * 128:(t4 + 1) * 128], wot[:, kt, :], start=(kt == 0), stop=(kt == 7))
                    stt(xr[:, ct * 128:(ct + 1) * 128], ps[:, 0:128], 0.5, xr[:, ct * 128:(ct + 1) * 128], OP.mult, OP.add)
                S.dma("sp", yout[r0:r0 + 128, :], xr[:], is_output=True)

        def rec(*fns):
            S.rec_start()
            for f in fns:
                f()
            return S.rec_stop()

        S.play(rec(lambda: XC(0), lambda: XH(0, 0)))
        for g in range(n_groups):
            S.fill_on = FILL_ALL or g > G_OWN0
            S.play(S.schedule(rec(lambda: YH(g, 0), lambda: XH(g, 1))))
            S.fill_on = FILL_ALL or g >= G_OWN0
            fns = [lambda: YH(g, 1), lambda: YE(g)]
            if g + 1 < n_groups:
                fns += [lambda: XC(g + 1), lambda: XH(g + 1, 0)]
            S.play(S.schedule(rec(*fns)))

        if dbg:
            print('SBUF bytes/partition', _sbytes[0], 'counts', S.cnt); print('tagcost us', {k: (v[0], round(v[1] / 1000)) for k, v in sorted(S.tagcost.items())}); print('model time us', max(S.etime.values()) / 1000, 'fillers', S.nfill); print('PE cols by tag', {k: (v[0], v[1], round(v[1] / 1.2 / 1000)) for k, v in PEC.items()}); print(sorted(_sblist, key=lambda t: -t[0]))
        S.finish()
        S.emit()
    return nc


def _host_inputs(inputs):
    f = lambda a: np.ascontiguousarray(np.asarray(a))
    x = f(inputs["x"])
    mem = f(inputs["mem"])
    pos = f(inputs["positions"])
    w_in = f(inputs["w_in"][0])
    cm, cf = _consts()
    sk = w_in[:, 2688:2816]
    w_skd = np.ascontiguousarray(np.concatenate([sk[:, 0:64], sk[:, 0:64], sk[:, 64:128], sk[:, 64:128]], axis=1))
    w2a2 = np.ascontiguousarray(np.concatenate([inputs["w2"][0], inputs["a2"][0]], axis=0)).astype(np.float32)

    def c4(v):
        return np.asarray(v, np.float32).reshape(4, 128).T

    def dup(v, n):
        return np.tile(np.asarray(v, np.float32).reshape(-1), n).reshape(128, 1)
    shared = dict(
        mem=None, w_in=w_in, w_skd=w_skd, w_mkv=f(inputs["w_mem_kv"][0]), w_pa=f(inputs["w_proj_a"][0]),
        w_pb=f(inputs["w_proj_b"][0]), w_pc=f(inputs["w_proj_c"][0]), w_o=f(inputs["w_out"][0]), w2a2=w2a2,
        g_row=f(inputs["norm_g"]).reshape(1, D), gm_row=f(inputs["mem_norm_g"]).reshape(1, D),
        muv_row=f(inputs["mu_rkv"][0, 2]).reshape(1, 512), cm=cm, cf=cf)
    invf = (np.float32(10000.0) ** (-(np.arange(32, dtype=np.float32) / np.float32(32)))).astype(np.float32)
    maps = []
    for c in range(NCORES):
        b, q = c // 4, c % 4
        end = OWN * (q + 1)
        xw = np.zeros((T, D), np.float32)
        xw[T - end:] = x[b, :end]
        pp = np.zeros((1, OWN + 128), np.int32)
        s0 = end - OWN - 128
        if s0 >= 0:
            pp[0] = pos[b, s0:end]
        else:
            pp[0, 128:] = pos[b, 0:end]
        pcv = np.zeros((128, NPC), np.float32)

        def put(name, a):
            o, n = PCN[name]
            pcv[:, o:o + n] = a
        put("mu_r", c4(inputs["mu_rkv"][0, 0]))
        put("mu_k", c4(inputs["mu_rkv"][0, 1]))
        put("mu_l", np.concatenate([inputs["mu_wa"][0, 0], inputs["mu_wa"][0, 1]]).reshape(128, 1))
        put("w0", c4(inputs["w0"][0]))
        put("a0", c4(inputs["a0"][0]))
        put("k_k", c4(inputs["k_k"][0]))
        put("k_a", c4(inputs["k_a"][0]))
        put("r_k", c4(np.asarray(inputs["r_k"][0]).reshape(-1)))
        put("lnx_g", c4(inputs["lnx_g"][0]))
        put("lnx_b", c4(inputs["lnx_b"][0]))
        put("qg", dup(inputs["q_norm_g"][0], 2))
        put("kg", dup(inputs["k_norm_g"][0], 2))
        put("sink", np.repeat(np.asarray(inputs["sinks"][0], np.float32).reshape(4, 2).T, 64, axis=0))
        put("xqg", np.asarray(inputs["xq_norm_g"][0], np.float32).reshape(128, 1))
        put("xkg", np.asarray(inputs["xk_norm_g"][0], np.float32).reshape(128, 1))
        put("invf", np.tile(invf, 4).reshape(128, 1))
        put("fm", np.full((128, 1), 0.0 if q == 0 else 1.0, np.float32))
        m = dict(shared)
        m.update(xw=xw, xown=np.ascontiguousarray(x[b, end - OWN:end]), pos=pp, mem=mem[b], pc=pcv)
        maps.append(m)
    return maps


_NC_CACHE = {}


def kernel(**inputs):
    inputs = {k: np.asarray(v) for k, v in inputs.items()}
    if "nc" not in _NC_CACHE:
        _NC_CACHE["nc"] = build_program()
    nc = _NC_CACHE["nc"]
    maps = _host_inputs(inputs)
    res = run_bass_kernel_spmd(nc, maps, core_ids=list(range(NCORES)))
    out = np.zeros((2, T, D), np.float32)
    for c in range(NCORES):
        b, q = c // 4, c % 4
        out[b, q * OWN:(q + 1) * OWN] = res.results[c]["y"]
    return out
```

```python
from contextlib import ExitStack
import numpy as np
import ml_dtypes
import concourse.bass as bass
import concourse.mybir as mybir
from concourse.bass_utils import run_bass_kernel_spmd

F32 = mybir.dt.float32
BF16 = mybir.dt.bfloat16
I32 = mybir.dt.int32
AF = mybir.ActivationFunctionType
OP = mybir.AluOpType

NCORES = 8
D = 1024
T = 8192
OWN = 2048
TG = 256
NT4 = TG // 128
NG = T // TG
G_OWN0 = NG - OWN // TG
RMS_EPS = 1e-6
LNX_EPS = 64e-5
C0 = -float(np.exp(-0.5))
PI = float(np.pi)
FILL_MIN, FILL_CAP, FILL_SCALE = 300.0, 24, 1.3
FILL_DENSITY = 1.3
SCHED_BUCKET = 150.0
FILL_ALL = False
Y_PREFIX_6 = True
YE_6 = True
PREFIX_W_ACC = True
OWN_W_ACC = False
POOL_RECIP = False
PE_COLD_GHZ = 1.2
PE_WARM_GHZ = 1.8

PCN = {}
def _pcdef():
    o = 0
    for name, n in [("mu_r", 4), ("mu_k", 4), ("mu_l", 1), ("w0", 4), ("a0", 4), ("k_k", 4), ("k_a", 4), ("r_k", 4),
                    ("lnx_g", 4), ("lnx_b", 4), ("qg", 1), ("kg", 1), ("sink", 4), ("xqg", 1), ("xkg", 1), ("invf", 1),
                    ("fm", 1)]:
        PCN[name] = (o, n)
        o += n
    return o
NPC = _pcdef()

CMN = {}
def _cmdef():
    o = 0
    for name, n in [("ident", 128), ("blk64", 128), ("ones", 128), ("msu", 256), ("miu", 256), ("msl", 256),
                    ("mcu", 512), ("mpl", 512), ("rot", 128), ("d0", 64)]:
        CMN[name] = (o, n)
        o += n
    return o
NCM = _cmdef()


def _consts():
    cm = np.zeros((128, NCM), np.float32)
    def put(name, a):
        o, n = CMN[name]
        assert a.shape == (128, n), (name, a.shape)
        cm[:, o:o + n] = a
    i = np.arange(128)
    r, c = i[:, None], i[None, :]
    same = (r // 64) == (c // 64)
    put("ident", (r == c).astype(np.float32))
    put("blk64", same.astype(np.float32))
    put("ones", np.ones((128, 128), np.float32))
    put("msu", np.tile(((r < c) & same).astype(np.float32), (1, 2)))
    put("miu", np.tile(((r <= c) & same).astype(np.float32), (1, 2)))
    put("msl", np.tile(((r > c) & same).astype(np.float32), (1, 2)))
    put("mcu", np.tile((c >= r).astype(np.float32), (1, 4)))
    put("mpl", np.tile((c < r).astype(np.float32), (1, 4)))
    rot = np.zeros((128, 128), np.float32)
    for cc in range(128):
        if cc % 64 < 32:
            rot[cc + 32, cc] = -1.0
        else:
            rot[cc - 32, cc] = 1.0
    put("rot", rot)
    put("d0", ((r % 64) == np.arange(64)[None, :]).astype(np.float32)[:, :64])
    cf = np.zeros((128, 2 * TG), np.float32)
    rm = np.ones(TG, np.float32)
    rm[::64] = 0.0
    cf[:, 0:TG] = rm[None, :]
    cf[:, TG:2 * TG] = ((np.arange(TG) % 64) + 1).astype(np.float32)[None, :]
    return cm.astype(ml_dtypes.bfloat16), cf


class Sched:
    def __init__(self, nc, stack):
        self.nc, self.stack = nc, stack
        self.names = ["pe", "act", "dve", "pool", "sp"]
        self.prog = {e: [] for e in self.names}
        self.cnt = {e: 0 for e in self.names}
        self.seg = 12000
        self.sems = {e: [] for e in self.names}
        self.seen = {e: {} for e in self.names}
        self.lastw, self.readers = {}, {}
        self.ND = 16
        self.dnext_pool = 0
        self.barrier_names = set()
        self.barrier_done = set()
        self.convsem = stack.enter_context(nc.semaphore("convsem"))
        self.nconv = 0
        self.dsems = [stack.enter_context(nc.semaphore(f"dsem{i}")) for i in range(self.ND)]
        self.dcnt = [0] * self.ND
        self.dlast = [None] * self.ND
        self.dnext = 0
        self.out_tokens = []
        self.rec = None
        self.filler = None
        self.fill_on = False
        self.cur_tset = "A"
        self.tagcost = {}
        self.curtag = ['?']
        self.nfill = 0
        self.fill_ns = 0.0
        self.etime = {e: 0.0 for e in self.names}

    def rec_start(self):
        self.rec = []

    def rec_stop(self):
        r, self.rec = self.rec, None
        return r

    def play(self, lst):
        for it in lst:
            if it[0] == "op":
                self.op(*it[1:5])
            elif it[0] == "conv":
                self._conv_dma(it[1], it[2])
            elif it[0] == "fill":
                for _ in range(it[1]):
                    self.prog["pe"].append(([], self.filler, None, 0))
                self.nfill += it[1]
            else:
                self.dma(*it[1:5])

    def schedule(self, lst):
        n = len(lst)
        lastw, readers = {}, {}
        preds = [set() for _ in range(n)]
        eng, cost, lat = [None] * n, [0.0] * n, [0.0] * n
        tset = [None] * n
        for i, it in enumerate(lst):
            if it[0] == "op":
                _, e, fn, ins, outs, c = it
                ins = [a for a in ins if a is not None and not isinstance(a, (int, float))]
                fs = 1
                for d_ in outs[0].shape[1:]:
                    fs *= int(d_)
                if c is None:
                    if e == "pe":
                        c = max(64, fs) / (PE_WARM_GHZ if self.fill_on else PE_COLD_GHZ) + 8
                    elif e == "act":
                        c = 200 + fs / 1.2
                    elif e == "dve":
                        c = 70 + fs * 1.04
                    else:
                        c = 600 + fs
                eng[i], cost[i], lat[i] = e, c, c + (200 if e == "pe" else 60)
                tset[i] = getattr(fn, "_tset", None)
                tg_ = getattr(fn, "_tag", "?")
                acc_ = self.tagcost.setdefault((tg_, e), [0, 0.0])
                acc_[0] += 1
                acc_[1] += c
            elif it[0] == "conv":
                ins, outs = [], []
                eng[i], cost[i], lat[i] = "pool", 1000.0, 1000.0
            else:
                _, q, out, in_, _io = it
                ins, outs = [in_], [out]
                eng[i], cost[i], lat[i] = q, 80.0, 2500.0
            for a in ins:
                k = self.key(a)
                if k in lastw:
                    preds[i].add(lastw[k])
            for a in outs:
                k = self.key(a)
                if k in lastw:
                    preds[i].add(lastw[k])
                preds[i].update(readers.get(k, ()))
            for a in ins:
                readers.setdefault(self.key(a), set()).add(i)
            for a in outs:
                k = self.key(a)
                lastw[k] = i
                readers[k] = set()
            preds[i].discard(i)
        succs = [[] for _ in range(n)]
        npred = [len(p_) for p_ in preds]
        for i in range(n):
            for p_ in preds[i]:
                succs[p_].append(i)
        blev = [0.0] * n
        for i in range(n - 1, -1, -1):
            m_ = 0.0
            for s_ in succs[i]:
                if blev[s_] > m_:
                    m_ = blev[s_]
            blev[i] = lat[i] + m_
        etime = dict(self.etime)
        t0 = max(etime.values())
        for e in etime:
            etime[e] = max(etime[e], t0 - 2000.0)
        fin = [0.0] * n
        ready = [0.0] * n
        avail = [i for i in range(n) if npred[i] == 0]
        order = []
        while avail:
            best, bi = None, None
            for i in avail:
                st = max(etime[eng[i]], ready[i])
                if tset[i] is not None and self.cur_tset not in tset[i]:
                    st += 1300.0
                key_ = (st // SCHED_BUCKET, -blev[i], i) if SCHED_BUCKET else (st, i)
                if best is None or key_ < best:
                    best, bi = key_, i
            avail.remove(bi)
            st = max(etime[eng[bi]], ready[bi])
            if tset[bi] is not None and self.cur_tset not in tset[bi]:
                self.cur_tset = tset[bi][0]
            if eng[bi] == "pe" and self.filler is not None and self.fill_cap > 0 and self.fill_on:
                gap = (st - etime["pe"]) * self.fill_scale
                if gap > self.fill_min:
                    nf = min(self.fill_cap, int(gap * FILL_DENSITY / 512.0 + 0.5))
                    if nf > 0:
                        order.append(("fill", nf))
            etime[eng[bi]] = st + cost[bi]
            fin[bi] = st + lat[bi]
            order.append(bi)
            for s_ in succs[bi]:
                r = fin[bi] + (120.0 if eng[s_] != eng[bi] else 0.0)
                if r > ready[s_]:
                    ready[s_] = r
                npred[s_] -= 1
                if npred[s_] == 0:
                    avail.append(s_)
        if getattr(self, "dbgwin", 0) > 0:
            self.dbgwin -= 1
            bus = {}
            for i in range(n):
                bus[eng[i]] = bus.get(eng[i], 0.0) + cost[i]
            print("window n=%d span=%.1fus busy=%s" % (n, (max(etime.values()) - t0) / 1000, {k: round(v / 1000, 1) for k, v in bus.items()}))
        self.etime = etime
        return [x if isinstance(x, tuple) else lst[x] for x in order]

    @staticmethod
    def merge(a, b):
        out, i, j = [], 0, 0
        na, nb = len(a), len(b)
        while i < na or j < nb:
            if j >= nb or (i < na and i * nb <= j * na):
                out.append(a[i]); i += 1
            else:
                out.append(b[j]); j += 1
        return out

    def _semfor(self, e, k):
        si = (k - 1) // self.seg
        while len(self.sems[e]) <= si:
            self.sems[e].append(self.stack.enter_context(self.nc.semaphore(f"s_{e}_{len(self.sems[e])}")))
        return self.sems[e][si], (k - 1) % self.seg + 1, si

    @staticmethod
    def key(ap):
        return ap.tensor.name

    def _waits(self, e, toks):
        best = {}
        for t in toks:
            if t is None:
                continue
            if t[0] == "c":
                _, f, k = t
                if f == e and e == "pe":
                    continue
                sem, val, si = self._semfor(f, k)
                kk = ("c", f)
                cur = best.get(kk)
                if cur is None or (si, val) > (cur[0], cur[1]):
                    best[kk] = (si, val, sem)
            else:
                _, i, val = t
                kk = ("d", i)
                cur = best.get(kk)
                if cur is None or val > cur[1]:
                    best[kk] = (0, val, self.dsems[i])
        wl = []
        for kk, (si, val, sem) in best.items():
            if self.seen[e].get(kk, (-1, 0)) >= (si, val):
                continue
            self.seen[e][kk] = (si, val)
            wl.append((sem, val))
        return wl

    def _deps(self, ins, outs):
        toks = []
        for a in ins:
            toks.append(self.lastw.get(self.key(a)))
        for a in outs:
            k = self.key(a)
            toks.append(self.lastw.get(k))
            toks.extend(self.readers.get(k, {}).values())
        return toks

    def _commit(self, tok, rid, ins, outs):
        for a in ins:
            self.readers.setdefault(self.key(a), {})[rid] = tok
        for a in outs:
            k = self.key(a)
            self.lastw[k] = tok
            self.readers[k] = {}

    def op(self, e, fn, ins, outs, cost=None):
        if self.rec is not None:
            try:
                fn._tag = self.curtag[0]
            except Exception:
                pass
            self.rec.append(("op", e, fn, ins, outs, cost))
            return
        ins = [a for a in ins if a is not None and not isinstance(a, (int, float))]
        wl = self._waits(e, self._deps(ins, outs))
        k = self.cnt[e] + 1
        self.cnt[e] = k
        sem, _, _ = self._semfor(e, k)
        self.prog[e].append((wl, fn, sem, 1))
        self._commit(("c", e, k), e, ins, outs)

    def dma(self, q, out, in_, is_output=False):
        if self.rec is not None:
            self.rec.append(("dma", q, out, in_, is_output))
            return
        half = self.ND // 2
        if q == "pool":
            i = half + self.dnext_pool
            self.dnext_pool = (self.dnext_pool + 1) % (self.ND - half)
        else:
            i = self.dnext
            self.dnext = (self.dnext + 1) % half
        toks = self._deps([in_], [out]) + [self.dlast[i]]
        wl = self._waits(q, toks)
        if self.key(in_) in self.barrier_names and q not in self.barrier_done:
            wl = wl + [(self.convsem, 16 * self.nconv)]
            self.barrier_done.add(q)
        self.dcnt[i] += 1
        tok = ("d", i, 16 * self.dcnt[i])
        self.dlast[i] = tok
        self.prog[q].append((wl, lambda eng, o=out, a=in_: eng.dma_start(out=o, in_=a), self.dsems[i], 16))
        self._commit(tok, ("d", i), [in_], [out])
        if is_output:
            self.out_tokens.append(tok)

    def conv_dma(self, out, in_):
        if self.rec is not None:
            self.rec.append(("conv", out, in_))
            return
        self._conv_dma(out, in_)

    def _conv_dma(self, out, in_):
        self.nconv += 1
        self.prog["pool"].append(([], lambda eng, o=out, a=in_: eng.dma_start(out=o, in_=a), self.convsem, 16))

    def finish(self):
        wl = self._waits("sp", self.out_tokens + [t for t in self.dlast if t is not None])
        self.prog["sp"].append((wl, None, None, 0))

    def emit(self):
        nc = self.nc
        engs = {"pe": "tensor", "act": "scalar", "dve": "vector", "pool": "gpsimd", "sp": "sync"}
        with nc.Block() as block:
            for e, bn in engs.items():
                prog = self.prog[e]

                def body(eng, prog=prog):
                    for wl, fn, sem, inc in prog:
                        for ws, wv in wl:
                            eng.wait_ge(ws, wv)
                        if fn is not None:
                            ins_ = fn(eng)
                            if sem is not None:
                                ins_.then_inc(sem, inc)
                getattr(block, bn)(body)


def build_program(n_groups=NG, g_own0=G_OWN0, dbg=False):
    G_OWN0 = g_own0
    nc = bass.Bass("TRN2", target_bir_lowering=False)
    st = ExitStack()
    with st:
        S = Sched(nc, st)

        def dram(name, shape, dt, kind="ExternalInput"):
            return nc.dram_tensor(name, list(shape), dt, kind=kind).ap()
        xw = dram("xw", [T, D], F32)
        xown = dram("xown", [OWN, D], F32)
        posd = dram("pos", [1, OWN + 128], I32)
        memd = dram("mem", [256, D], F32)
        w_in = dram("w_in", [D, 7552], F32)
        w_skd = dram("w_skd", [D, 256], F32)
        w_mkv = dram("w_mkv", [D, 1024], F32)
        w_pa = dram("w_pa", [512, D], F32)
        w_pb = dram("w_pb", [512, D], F32)
        w_pc = dram("w_pc", [512, D], F32)
        w_o = dram("w_o", [D, D], F32)
        w2a2d = dram("w2a2", [128, 512], F32)
        pcd = dram("pc", [128, NPC], F32)
        growd = dram("g_row", [1, D], F32)
        gmrowd = dram("gm_row", [1, D], F32)
        muvd = dram("muv_row", [1, 512], F32)
        cmd = dram("cm", [128, NCM], BF16)
        cfd = dram("cf", [128, 2 * TG], F32)
        yout = dram("y", [OWN, D], F32, kind="ExternalOutput")

        _n = [0]

        _sbytes = [0]
        _sblist = []

        def sb(shape, dt, name=None):
            _n[0] += 1
            _sbytes[0] += int(np.prod(shape[1:])) * (2 if dt == BF16 else 4)
            _sblist.append((int(np.prod(shape[1:])) * (2 if dt == BF16 else 4), name))
            return st.enter_context(nc.sbuf_tensor("sb_" + (name or f"t{_n[0]}"), list(shape), dt))

        class Ring:
            def __init__(self, n, shape, dt, name):
                self.t = [sb(shape, dt, f"{name}{i}") for i in range(n)]
                self.i = 0

            def get(self):
                t = self.t[self.i % len(self.t)]
                self.i += 1
                return t

        psum = [st.enter_context(nc.psum_tensor(f"ps{i}", [128, 512], F32)) for i in range(8)]
        CUR = ["Y"]
        PSB = {"X": [6, 7], "Y": [0, 1, 2, 3, 4], "Y0": [0, 1, 2], "Y1": [3, 4, 5]}
        PSI = {"X": 0, "Y": 0, "Y0": 0, "Y1": 0}
        PSY_BANK = psum[5]
        JUNK = psum[7]
        S.fill_min, S.fill_cap, S.fill_scale = FILL_MIN, FILL_CAP, FILL_SCALE

        XOWN = [False]
        YOWN = [False]
        INYE = [False]

        def PS():
            c = CUR[0]
            if c == "X" and XOWN[0] and FILL_CAP > 0:
                return psum[6]
            banks = PSB[c]
            if c == "Y" and (not YOWN[0] or (INYE[0] and YE_6)) and Y_PREFIX_6:
                banks = [0, 1, 2, 3, 4, 5]
            t = psum[banks[PSI[c] % len(banks)]]
            PSI[c] += 1
            return t

        class SRing:
            def __init__(self, nx, ny, shape, dt, name):
                self.r = {"X": Ring(nx, shape, dt, name + "x") if nx else None,
                          "Y": Ring(ny, shape, dt, name + "y") if ny else None}

            def get(self):
                return self.r[CUR[0][0]].get()

        PEC = {}
        TAG = S.curtag
        TAG[0] = "setup"

        def _pec(out):
            fs = 1
            for d_ in out.shape[1:]:
                fs *= int(d_)
            k = TAG[0]
            c = PEC.setdefault(k, [0, 0])
            c[0] += 1
            c[1] += max(64, fs)

        def mm(out, lhsT, rhs, start=True, stop=True):
            _pec(out)
            S.op("pe", lambda e: e.matmul(out, lhsT, rhs, start=start, stop=stop), [lhsT, rhs], [out])

        def tr(out, in_, ident):
            _pec(out)
            S.op("pe", lambda e: e.transpose(out, in_, ident), [in_, ident], [out])

        TSET = {AF.Exp: ("A", "B"), AF.Tanh: ("A",), AF.Ln: ("B",), AF.Sin: ("C",)}

        def act(out, in_, func, bias=None, scale=None, accum=None):
            kw = {}
            if bias is not None:
                kw["bias"] = bias
            if scale is not None:
                kw["scale"] = scale
            if accum is not None:
                kw["accum_out"] = accum
            outs = [out] + ([accum] if accum is not None else [])
            fn_ = lambda e: e.activation(out, in_, func, **kw)
            fn_._tset = TSET.get(func)
            S.op("act", fn_, [in_, bias, scale], outs)

        def tt(eng, out, a, b, op):
            S.op(eng, lambda e: e.tensor_tensor(out, a, b, op), [a, b], [out])

        def ts(eng, out, a, s1, s2, op0, op1=None):
            if op1 is None:
                S.op(eng, lambda e: e.tensor_scalar(out, a, s1, None, op0), [a, s1], [out])
            else:
                S.op(eng, lambda e: e.tensor_scalar(out, a, s1, s2, op0, op1), [a, s1, s2], [out])

        def stt(out, a, s, b, op0, op1):
            S.op("dve", lambda e: e.scalar_tensor_tensor(out, a, s, b, op0, op1), [a, s, b], [out])

        def cp(eng, out, in_):
            if eng == "act":
                act(out, in_, AF.Copy)
            else:
                S.op(eng, lambda e: e.tensor_copy(out, in_), [in_], [out])

        def memset(eng, out, val):
            S.op(eng, lambda e: e.memset(out, val), [], [out])

        _rr = [0]

        def evac_eng():
            _rr[0] += 1
            return "act" if _rr[0] % 2 else "dve"

        cm = sb([128, NCM], BF16, "cm")
        cf = sb([128, 2 * TG], F32, "cf")
        pc = sb([128, NPC], F32, "pc")
        grow = sb([128, D], F32, "grow")
        Ybf = sb([128, 4, TG], BF16, "Ybf")
        Ysq = sb([128, 4, TG], BF16, "Ysq")
        muv = Ybf[:].bitcast(F32).rearrange("p j t -> p (j t)")
        omuv = Ysq[:].bitcast(F32).rearrange("p j t -> p (j t)")
        S.dma("sp", cm[:], cmd)
        S.dma("sp", cf[:], cfd)
        S.dma("sp", pc[:], pcd)
        S.dma("sp", grow[:], gmrowd.partition_broadcast(128))
        S.dma("sp", muv, muvd.partition_broadcast(128))

        def C(name):
            o, n = CMN[name]
            return cm[:, o:o + n]

        def PCc(name, j=0):
            o, n = PCN[name]
            return pc[:, o + j:o + j + 1]
        ident = C("ident")
        S.filler = lambda e: e.matmul(JUNK[:, 0:512], ident, cm[:, 0:512], start=True, stop=True)
        blk64 = C("blk64")
        ones = C("ones")
        rmask = cf[:, 0:TG]
        idx1 = cf[:, TG:2 * TG]

        def rsq(out, in_):
            act(out, in_, AF.Ln)
            act(out, out, AF.Exp, scale=-0.5)

        ts("dve", omuv, muv, -1.0, 1.0, OP.mult, OP.add)
        esink = sb([128, 4], F32, "esink")
        o_s, _ = PCN["sink"]
        act(esink[:], pc[:, o_s:o_s + 4], AF.Exp)
        o_ka, _ = PCN["k_a"]
        hka = sb([128, 4], F32, "hka")
        nhka = sb([128, 4], F32, "nhka")
        ts("dve", hka[:], pc[:, o_ka:o_ka + 4], 0.5, None, OP.mult)
        ts("dve", nhka[:], pc[:, o_ka:o_ka + 4], -0.5, None, OP.mult)
        hw0 = sb([128, 4], F32, "hw0")
        ha0 = sb([128, 4], F32, "ha0")
        ts("dve", hw0[:], pc[:, PCN["w0"][0]:PCN["w0"][0] + 4], 0.5, None, OP.mult)
        ts("dve", ha0[:], pc[:, PCN["a0"][0]:PCN["a0"][0] + 4], 0.5, None, OP.mult)
        mhalf = sb([128, 1], F32, "mhalf")
        memset("pool", mhalf[:], -0.5)
        mone = sb([128, 1], F32, "mone")
        memset("pool", mone[:], -1.0)

        w_in_v = w_in.rearrange("(kt p) c -> p kt c", p=128)
        Wk = sb([128, 8, 640], BF16, "Wk")
        S.dma("pool", Wk[:, :, 0:512], w_in_v[:, :, 512:1024])
        S.dma("pool", Wk[:, :, 512:640], w_in_v[:, :, 1536:1664])
        Wv1 = sb([128, 8, 512], BF16, "Wv1")
        Wv2 = sb([128, 8, 512], BF16, "Wv2")
        xin = SRing(2, 1, [128, D], F32, "xin")
        for ct in range(4):
            stg = xin.get()
            stg3 = stg[:].rearrange("p (k c) -> p k c", k=8)
            S.dma("sp", stg3, w_in_v[:, :, 1024 + ct * 128:1024 + (ct + 1) * 128])
            for kt in range(8):
                tt("dve", Wv1[:, kt, ct * 128:(ct + 1) * 128], stg3[:, kt, :], omuv[:, ct * 128:(ct + 1) * 128], OP.mult)
                tt("dve", Wv2[:, kt, ct * 128:(ct + 1) * 128], stg3[:, kt, :], muv[:, ct * 128:(ct + 1) * 128], OP.mult)
        w2a2 = sb([128, 512], BF16, "w2a2")
        S.dma("pool", w2a2[:], w2a2d)
        wp_v = [w.rearrange("(kt p) c -> p kt c", p=128) for w in (w_pa, w_pb, w_pc)]
        w_o_v = w_o.rearrange("(kt p) c -> p kt c", p=128)
        wring = SRing(2, 6, [128, 8, 128], BF16, "wstream")
        BFV = {}
        CONV = []
        for nm_, src_, rows_, cols_ in (("wbf_in", w_in, D, 7552), ("wbf_skd", w_skd, D, 256), ("wbf_pa", w_pa, 512, D),
                                       ("wbf_pb", w_pb, 512, D), ("wbf_pc", w_pc, 512, D), ("wbf_o", w_o, D, D)):
            nkt, ntl = rows_ // 128, cols_ // 128
            scr = nc.dram_tensor(nm_, [ntl, 128, nkt, 128], BF16, kind="Internal").ap()
            S.barrier_names.add(nm_)
            TCH = 15
            for kt_ in range(nkt):
                for t0_ in range(0, ntl, TCH):
                    nt_ = min(TCH, ntl - t0_)
                    src_ap = src_[kt_ * 128:(kt_ + 1) * 128, t0_ * 128:(t0_ + nt_) * 128].rearrange("p (t c) -> p t c", c=128)
                    dst_ap = scr[t0_:t0_ + nt_, :, kt_, :].rearrange("t p c -> p t c")
                    CONV.append((dst_ap, src_ap))
            BFV[src_.tensor.name] = scr

        wpring = Ring(3, [128, 4, 128], BF16, "wpstream")

        def wtile(src_view, c0, nk=8):
            t_ = wpring.get() if nk == 4 else wring.get()
            bv = BFV.get(src_view.tensor.name)
            if bv is not None:
                S.dma("sp", t_[:, 0:nk, :], bv[c0 // 128])
            else:
                S.dma("pool", t_[:, 0:nk, :], src_view[:, :, c0:c0 + 128])
            return t_

        xnr = Ring(2, [128, D], BF16, "xn")
        col = Ring(8, [128, 1], F32, "col")
        fT = SRing(10, 12, [128, TG], F32, "fT")
        bT = SRing(4, 8, [128, TG], BF16, "bT")
        eT = Ring(2, [128, 512], BF16, "eT")

        def norm_transpose(src_dram_rows, dst, dcol0):
            xt = xin.get()
            S.dma("sp", xt[:], src_dram_rows)
            ss = col.get()
            xn = xnr.get()
            act(xn[:], xt[:], AF.Square, accum=ss[:])
            rs = col.get()
            ts("dve", rs[:], ss[:], 1.0 / D, RMS_EPS, OP.mult, OP.add)
            rstd = col.get()
            S.op("pool", lambda e, o=rstd[:], a=rs[:], b=mhalf[:]: e.tensor_tensor(o, a, b, OP.pow), [rs[:], mhalf[:]], [rstd[:]], cost=1500.0)
            stt(xn[:], xt[:], rstd[:], grow[:], OP.mult, OP.mult)
            ps = PS()
            psb = ps[:].bitcast(BF16).rearrange("p (k t) -> p k t", k=8)
            for kt in range(8):
                tr(psb[:, kt, :], xn[:, kt * 128:(kt + 1) * 128], ident)
            cp("act", dst[:, :, dcol0:dcol0 + 128], psb)

        MG = sb([128, 8, TG], BF16, "MG")
        memT = MG
        for mt in range(2):
            norm_transpose(memd[mt * 128:(mt + 1) * 128, :], memT, mt * 128)
        S.dma("sp", grow[:], growd.partition_broadcast(128))
        KmT = sb([128, 4, 256], BF16, "KmT")
        Vmem = sb([128, 2, 512], BF16, "Vmem")
        w_mkv_v = w_mkv.rearrange("(kt p) c -> p kt c", p=128)
        for hd in range(4):
            wt = wtile(w_mkv_v, hd * 128)
            ps = PS()
            for kt in range(8):
                mm(ps[:, 0:256], wt[:, kt, :], memT[:, kt, :], start=(kt == 0), stop=(kt == 7))
            sq = bT.get()
            act(sq[:, 0:256], ps[:, 0:256], AF.Square)
            ps2 = PS()
            mm(ps2[:, 0:256], ones, sq[:, 0:256])
            ms = fT.get()
            ts("dve", ms[:, 0:256], ps2[:, 0:256], 1.0 / 128, RMS_EPS, OP.mult, OP.add)
            rn = fT.get()
            rsq(rn[:, 0:256], ms[:, 0:256])
            stt(KmT[:, hd, :], ps[:, 0:256], PCc("xkg"), rn[:, 0:256], OP.mult, OP.mult)
        for ct in range(4):
            wt = wtile(w_mkv_v, 512 + ct * 128)
            for mt in range(2):
                ps = PS()
                for kt in range(8):
                    mm(ps[:, 0:128], memT[:, kt, mt * 128:(mt + 1) * 128], wt[:, kt, :], start=(kt == 0), stop=(kt == 7))
                cp(evac_eng(), Vmem[:, mt, ct * 128:(ct + 1) * 128], ps[:, 0:128])

        NCG = max(0, G_OWN0 - 1)
        CPG = (len(CONV) + NCG - 1) // NCG if NCG else 0
        if not NCG:
            for o_, i_ in CONV:
                S.conv_dma(o_, i_)

        NCH = TG // 64
        hTb = [sb([128, 8, TG + 1], BF16, f"hT{i}") for i in range(2)]
        memset("pool", hTb[0][:], 0.0)
        memset("pool", hTb[1][:], 0.0)
        Uk = [sb([128, TG + 1], F32, f"Uk{j}") for j in range(4)]
        Ul = sb([128, TG + 1], F32, "Ul")
        Ur = [sb([128, TG + 1], F32, f"Ur{j}") for j in range(4)]
        for u in Uk + [Ul] + Ur:
            memset("pool", u[:], 0.0)
        lt = sb([128, TG], BF16, "lt")
        def H2(name, dt=BF16):
            return [sb([128, 2, TG], dt, f"{name}{hf}") for hf in range(2)]
        AtT, BtT, KtT, RtT, KhT, BhT = H2("AtT"), H2("BtT"), H2("KtT"), H2("RtT"), H2("KhT"), H2("BhT")
        KP, RX, VT = H2("KP"), H2("RX"), H2("VT")
        gam = [sb([128, 2, NCH], F32, f"gam{hf}") for hf in range(2)]
        TOKA = [[sb([128, 512], BF16, f"TOKA{hf}_{i}") for i in range(NT4)] for hf in range(2)]
        TOKK = [[sb([128, 256], BF16, f"TOKK{hf}_{i}") for i in range(NT4)] for hf in range(2)]
        Vtok = [[sb([128, 256], BF16, f"Vtok{hf}_{i}") for i in range(NT4)] for hf in range(2)]
        N_bt = [[sb([128, 4, 128], BF16, f"Nb{t}_{i}") for i in range(2)] for t in range(NT4)]
        Z_bt = [[sb([128, 4, 128], BF16, f"Zb{t}_{i}") for i in range(2)] for t in range(NT4)]
        W_bt = [[sb([128, 4, 128], BF16, f"Wb{t}_{i}") for i in range(2)] for t in range(NT4)]
        Makt = [sb([128, 4, 128], BF16, f"Mak{t}") for t in range(NT4)]
        Mrbt = [sb([128, 4, 128], BF16, f"Mrb{t}") for t in range(NT4)]
        Mrkt = [sb([128, 4, 128], BF16, f"Mrk{t}") for t in range(NT4)]
        Pm = sb([128, 2, 64], BF16, "Pm")
        Qsb = sb([128, 2, 64], F32, "Qsb")
        Stz = [[sb([128, 2, 2, 64], BF16, f"Stz{hf}_{i}") for i in range(2)] for hf in range(2)]
        for hf in range(2):
            for i in range(2):
                memset("pool", Stz[hf][i][:], 0.0)
        RG = sb([128, 2, 128], BF16, "RG")
        kx_t, sg_t, aa_t, Cs_t, eNC_t, eCp_t, eCL_t, kkn_t, bb_t = [sb([128, TG], F32, f"pre{i}") for i in range(9)]
        YA = sb([128, 4, TG], BF16, "YA")
        YB = sb([128, 4, TG], BF16, "YB")
        YC = sb([128, 4, TG], BF16, "YC")
        Qr = sb([128, 4, TG], BF16, "Qr")
        Kr = sb([128, 2, 128 + TG], BF16, "Kr")
        NV = NT4 + 1
        Vs = [sb([128, 128], BF16, f"Vs{i}") for i in range(NV)]
        for v_ in Vs:
            memset("pool", v_[:], 0.0)
        memset("pool", Kr[:], 0.0)
        cosT = sb([128, TG], F32, "cosT")
        sinT = sb([128, TG], F32, "sinT")
        posi = sb([128, TG], I32, "posi")
        kfi = sb([128, TG], I32, "kfi")
        PEX = {(w_, p_): sb([128, 512], BF16, f"pex{w_}{p_}") for w_ in "cp" for p_ in range(2)}
        cidx = [0, 0]
        w_skd_v = w_skd.rearrange("(kt p) c -> p kt c", p=128)

        def proj(hT, wt_tile, wcols):
            ps = PS()
            for kt in range(8):
                mm(ps[:, 0:TG], wt_tile[:, kt, wcols], hT[:, kt, 1:TG + 1], start=(kt == 0), stop=(kt == 7))
            return ps[:, 0:TG]

        def silu2(ps):
            th = bT.get()
            act(th[:], ps, AF.Tanh, scale=0.5)
            sz = bT.get()
            stt(sz[:], th[:], 1.0, ps, OP.add, OP.mult)
            return sz

        def shift_mix(ps, U, mu_ap, out):
            cp("dve", U[:, 0:1], U[:, TG:TG + 1])
            cp("act", U[:, 1:TG + 1], ps)
            d = fT.get()
            tt("dve", d[:], U[:, 0:TG], U[:, 1:TG + 1], OP.subtract)
            stt(out, d[:], mu_ap, U[:, 1:TG + 1], OP.mult, OP.add)

        def XC(g):
            CUR[0] = "X"
            XOWN[0] = FILL_ALL or g >= G_OWN0
            TAG[0] = "XC" + ("o" if g >= G_OWN0 else "p")
            hT, hTp = hTb[g % 2], hTb[(g - 1) % 2]
            if g > 0:
                cp("dve", hT[:, :, 0:1], hTp[:, :, TG:TG + 1])
            for t4 in range(NT4):
                r0 = g * TG + t4 * 128
                norm_transpose(xw[r0:r0 + 128, :], hT, 1 + t4 * 128)
            if g < NCG:
                for o_, i_ in CONV[g * CPG:(g + 1) * CPG]:
                    S.conv_dma(o_, i_)
            ps = proj(hT, Wk, slice(512, 640))
            lmix = fT.get()
            shift_mix(ps, Ul, PCc("mu_l"), lmix[:])
            act(lt[0:64, :], lmix[0:64, :], AF.Tanh)
            cp("dve", lt[64:128, :], lmix[64:128, :])

        def XH(g, hf):
            CUR[0] = "X"
            XOWN[0] = FILL_ALL or g >= G_OWN0
            TAG[0] = "XH" + ("o" if g >= G_OWN0 else "p")
            own = g >= G_OWN0
            hT = hTb[g % 2]
            for t4 in range(NT4):
                ps = PS()
                vs_ = slice(256 * hf, 256 * hf + 256)
                for kt in range(8):
                    mm(ps[:, 0:256], hT[:, kt, 1 + t4 * 128:1 + (t4 + 1) * 128], Wv1[:, kt, vs_], start=(kt == 0), stop=False)
                for kt in range(8):
                    mm(ps[:, 0:256], hT[:, kt, t4 * 128:(t4 + 1) * 128], Wv2[:, kt, vs_], start=False, stop=(kt == 7))
                cp(evac_eng(), Vtok[hf][t4][:], ps[:, 0:256])
            for jl in range(2):
                j = 2 * hf + jl
                ps = proj(hT, Wk, slice(j * 128, (j + 1) * 128))
                kx = kx_t
                shift_mix(ps, Uk[j], PCc("mu_k", j), kx[:])
                if own or g == G_OWN0 - 1:
                    wt = wtile(w_in_v, j * 128)
                    ps = proj(hT, wt, slice(0, 128))
                    shift_mix(ps, Ur[j], PCc("mu_r", j), RX[hf][:, jl, :])
                psw = PS()
                mm(psw[:, 0:TG], w2a2[0:64, j * 128:(j + 1) * 128], lt[0:64, :])
                thw = sg_t
                act(thw[:], psw[:, 0:TG], AF.Tanh, bias=hw0[:, j:j + 1], scale=0.5)
                psa = PS()
                mm(psa[:, 0:TG], w2a2[64:128, j * 128:(j + 1) * 128], lt[64:128, :])
                tha = aa_t
                act(tha[:], psa[:, 0:TG], AF.Tanh, bias=ha0[:, j:j + 1], scale=0.5)
                Cs = Cs_t
                S.op("dve", lambda e, o=Cs[:], m=rmask, s_=thw[:]: e.tensor_tensor_scan(o, m, s_, 0.0, OP.mult, OP.add),
                     [rmask, thw[:]], [Cs[:]], cost=600.0)
                tt("dve", Cs[:], Cs[:], idx1, OP.add)
                Cp = fT.get()
                stt(Cp[:], thw[:], -1.0, Cs[:], OP.mult, OP.add)
                CL = fT.get()
                Cs3 = Cs[:].rearrange("p (c l) -> p c l", l=64)
                tt("dve", CL[:].rearrange("p (c l) -> p c l", l=64), Cs3[:, :, 63:64].broadcast_to([128, NCH, 64]),
                   Cs3, OP.subtract)
                HC0 = 0.5 * C0
                act(gam[hf][:, jl:jl + 1, :].rearrange("p o c -> p c o"), Cs3[:, :, 63:64], AF.Exp, scale=HC0)
                eNC, eCp, eCL = eNC_t, eCp_t, eCL_t
                act(eNC[:], Cs[:], AF.Exp, scale=-HC0)
                act(eCp[:], Cp[:], AF.Exp, scale=HC0, bias=-HC0)
                act(eCL[:], CL[:], AF.Exp, scale=HC0)
                sq = bT.get()
                act(sq[:], kx[:], AF.Square, scale=PCc("k_k", j))
                pss = PS()
                mm(pss[:, 0:TG], blk64, sq[:])
                mx = fT.get()
                ts("dve", mx[:], pss[:, 0:TG], 1e-24, None, OP.max)
                rn2 = fT.get()
                if POOL_RECIP:
                    S.op("pool", lambda e, o=rn2[:], a=mx[:], b=mone[:, 0:1].broadcast_to([128, TG]): e.tensor_tensor(o, a, b, OP.pow),
                         [mx[:], mone[:]], [rn2[:]], cost=7000.0)
                else:
                    S.op("dve", lambda e, o=rn2[:], i=mx[:]: e.reciprocal(o, i), [mx[:]], [rn2[:]], cost=70 + 8 * TG)
                kkr = kkn_t
                stt(kkr[:], kx[:], PCc("k_k", j), rn2[:], OP.mult, OP.mult)
                t1 = fT.get()
                ts("dve", t1[:], tha[:], hka[:, j:j + 1], nhka[:, j:j + 1], OP.mult, OP.add)
                kp = KP[hf][:, jl, :]
                stt(kp, t1[:], 1.0, kx[:], OP.add, OP.mult)
                bb = bb_t
                stt(bb[:], tha[:], 1.0, kx[:], OP.add, OP.mult)
                tt("dve", KtT[hf][:, jl, :], kp, eNC[:], OP.mult)
                stt(BtT[hf][:, jl, :], bb[:], PCc("k_k", j), eNC[:], OP.mult, OP.mult)
                stt(AtT[hf][:, jl, :], kkr[:], -0.5, eCp[:], OP.mult, OP.mult)
                tt("dve", KhT[hf][:, jl, :], kp, eCL[:], OP.mult)
                stt(BhT[hf][:, jl, :], bb[:], PCc("k_k", j), eCL[:], OP.mult, OP.mult)
                if own:
                    eC = fT.get()
                    act(eC[:], Cs[:], AF.Exp, scale=0.5 * C0)
                    tt("dve", RtT[hf][:, jl, :], RX[hf][:, jl, :], eC[:], OP.mult)
            for t4 in range(NT4):
                cs = slice(t4 * 128, (t4 + 1) * 128)
                ps = PS()
                psb = ps[:].bitcast(BF16)
                for jl in range(2):
                    tr(psb[:, jl * 128:(jl + 1) * 128], AtT[hf][:, jl, cs], ident)
                    tr(psb[:, 256 + jl * 128:256 + (jl + 1) * 128], BhT[hf][:, jl, cs], ident)
                    tr(psb[:, 512 + jl * 128:512 + (jl + 1) * 128], KhT[hf][:, jl, cs], ident)
                    if own:
                        tr(psb[:, 768 + jl * 128:768 + (jl + 1) * 128], Vtok[hf][t4][:, jl * 128:(jl + 1) * 128], ident)
                cp("act", TOKA[hf][t4][:], psb[:, 0:512])
                cp("act", TOKK[hf][t4][:], psb[:, 512:768])
                if own:
                    cp("act", VT[hf][:, :, cs], psb[:, 768:1024].rearrange("p (j t) -> p j t", j=2))

        def YH(g, hf):
            CUR[0] = "Y"
            INYE[0] = False
            YOWN[0] = g >= G_OWN0
            TAG[0] = "YH" + ("o" if g >= G_OWN0 else "p")
            own = g >= G_OWN0
            hT = hTb[g % 2]
            for t4 in range(NT4):
                CUR[0] = f"Y{t4}"
                N_b, Z_b, W_b, Mak, Mrb, Mrk = N_bt[t4], Z_bt[t4], W_bt[t4], Makt[t4], Mrbt[t4], Mrkt[t4]
                cs = slice(t4 * 128, (t4 + 1) * 128)

                def par_mm(lhs, rhs_):
                    banks = []
                    for par in range(2):
                        ps = PS()
                        pv = ps[:, 0:256].rearrange("p (j t) -> p j t", j=2)
                        pr = slice(par * 64, par * 64 + 64)
                        for jl in range(2):
                            mm(pv[:, jl, :], lhs[hf][pr, jl, cs], rhs_[hf][pr, jl, cs])
                        banks.append(pv)
                    return banks

                def evac_par(banks, dst, mask):
                    d4 = dst[:].rearrange("p (j par) t -> p par j t", par=2)
                    for par in range(2):
                        tt("dve", d4[:, par], banks[par], mask.rearrange("p (j t) -> p j t", j=2), OP.mult)

                N0, Z0, W0 = N_b[0], Z_b[0], W_b[0]
                evac_par(par_mm(BtT, AtT), N0, C("msu"))
                evac_par(par_mm(KtT, AtT), Mak, C("msu"))
                evac_par(par_mm(AtT, BtT), Z0, C("msl"))
                if own:
                    evac_par(par_mm(BtT, RtT), Mrb, C("miu"))
                    evac_par(par_mm(KtT, RtT), Mrk, C("miu"))
                if own:
                    cp("dve", W0[:, :, 0:64], TOKA[hf][t4][:, 0:256].rearrange("p (h k) -> p h k", h=4))
                ps = PS()
                pv = ps[:, 0:256].rearrange("p (h v) -> p h v", h=4)
                for hl in range(4):
                    mm(pv[:, hl, :], Mak[:, hl, :], Vtok[hf][t4][:, hl * 64:(hl + 1) * 64])
                cp("act", W0[:, :, 64:128], pv)
                for lv in range(6):
                    Nc, Zc, Wc = N_b[lv % 2], Z_b[lv % 2], W_b[lv % 2]
                    Nn, Zn, Wn = N_b[(lv + 1) % 2], Z_b[(lv + 1) % 2], W_b[(lv + 1) % 2]
                    ps = PS()
                    if own:
                        pv = ps[:].rearrange("p (h t) -> p h t", h=4)
                        if OWN_W_ACC:
                            for hl in range(4):
                                mm(pv[:, hl, :], ident, Wc[:, hl, :], start=True, stop=False)
                                mm(pv[:, hl, :], Nc[:, hl, :], Wc[:, hl, :], start=False, stop=True)
                            cp("act", Wn[:], pv)
                        else:
                            for hl in range(4):
                                mm(pv[:, hl, :], Nc[:, hl, :], Wc[:, hl, :])
                            tt("dve", Wn[:], pv, Wc[:], OP.add)
                    else:
                        pv = ps[:, 0:256].rearrange("p (h k) -> p h k", h=4)
                        Wc_ap = (TOKA[hf][t4][:, 256:512].rearrange("p (h k) -> p h k", h=4) if lv == 0
                                 else Wc[:, :, 0:64])
                        if PREFIX_W_ACC:
                            for hl in range(4):
                                mm(pv[:, hl, :], ident, Wc_ap[:, hl, :], start=True, stop=False)
                                mm(pv[:, hl, :], Zc[:, hl, :], Wc_ap[:, hl, :], start=False, stop=True)
                            cp("act", Wn[:, :, 0:64], pv)
                        else:
                            for hl in range(4):
                                mm(pv[:, hl, :], Zc[:, hl, :], Wc_ap[:, hl, :])
                            tt("dve", Wn[:, :, 0:64], pv, Wc_ap, OP.add)
                    if lv < 5:
                        ps = PS()
                        pv = ps[:].rearrange("p (h t) -> p h t", h=4)
                        for hl in range(4):
                            mm(pv[:, hl, :], Nc[:, hl, :], Zc[:, hl, :])
                        cp("act", Zn[:], pv)
                        ps = PS()
                        pv = ps[:].rearrange("p (h t) -> p h t", h=4)
                        for hl in range(4):
                            mm(pv[:, hl, :], Zc[:, hl, :], Nc[:, hl, :])
                        cp("act", Nn[:], pv)
            CUR[0] = "Y"
            for t4 in range(NT4):
                N_b, Z_b, W_b, Mak, Mrb, Mrk = N_bt[t4], Z_bt[t4], W_bt[t4], Makt[t4], Mrbt[t4], Mrkt[t4]
                cs = slice(t4 * 128, (t4 + 1) * 128)
                Wf = W_b[0]
                if own:
                    ps = PS()
                    pv = ps[:, 0:256].rearrange("p (j t) -> p j t", j=2)
                    for hl in range(4):
                        jl, par = hl // 2, hl % 2
                        mm(pv[par * 64:par * 64 + 64, jl, :], Wf[:, hl, 0:64], Mrb[:, hl, :])
                    tt("dve", RG[:], pv, RtT[hf][:, :, cs], OP.add)
                    pY = PSY_BANK[:, 0:256].rearrange("p (j t) -> p j t", j=2)
                for c in range(2):
                    cr = slice(c * 64, c * 64 + 64)
                    Sc, Sn = Stz[hf][cidx[hf] % 2], Stz[hf][(cidx[hf] + 1) % 2]
                    chunk_in_group = t4 * 2 + c
                    psP = PS()
                    pP = psP[:, 0:128].rearrange("p (j k) -> p j k", j=2)
                    psQ = PS()
                    pQ = psQ[:, 0:128].rearrange("p (j k) -> p j k", j=2)
                    for hl in range(4):
                        jl, par = hl // 2, hl % 2
                        pr = slice(par * 64, par * 64 + 64)
                        if own:
                            Bh_tok = TOKA[hf][t4][cr, 256 + hl * 64:256 + (hl + 1) * 64]
                            mm(pP[pr, jl, :], Wf[cr, hl, 0:64], Bh_tok)
                            mm(pQ[pr, jl, :], Bh_tok, Wf[cr, hl, 64:128], start=True, stop=False)
                        else:
                            mm(pP[pr, jl, :], TOKA[hf][t4][cr, hl * 64:(hl + 1) * 64], Wf[cr, hl, 0:64])
                            mm(pQ[pr, jl, :], Wf[cr, hl, 0:64], Wf[cr, hl, 64:128], start=True, stop=False)
                        mm(pQ[pr, jl, :], TOKK[hf][t4][cr, hl * 64:(hl + 1) * 64], Vtok[hf][t4][cr, hl * 64:(hl + 1) * 64],
                           start=False, stop=True)
                    for jl in range(2):
                        stt(Pm[:, jl, :], C("d0"), gam[hf][:, jl, chunk_in_group:chunk_in_group + 1], pP[:, jl, :],
                            OP.mult, OP.add)
                    cp("act", Qsb[:], pQ)
                    if own:
                        oc = slice(c * 64, c * 64 + 64)
                        for hl in range(4):
                            jl, par = hl // 2, hl % 2
                            pr = slice(par * 64, par * 64 + 64)
                            mm(pY[pr, jl, oc], Sc[:, par, jl, :], RG[:, jl, oc], start=True, stop=False)
                            mm(pY[pr, jl, oc], Wf[:, hl, 64:128], Mrb[:, hl, oc], start=False, stop=False)
                            mm(pY[pr, jl, oc], Vtok[hf][t4][:, hl * 64:(hl + 1) * 64], Mrk[:, hl, oc], start=False, stop=True)
                    psS = PS()
                    pS = psS[:, 0:128].rearrange("p (j k) -> p j k", j=2)
                    for hl in range(4):
                        jl, par = hl // 2, hl % 2
                        mm(pS[par * 64:par * 64 + 64, jl, :], Pm[:, jl, :], Sc[:, par, jl, :])
                    for par in range(2):
                        pr = slice(par * 64, par * 64 + 64)
                        tt("dve", Sn[pr, par], pS[pr], Qsb[pr], OP.add)
                    cidx[hf] += 1
                if own:
                    cp("act", Ybf[:, 2 * hf:2 * hf + 2, cs], pY)
                    act(Ysq[:, 2 * hf:2 * hf + 2, cs], pY, AF.Square)
            if not own:
                return
            for jl in range(2):
                j = 2 * hf + jl
                ps1 = PS()
                mm(ps1[:, 0:TG], blk64, Ybf[:, j, :])
                ps2 = PS()
                mm(ps2[:, 0:TG], blk64, Ysq[:, j, :])
                mean = fT.get()
                act(mean[:], ps1[:, 0:TG], AF.Copy, scale=1.0 / 64)
                msq = fT.get()
                tt("dve", msq[:], mean[:], mean[:], OP.mult)
                var = fT.get()
                stt(var[:], ps2[:, 0:TG], 1.0 / 64, msq[:], OP.mult, OP.subtract)
                ve = fT.get()
                ts("dve", ve[:], var[:], LNX_EPS, None, OP.add)
                rstd = fT.get()
                rsq(rstd[:], ve[:])
                yc = fT.get()
                tt("dve", yc[:], Ybf[:, j, :], mean[:], OP.subtract)
                yn = fT.get()
                tt("dve", yn[:], yc[:], rstd[:], OP.mult)
                yg = fT.get()
                ts("dve", yg[:], yn[:], PCc("lnx_g", j), PCc("lnx_b", j), OP.mult, OP.add)
                rk = bT.get()
                stt(rk[:], RX[hf][:, jl, :], PCc("r_k", j), KP[hf][:, jl, :], OP.mult, OP.mult)
                psb_ = PS()
                mm(psb_[:, 0:TG], blk64, rk[:])
                bv = fT.get()
                tt("dve", bv[:], psb_[:, 0:TG], VT[hf][:, jl, :], OP.mult)
                yb = fT.get()
                tt("dve", yb[:], bv[:], yg[:], OP.add)
                wt = wtile(w_in_v, 1664 + j * 128)
                ps = proj(hT, wt, slice(0, 128))
                sz = silu2(ps)
                stt(YA[:, j, :], yb[:], 0.5, sz[:], OP.mult, OP.mult)

        def YE(g):
            CUR[0] = "Y"
            INYE[0] = True
            YOWN[0] = g >= G_OWN0
            TAG[0] = "YE" + ("o" if g >= G_OWN0 else "p")
            if g < G_OWN0 - 1:
                return
            own = g >= G_OWN0
            og = g - G_OWN0
            hT = hTb[g % 2]
            if own:
                cp("dve", Kr[:, :, 0:128], Kr[:, :, TG:TG + 128])
                S.dma("sp", posi[:], posd[:, 128 + og * TG:128 + (og + 1) * TG].partition_broadcast(128))
            else:
                memset("pool", posi[:], 0)
                S.dma("sp", posi[:, TG - 128:TG], posd[:, 0:128].partition_broadcast(128))
            posf = fT.get()
            cp("dve", posf[:], posi[:])
            ang = fT.get()
            ts("dve", ang[:], posf[:], PCc("invf"), None, OP.mult)
            ts("dve", kfi[:], ang[:], 1.0 / (2 * PI), None, OP.mult)
            kff = fT.get()
            cp("dve", kff[:], kfi[:])
            rr = fT.get()
            stt(rr[:], kff[:], -2 * PI, ang[:], OP.mult, OP.add)
            wa = fT.get()
            ts("dve", wa[:], rr[:], -PI, 2 * PI, OP.is_lt, OP.mult)
            wb = fT.get()
            ts("dve", wb[:], rr[:], PI, -2 * PI, OP.is_gt, OP.mult)
            rw0 = fT.get()
            tt("dve", rw0[:], rr[:], wa[:], OP.add)
            rw = fT.get()
            tt("dve", rw[:], rw0[:], wb[:], OP.add)
            yc_ = fT.get()
            ts("dve", yc_[:], rw[:], PI / 2, None, OP.add)
            wc = fT.get()
            ts("dve", wc[:], yc_[:], PI, -2 * PI, OP.is_gt, OP.mult)
            rc = fT.get()
            tt("dve", rc[:], yc_[:], wc[:], OP.add)
            act(sinT[:], rw[:], AF.Sin)
            act(cosT[:], rc[:], AF.Sin)

            def head_norm_rope(ps, g_ap, dst, nparts_scale, do_rope=True):
                raw = fT.get()
                cp("act", raw[:], ps)
                sq = bT.get()
                act(sq[:], ps, AF.Square)
                pss = PS()
                mm(pss[:, 0:TG], blk64 if nparts_scale == 64 else ones, sq[:])
                ms = fT.get()
                ts("dve", ms[:], pss[:, 0:TG], 1.0 / nparts_scale, RMS_EPS, OP.mult, OP.add)
                rn = fT.get()
                rsq(rn[:], ms[:])
                if not do_rope:
                    stt(dst, raw[:], g_ap, rn[:], OP.mult, OP.mult)
                    return
                qn = bT.get()
                stt(qn[:], raw[:], g_ap, rn[:], OP.mult, OP.mult)
                psr = PS()
                mm(psr[:, 0:TG], C("rot"), qn[:])
                t1 = fT.get()
                tt("dve", t1[:], qn[:], cosT[:], OP.mult)
                t2 = fT.get()
                tt("dve", t2[:], psr[:, 0:TG], sinT[:], OP.mult)
                tt("dve", dst, t1[:], t2[:], OP.add)

            for kvh in range(2):
                wt = wtile(w_skd_v, kvh * 128)
                ps = proj(hT, wt, slice(0, 128))
                head_norm_rope(ps, PCc("kg"), Kr[:, kvh, 128:128 + TG], 64)
            wt = wtile(w_in_v, 2816)
            for t4 in range(NT4):
                if not own and t4 < NT4 - 1:
                    continue
                ps = PS()
                for kt in range(8):
                    mm(ps[:, 0:128], hT[:, kt, 1 + t4 * 128:1 + (t4 + 1) * 128], wt[:, kt, :], start=(kt == 0), stop=(kt == 7))
                vi = (0 if not own else 1 + og * NT4 + t4) % NV
                cp(evac_eng(), Vs[vi][:], ps[:, 0:128])
            if not own:
                return
            for j in range(4):
                wt = wtile(w_in_v, 2176 + j * 128)
                ps = proj(hT, wt, slice(0, 128))
                head_norm_rope(ps, PCc("qg"), Qr[:, j, :], 64)
            for t4 in range(NT4):
                blk = og * NT4 + t4
                qs = slice(t4 * 128, (t4 + 1) * 128)
                kc = slice(128 + t4 * 128, 128 + (t4 + 1) * 128)
                kp_ = slice(t4 * 128, (t4 + 1) * 128)
                Vc, Vp = Vs[(1 + blk) % NV], Vs[blk % NV]
                for par in range(2):
                    pr = slice(par * 64, par * 64 + 64)
                    for which, ksl, msk in (("c", kc, C("mcu")), ("p", kp_, C("mpl"))):
                        ps = PS()
                        pv = ps[:].rearrange("p (j t) -> p j t", j=4)
                        for j in range(4):
                            mm(pv[:, j, :], Kr[pr, j // 2, ksl], Qr[pr, j, qs])
                        et = eT.get()
                        act(et[:], ps[:], AF.Exp, scale=0.125)
                        dst = PEX[(which, par)]
                        if which == "p" and blk == 0:
                            stt(dst[:], et[:], PCc("fm"), msk, OP.mult, OP.mult)
                        else:
                            tt("dve", dst[:], et[:], msk, OP.mult)
                pso = PS()
                po = pso[:].rearrange("p (j t) -> p j t", j=4)
                psd = PS()
                pd = psd[:].rearrange("p (j t) -> p j t", j=4)
                for j in range(4):
                    kvh = j // 2
                    for par in range(2):
                        pr = slice(par * 64, par * 64 + 64)
                        pc_ = PEX[("c", par)][:, j * 128:(j + 1) * 128]
                        pp_ = PEX[("p", par)][:, j * 128:(j + 1) * 128]
                        mm(po[pr, j, :], Vc[:, kvh * 64:(kvh + 1) * 64], pc_, start=True, stop=False)
                        mm(po[pr, j, :], Vp[:, kvh * 64:(kvh + 1) * 64], pp_, start=False, stop=True)
                        mm(pd[pr, j, :], ones[:, 0:64], pc_, start=True, stop=False)
                        mm(pd[pr, j, :], ones[:, 0:64], pp_, start=False, stop=True)
                den = [fT.get(), fT.get()]
                for j in range(4):
                    ts("dve", den[j // 2][:, (j % 2) * 128:(j % 2 + 1) * 128], pd[:, j, :], esink[:, j:j + 1], None, OP.add)
                for hf2 in range(2):
                    rden = fT.get()
                    S.op("dve", lambda e, o=rden[:], i=den[hf2][:]: e.reciprocal(o, i), [den[hf2][:]], [rden[:]])
                    tt("dve", YB[:, 2 * hf2:2 * hf2 + 2, qs], po[:, 2 * hf2:2 * hf2 + 2, :],
                       rden[:].rearrange("p (j t) -> p j t", j=2), OP.mult)
            for j in range(4):
                wt = wtile(w_in_v, 2944 + j * 128)
                ps = proj(hT, wt, slice(0, 128))
                sz = silu2(ps)
                stt(YB[:, j, :], YB[:, j, :], 0.5, sz[:], OP.mult, OP.mult)
            for hd in range(4):
                wt = wtile(w_in_v, 3456 + hd * 128)
                ps = proj(hT, wt, slice(0, 128))
                qx = bT.get()
                head_norm_rope(ps, PCc("xqg"), qx[:], 128, do_rope=False)
                pes = []
                for mt in range(2):
                    pss = PS()
                    mm(pss[:, 0:TG], KmT[:, hd, mt * 128:(mt + 1) * 128], qx[:])
                    pe_ = bT.get()
                    act(pe_[:], pss[:, 0:TG], AF.Exp, scale=float(128 ** -0.5))
                    pes.append(pe_)
                pso = PS()
                psd = PS()
                for mt in range(2):
                    mm(pso[:, 0:TG], Vmem[:, mt, hd * 128:(hd + 1) * 128], pes[mt][:], start=(mt == 0), stop=(mt == 1))
                for mt in range(2):
                    mm(psd[:, 0:TG], ones, pes[mt][:], start=(mt == 0), stop=(mt == 1))
                rden = fT.get()
                S.op("dve", lambda e, o=rden[:], i=psd[:, 0:TG]: e.reciprocal(o, i), [psd[:, 0:TG]], [rden[:]])
                oc_ = fT.get()
                tt("dve", oc_[:], pso[:, 0:TG], rden[:], OP.mult)
                wt = wtile(w_in_v, 3968 + hd * 128)
                ps = proj(hT, wt, slice(0, 128))
                sz = silu2(ps)
                stt(YC[:, hd, :], oc_[:], 0.5, sz[:], OP.mult, OP.mult)
            for dt_ in range(8):
                acc = None
                for br, Yb in enumerate((YA, YB, YC)):
                    wt = wtile(w_in_v, 4480 + br * 1024 + dt_ * 128)
                    psg = proj(hT, wt, slice(0, 128))
                    gt = bT.get()
                    act(gt[:], psg, AF.Tanh, scale=0.5)
                    wpt = wtile(wp_v[br], dt_ * 128, nk=4)
                    psp = PS()
                    for kt in range(4):
                        mm(psp[:, 0:TG], wpt[:, kt, :], Yb[:, kt, :], start=(kt == 0), stop=(kt == 3))
                    tmp = fT.get()
                    stt(tmp[:], gt[:], 1.0, psp[:, 0:TG], OP.add, OP.mult)
                    if acc is None:
                        acc = tmp
                    elif br == 2:
                        tt("dve", MG[:, dt_, :], acc[:], tmp[:], OP.add)
                    else:
                        acc2 = fT.get()
                        tt("dve", acc2[:], acc[:], tmp[:], OP.add)
                        acc = acc2
            for t4 in range(NT4):
                xr = xin.get()
                r0 = og * TG + t4 * 128
                S.dma("sp", xr[:], xown[r0:r0 + 128, :])
                for ct in range(8):
                    wot = wtile(w_o_v, ct * 128)
                    ps = PS()
                    for kt in range(8):
                        mm(ps[:, 0:128], MG[:, kt, t4 * 128:(t4 + 1) * 128], wot[:, kt, :], start=(kt == 0), stop=(kt == 7))
                    stt(xr[:, ct * 128:(ct + 1) * 128], ps[:, 0:128], 0.5, xr[:, ct * 128:(ct + 1) * 128], OP.mult, OP.add)
                S.dma("sp", yout[r0:r0 + 128, :], xr[:], is_output=True)

        def rec(*fns):
            S.rec_start()
            for f in fns:
                f()
            return S.rec_stop()

        S.play(rec(lambda: XC(0), lambda: XH(0, 0)))
        for g in range(n_groups):
            S.fill_on = FILL_ALL or g > G_OWN0
            S.play(S.schedule(rec(lambda: YH(g, 0), lambda: XH(g, 1))))
            S.fill_on = FILL_ALL or g >= G_OWN0
            fns = [lambda: YH(g, 1), lambda: YE(g)]
            if g + 1 < n_groups:
                fns += [lambda: XC(g + 1), lambda: XH(g + 1, 0)]
            S.play(S.schedule(rec(*fns)))

        if dbg:
            print('SBUF bytes/partition', _sbytes[0], 'counts', S.cnt); print('tagcost us', {k: (v[0], round(v[1] / 1000)) for k, v in sorted(S.tagcost.items())}); print('model time us', max(S.etime.values()) / 1000, 'fillers', S.nfill); print('PE cols by tag', {k: (v[0], v[1], round(v[1] / 1.2 / 1000)) for k, v in PEC.items()}); print(sorted(_sblist, key=lambda t: -t[0]))
        S.finish()
        S.emit()
    return nc


def _host_inputs(inputs):
    f = lambda a: np.ascontiguousarray(np.asarray(a))
    x = f(inputs["x"])
    mem = f(inputs["mem"])
    pos = f(inputs["positions"])
    w_in = f(inputs["w_in"][0])
    cm, cf = _consts()
    sk = w_in[:, 2688:2816]
    w_skd = np.ascontiguousarray(np.concatenate([sk[:, 0:64], sk[:, 0:64], sk[:, 64:128], sk[:, 64:128]], axis=1))
    w2a2 = np.ascontiguousarray(np.concatenate([inputs["w2"][0], inputs["a2"][0]], axis=0)).astype(np.float32)

    def c4(v):
        return np.asarray(v, np.float32).reshape(4, 128).T

    def dup(v, n):
        return np.tile(np.asarray(v, np.float32).reshape(-1), n).reshape(128, 1)
    shared = dict(
        mem=None, w_in=w_in, w_skd=w_skd, w_mkv=f(inputs["w_mem_kv"][0]), w_pa=f(inputs["w_proj_a"][0]),
        w_pb=f(inputs["w_proj_b"][0]), w_pc=f(inputs["w_proj_c"][0]), w_o=f(inputs["w_out"][0]), w2a2=w2a2,
        g_row=f(inputs["norm_g"]).reshape(1, D), gm_row=f(inputs["mem_norm_g"]).reshape(1, D),
        muv_row=f(inputs["mu_rkv"][0, 2]).reshape(1, 512), cm=cm, cf=cf)
    invf = (np.float32(10000.0) ** (-(np.arange(32, dtype=np.float32) / np.float32(32)))).astype(np.float32)
    maps = []
    for c in range(NCORES):
        b, q = c // 4, c % 4
        end = OWN * (q + 1)
        xw = np.zeros((T, D), np.float32)
        xw[T - end:] = x[b, :end]
        pp = np.zeros((1, OWN + 128), np.int32)
        s0 = end - OWN - 128
        if s0 >= 0:
            pp[0] = pos[b, s0:end]
        else:
            pp[0, 128:] = pos[b, 0:end]
        pcv = np.zeros((128, NPC), np.float32)

        def put(name, a):
            o, n = PCN[name]
            pcv[:, o:o + n] = a
        put("mu_r", c4(inputs["mu_rkv"][0, 0]))
        put("mu_k", c4(inputs["mu_rkv"][0, 1]))
        put("mu_l", np.concatenate([inputs["mu_wa"][0, 0], inputs["mu_wa"][0, 1]]).reshape(128, 1))
        put("w0", c4(inputs["w0"][0]))
        put("a0", c4(inputs["a0"][0]))
        put("k_k", c4(inputs["k_k"][0]))
        put("k_a", c4(inputs["k_a"][0]))
        put("r_k", c4(np.asarray(inputs["r_k"][0]).reshape(-1)))
        put("lnx_g", c4(inputs["lnx_g"][0]))
        put("lnx_b", c4(inputs["lnx_b"][0]))
        put("qg", dup(inputs["q_norm_g"][0], 2))
        put("kg", dup(inputs["k_norm_g"][0], 2))
        put("sink", np.repeat(np.asarray(inputs["sinks"][0], np.float32).reshape(4, 2).T, 64, axis=0))
        put("xqg", np.asarray(inputs["xq_norm_g"][0], np.float32).reshape(128, 1))
        put("xkg", np.asarray(inputs["xk_norm_g"][0], np.float32).reshape(128, 1))
        put("invf", np.tile(invf, 4).reshape(128, 1))
        put("fm", np.full((128, 1), 0.0 if q == 0 else 1.0, np.float32))
        m = dict(shared)
        m.update(xw=xw, xown=np.ascontiguousarray(x[b, end - OWN:end]), pos=pp, mem=mem[b], pc=pcv)
        maps.append(m)
    return maps


_NC_CACHE = {}


def kernel(**inputs):
    inputs = {k: np.asarray(v) for k, v in inputs.items()}
    if "nc" not in _NC_CACHE:
        _NC_CACHE["nc"] = build_program()
    nc = _NC_CACHE["nc"]
    maps = _host_inputs(inputs)
    res = run_bass_kernel_spmd(nc, maps, core_ids=list(range(NCORES)))
    out = np.zeros((2, T, D), np.float32)
    for c in range(NCORES):
        b, q = c // 4, c % 4
        out[b, q * OWN:(q + 1) * OWN] = res.results[c]["y"]
    return out
```

```python
from contextlib import ExitStack
import numpy as np
import ml_dtypes
import concourse.bass as bass
import concourse.mybir as mybir
from concourse.bass_utils import run_bass_kernel_spmd

F32 = mybir.dt.float32
BF16 = mybir.dt.bfloat16
I32 = mybir.dt.int32
AF = mybir.ActivationFunctionType
OP = mybir.AluOpType

NCORES = 8
D = 1024
T = 8192
OWN = 2048
TG = 256
NT4 = TG // 128
NG = T // TG
G_OWN0 = NG - OWN // TG
RMS_EPS = 1e-6
LNX_EPS = 64e-5
C0 = -float(np.exp(-0.5))
PI = float(np.pi)
FILL_MIN, FILL_CAP, FILL_SCALE = 300.0, 24, 1.3
FILL_DENSITY = 1.3
SCHED_BUCKET = 150.0
FILL_ALL = False
Y_PREFIX_6 = True
YE_6 = True
SEM_LAT = 300.0
PREFIX_W_ACC = True
OWN_W_ACC = False
POOL_RECIP = False
PE_COLD_GHZ = 1.2
PE_WARM_GHZ = 1.8

PCN = {}
def _pcdef():
    o = 0
    for name, n in [("mu_r", 4), ("mu_k", 4), ("mu_l", 1), ("w0", 4), ("a0", 4), ("k_k", 4), ("k_a", 4), ("r_k", 4),
                    ("lnx_g", 4), ("lnx_b", 4), ("qg", 1), ("kg", 1), ("sink", 4), ("xqg", 1), ("xkg", 1), ("invf", 1),
                    ("fm", 1)]:
        PCN[name] = (o, n)
        o += n
    return o
NPC = _pcdef()

CMN = {}
def _cmdef():
    o = 0
    for name, n in [("ident", 128), ("blk64", 128), ("ones", 128), ("msu", 256), ("miu", 256), ("msl", 256),
                    ("mcu", 512), ("mpl", 512), ("rot", 128), ("d0", 64)]:
        CMN[name] = (o, n)
        o += n
    return o
NCM = _cmdef()


def _consts():
    cm = np.zeros((128, NCM), np.float32)
    def put(name, a):
        o, n = CMN[name]
        assert a.shape == (128, n), (name, a.shape)
        cm[:, o:o + n] = a
    i = np.arange(128)
    r, c = i[:, None], i[None, :]
    same = (r // 64) == (c // 64)
    put("ident", (r == c).astype(np.float32))
    put("blk64", same.astype(np.float32))
    put("ones", np.ones((128, 128), np.float32))
    put("msu", np.tile(((r < c) & same).astype(np.float32), (1, 2)))
    put("miu", np.tile(((r <= c) & same).astype(np.float32), (1, 2)))
    put("msl", np.tile(((r > c) & same).astype(np.float32), (1, 2)))
    put("mcu", np.tile((c >= r).astype(np.float32), (1, 4)))
    put("mpl", np.tile((c < r).astype(np.float32), (1, 4)))
    rot = np.zeros((128, 128), np.float32)
    for cc in range(128):
        if cc % 64 < 32:
            rot[cc + 32, cc] = -1.0
        else:
            rot[cc - 32, cc] = 1.0
    put("rot", rot)
    put("d0", ((r % 64) == np.arange(64)[None, :]).astype(np.float32)[:, :64])
    cf = np.zeros((128, 2 * TG), np.float32)
    rm = np.ones(TG, np.float32)
    rm[::64] = 0.0
    cf[:, 0:TG] = rm[None, :]
    cf[:, TG:2 * TG] = ((np.arange(TG) % 64) + 1).astype(np.float32)[None, :]
    return cm.astype(ml_dtypes.bfloat16), cf


class Sched:
    def __init__(self, nc, stack):
        self.nc, self.stack = nc, stack
        self.names = ["pe", "act", "dve", "pool", "sp"]
        self.prog = {e: [] for e in self.names}
        self.cnt = {e: 0 for e in self.names}
        self.seg = 12000
        self.sems = {e: [] for e in self.names}
        self.seen = {e: {} for e in self.names}
        self.lastw, self.readers = {}, {}
        self.ND = 16
        self.dnext_pool = 0
        self.barrier_names = set()
        self.barrier_done = set()
        self.convsem = stack.enter_context(nc.semaphore("convsem"))
        self.nconv = 0
        self.dsems = [stack.enter_context(nc.semaphore(f"dsem{i}")) for i in range(self.ND)]
        self.dcnt = [0] * self.ND
        self.dlast = [None] * self.ND
        self.dnext = 0
        self.out_tokens = []
        self.rec = None
        self.filler = None
        self.fill_on = False
        self.cur_tset = "A"
        self.tagcost = {}
        self.curtag = ['?']
        self.nfill = 0
        self.fill_ns = 0.0
        self.etime = {e: 0.0 for e in self.names}

    def rec_start(self):
        self.rec = []

    def rec_stop(self):
        r, self.rec = self.rec, None
        return r

    def play(self, lst):
        for it in lst:
            if it[0] == "op":
                self.op(*it[1:5])
            elif it[0] == "conv":
                self._conv_dma(it[1], it[2])
            elif it[0] == "fill":
                for _ in range(it[1]):
                    self.prog["pe"].append(([], self.filler, None, 0))
                self.nfill += it[1]
            else:
                self.dma(*it[1:5])

    def schedule(self, lst):
        n = len(lst)
        lastw, readers = {}, {}
        preds = [set() for _ in range(n)]
        eng, cost, lat = [None] * n, [0.0] * n, [0.0] * n
        tset = [None] * n
        for i, it in enumerate(lst):
            if it[0] == "op":
                _, e, fn, ins, outs, c = it
                ins = [a for a in ins if a is not None and not isinstance(a, (int, float))]
                fs = 1
                for d_ in outs[0].shape[1:]:
                    fs *= int(d_)
                if c is None:
                    if e == "pe":
                        c = max(64, fs) / (PE_WARM_GHZ if self.fill_on else PE_COLD_GHZ) + 8
                    elif e == "act":
                        c = 200 + fs / 1.2
                    elif e == "dve":
                        c = 70 + fs * 1.04
                    else:
                        c = 600 + fs
                eng[i], cost[i], lat[i] = e, c, c + (200 if e == "pe" else 60)
                tset[i] = getattr(fn, "_tset", None)
                tg_ = getattr(fn, "_tag", "?")
                acc_ = self.tagcost.setdefault((tg_, e), [0, 0.0])
                acc_[0] += 1
                acc_[1] += c
            elif it[0] == "conv":
                ins, outs = [], []
                eng[i], cost[i], lat[i] = "pool", 1000.0, 1000.0
            else:
                _, q, out, in_, _io = it
                ins, outs = [in_], [out]
                eng[i], cost[i], lat[i] = q, 80.0, 2500.0
            for a in ins:
                k = self.key(a)
                if k in lastw:
                    preds[i].add(lastw[k])
            for a in outs:
                k = self.key(a)
                if k in lastw:
                    preds[i].add(lastw[k])
                preds[i].update(readers.get(k, ()))
            for a in ins:
                readers.setdefault(self.key(a), set()).add(i)
            for a in outs:
                k = self.key(a)
                lastw[k] = i
                readers[k] = set()
            preds[i].discard(i)
        succs = [[] for _ in range(n)]
        npred = [len(p_) for p_ in preds]
        for i in range(n):
            for p_ in preds[i]:
                succs[p_].append(i)
        blev = [0.0] * n
        for i in range(n - 1, -1, -1):
            m_ = 0.0
            for s_ in succs[i]:
                if blev[s_] > m_:
                    m_ = blev[s_]
            blev[i] = lat[i] + m_
        etime = dict(self.etime)
        t0 = max(etime.values())
        for e in etime:
            etime[e] = max(etime[e], t0 - 2000.0)
        fin = [0.0] * n
        ready = [0.0] * n
        avail = [i for i in range(n) if npred[i] == 0]
        order = []
        while avail:
            best, bi = None, None
            for i in avail:
                st = max(etime[eng[i]], ready[i])
                if tset[i] is not None and self.cur_tset not in tset[i]:
                    st += 1300.0
                key_ = (st // SCHED_BUCKET, -blev[i], i) if SCHED_BUCKET else (st, i)
                if best is None or key_ < best:
                    best, bi = key_, i
            avail.remove(bi)
            st = max(etime[eng[bi]], ready[bi])
            if tset[bi] is not None and self.cur_tset not in tset[bi]:
                self.cur_tset = tset[bi][0]
            if eng[bi] == "pe" and self.filler is not None and self.fill_cap > 0 and self.fill_on:
                gap = (st - etime["pe"]) * self.fill_scale
                if gap > self.fill_min:
                    nf = min(self.fill_cap, int(gap * FILL_DENSITY / 512.0 + 0.5))
                    if nf > 0:
                        order.append(("fill", nf))
            etime[eng[bi]] = st + cost[bi]
            fin[bi] = st + lat[bi]
            order.append(bi)
            for s_ in succs[bi]:
                r = fin[bi] + (SEM_LAT if eng[s_] != eng[bi] else 0.0)
                if r > ready[s_]:
                    ready[s_] = r
                npred[s_] -= 1
                if npred[s_] == 0:
                    avail.append(s_)
        if getattr(self, "dbgwin", 0) > 0:
            self.dbgwin -= 1
            bus = {}
            for i in range(n):
                bus[eng[i]] = bus.get(eng[i], 0.0) + cost[i]
            print("window n=%d span=%.1fus busy=%s" % (n, (max(etime.values()) - t0) / 1000, {k: round(v / 1000, 1) for k, v in bus.items()}))
        self.etime = etime
        return [x if isinstance(x, tuple) else lst[x] for x in order]

    @staticmethod
    def merge(a, b):
        out, i, j = [], 0, 0
        na, nb = len(a), len(b)
        while i < na or j < nb:
            if j >= nb or (i < na and i * nb <= j * na):
                out.append(a[i]); i += 1
            else:
                out.append(b[j]); j += 1
        return out

    def _semfor(self, e, k):
        si = (k - 1) // self.seg
        while len(self.sems[e]) <= si:
            self.sems[e].append(self.stack.enter_context(self.nc.semaphore(f"s_{e}_{len(self.sems[e])}")))
        return self.sems[e][si], (k - 1) % self.seg + 1, si

    @staticmethod
    def key(ap):
        return ap.tensor.name

    def _waits(self, e, toks):
        best = {}
        for t in toks:
            if t is None:
                continue
            if t[0] == "c":
                _, f, k = t
                if f == e and e == "pe":
                    continue
                sem, val, si = self._semfor(f, k)
                kk = ("c", f)
                cur = best.get(kk)
                if cur is None or (si, val) > (cur[0], cur[1]):
                    best[kk] = (si, val, sem)
            else:
                _, i, val = t
                kk = ("d", i)
                cur = best.get(kk)
                if cur is None or val > cur[1]:
                    best[kk] = (0, val, self.dsems[i])
        wl = []
        for kk, (si, val, sem) in best.items():
            if self.seen[e].get(kk, (-1, 0)) >= (si, val):
                continue
            self.seen[e][kk] = (si, val)
            wl.append((sem, val))
        return wl

    def _deps(self, ins, outs):
        toks = []
        for a in ins:
            toks.append(self.lastw.get(self.key(a)))
        for a in outs:
            k = self.key(a)
            toks.append(self.lastw.get(k))
            toks.extend(self.readers.get(k, {}).values())
        return toks

    def _commit(self, tok, rid, ins, outs):
        for a in ins:
            self.readers.setdefault(self.key(a), {})[rid] = tok
        for a in outs:
            k = self.key(a)
            self.lastw[k] = tok
            self.readers[k] = {}

    def op(self, e, fn, ins, outs, cost=None):
        if self.rec is not None:
            try:
                fn._tag = self.curtag[0]
            except Exception:
                pass
            self.rec.append(("op", e, fn, ins, outs, cost))
            return
        ins = [a for a in ins if a is not None and not isinstance(a, (int, float))]
        wl = self._waits(e, self._deps(ins, outs))
        k = self.cnt[e] + 1
        self.cnt[e] = k
        sem, _, _ = self._semfor(e, k)
        self.prog[e].append((wl, fn, sem, 1))
        self._commit(("c", e, k), e, ins, outs)

    def dma(self, q, out, in_, is_output=False):
        if self.rec is not None:
            self.rec.append(("dma", q, out, in_, is_output))
            return
        half = self.ND // 2
        if q == "pool":
            i = half + self.dnext_pool
            self.dnext_pool = (self.dnext_pool + 1) % (self.ND - half)
        else:
            i = self.dnext
            self.dnext = (self.dnext + 1) % half
        toks = self._deps([in_], [out]) + [self.dlast[i]]
        wl = self._waits(q, toks)
        if self.key(in_) in self.barrier_names and q not in self.barrier_done:
            wl = wl + [(self.convsem, 16 * self.nconv)]
            self.barrier_done.add(q)
        self.dcnt[i] += 1
        tok = ("d", i, 16 * self.dcnt[i])
        self.dlast[i] = tok
        self.prog[q].append((wl, lambda eng, o=out, a=in_: eng.dma_start(out=o, in_=a), self.dsems[i], 16))
        self._commit(tok, ("d", i), [in_], [out])
        if is_output:
            self.out_tokens.append(tok)

    def conv_dma(self, out, in_):
        if self.rec is not None:
            self.rec.append(("conv", out, in_))
            return
        self._conv_dma(out, in_)

    def _conv_dma(self, out, in_):
        self.nconv += 1
        self.prog["pool"].append(([], lambda eng, o=out, a=in_: eng.dma_start(out=o, in_=a), self.convsem, 16))

    def finish(self):
        wl = self._waits("sp", self.out_tokens + [t for t in self.dlast if t is not None])
        self.prog["sp"].append((wl, None, None, 0))

    def emit(self):
        nc = self.nc
        engs = {"pe": "tensor", "act": "scalar", "dve": "vector", "pool": "gpsimd", "sp": "sync"}
        with nc.Block() as block:
            for e, bn in engs.items():
                prog = self.prog[e]

                def body(eng, prog=prog):
                    for wl, fn, sem, inc in prog:
                        for ws, wv in wl:
                            eng.wait_ge(ws, wv)
                        if fn is not None:
                            ins_ = fn(eng)
                            if sem is not None:
                                ins_.then_inc(sem, inc)
                getattr(block, bn)(body)


def build_program(n_groups=NG, g_own0=G_OWN0, dbg=False):
    G_OWN0 = g_own0
    nc = bass.Bass("TRN2", target_bir_lowering=False)
    st = ExitStack()
    with st:
        S = Sched(nc, st)

        def dram(name, shape, dt, kind="ExternalInput"):
            return nc.dram_tensor(name, list(shape), dt, kind=kind).ap()
        xw = dram("xw", [T, D], F32)
        xown = dram("xown", [OWN, D], F32)
        posd = dram("pos", [1, OWN + 128], I32)
        memd = dram("mem", [256, D], F32)
        w_in = dram("w_in", [D, 7552], F32)
        w_skd = dram("w_skd", [D, 256], F32)
        w_mkv = dram("w_mkv", [D, 1024], F32)
        w_pa = dram("w_pa", [512, D], F32)
        w_pb = dram("w_pb", [512, D], F32)
        w_pc = dram("w_pc", [512, D], F32)
        w_o = dram("w_o", [D, D], F32)
        w2a2d = dram("w2a2", [128, 512], F32)
        pcd = dram("pc", [128, NPC], F32)
        growd = dram("g_row", [1, D], F32)
        gmrowd = dram("gm_row", [1, D], F32)
        muvd = dram("muv_row", [1, 512], F32)
        cmd = dram("cm", [128, NCM], BF16)
        cfd = dram("cf", [128, 2 * TG], F32)
        yout = dram("y", [OWN, D], F32, kind="ExternalOutput")

        _n = [0]

        _sbytes = [0]
        _sblist = []

        def sb(shape, dt, name=None):
            _n[0] += 1
            _sbytes[0] += int(np.prod(shape[1:])) * (2 if dt == BF16 else 4)
            _sblist.append((int(np.prod(shape[1:])) * (2 if dt == BF16 else 4), name))
            return st.enter_context(nc.sbuf_tensor("sb_" + (name or f"t{_n[0]}"), list(shape), dt))

        class Ring:
            def __init__(self, n, shape, dt, name):
                self.t = [sb(shape, dt, f"{name}{i}") for i in range(n)]
                self.i = 0

            def get(self):
                t = self.t[self.i % len(self.t)]
                self.i += 1
                return t

        psum = [st.enter_context(nc.psum_tensor(f"ps{i}", [128, 512], F32)) for i in range(8)]
        CUR = ["Y"]
        PSB = {"X": [6, 7], "Y": [0, 1, 2, 3, 4], "Y0": [0, 1, 2], "Y1": [3, 4, 5]}
        PSI = {"X": 0, "Y": 0, "Y0": 0, "Y1": 0}
        PSY_BANK = psum[5]
        JUNK = psum[7]
        S.fill_min, S.fill_cap, S.fill_scale = FILL_MIN, FILL_CAP, FILL_SCALE

        XOWN = [False]
        YOWN = [False]
        INYE = [False]

        def PS():
            c = CUR[0]
            if c == "X" and XOWN[0] and FILL_CAP > 0:
                return psum[6]
            banks = PSB[c]
            if c == "Y" and (not YOWN[0] or (INYE[0] and YE_6)) and Y_PREFIX_6:
                banks = [0, 1, 2, 3, 4, 5]
            t = psum[banks[PSI[c] % len(banks)]]
            PSI[c] += 1
            return t

        class SRing:
            def __init__(self, nx, ny, shape, dt, name):
                self.r = {"X": Ring(nx, shape, dt, name + "x") if nx else None,
                          "Y": Ring(ny, shape, dt, name + "y") if ny else None}

            def get(self):
                return self.r[CUR[0][0]].get()

        PEC = {}
        TAG = S.curtag
        TAG[0] = "setup"

        def _pec(out):
            fs = 1
            for d_ in out.shape[1:]:
                fs *= int(d_)
            k = TAG[0]
            c = PEC.setdefault(k, [0, 0])
            c[0] += 1
            c[1] += max(64, fs)

        def mm(out, lhsT, rhs, start=True, stop=True):
            _pec(out)
            S.op("pe", lambda e: e.matmul(out, lhsT, rhs, start=start, stop=stop), [lhsT, rhs], [out])

        def tr(out, in_, ident):
            _pec(out)
            S.op("pe", lambda e: e.transpose(out, in_, ident), [in_, ident], [out])

        TSET = {AF.Exp: ("A", "B"), AF.Tanh: ("A",), AF.Ln: ("B",), AF.Sin: ("C",)}

        def act(out, in_, func, bias=None, scale=None, accum=None):
            kw = {}
            if bias is not None:
                kw["bias"] = bias
            if scale is not None:
                kw["scale"] = scale
            if accum is not None:
                kw["accum_out"] = accum
            outs = [out] + ([accum] if accum is not None else [])
            fn_ = lambda e: e.activation(out, in_, func, **kw)
            fn_._tset = TSET.get(func)
            S.op("act", fn_, [in_, bias, scale], outs)

        def tt(eng, out, a, b, op):
            S.op(eng, lambda e: e.tensor_tensor(out, a, b, op), [a, b], [out])

        def ts(eng, out, a, s1, s2, op0, op1=None):
            if op1 is None:
                S.op(eng, lambda e: e.tensor_scalar(out, a, s1, None, op0), [a, s1], [out])
            else:
                S.op(eng, lambda e: e.tensor_scalar(out, a, s1, s2, op0, op1), [a, s1, s2], [out])

        def stt(out, a, s, b, op0, op1):
            S.op("dve", lambda e: e.scalar_tensor_tensor(out, a, s, b, op0, op1), [a, s, b], [out])

        def cp(eng, out, in_):
            if eng == "act":
                act(out, in_, AF.Copy)
            else:
                S.op(eng, lambda e: e.tensor_copy(out, in_), [in_], [out])

        def memset(eng, out, val):
            S.op(eng, lambda e: e.memset(out, val), [], [out])

        _rr = [0]

        def evac_eng():
            _rr[0] += 1
            return "act" if _rr[0] % 2 else "dve"

        cm = sb([128, NCM], BF16, "cm")
        cf = sb([128, 2 * TG], F32, "cf")
        pc = sb([128, NPC], F32, "pc")
        grow = sb([128, D], F32, "grow")
        Ybf = sb([128, 4, TG], BF16, "Ybf")
        Ysq = sb([128, 4, TG], BF16, "Ysq")
        muv = Ybf[:].bitcast(F32).rearrange("p j t -> p (j t)")
        omuv = Ysq[:].bitcast(F32).rearrange("p j t -> p (j t)")
        S.dma("sp", cm[:], cmd)
        S.dma("sp", cf[:], cfd)
        S.dma("sp", pc[:], pcd)
        S.dma("sp", grow[:], gmrowd.partition_broadcast(128))
        S.dma("sp", muv, muvd.partition_broadcast(128))

        def C(name):
            o, n = CMN[name]
            return cm[:, o:o + n]

        def PCc(name, j=0):
            o, n = PCN[name]
            return pc[:, o + j:o + j + 1]
        ident = C("ident")
        S.filler = lambda e: e.matmul(JUNK[:, 0:512], ident, cm[:, 0:512], start=True, stop=True)
        blk64 = C("blk64")
        ones = C("ones")
        rmask = cf[:, 0:TG]
        idx1 = cf[:, TG:2 * TG]

        def rsq(out, in_):
            act(out, in_, AF.Ln)
            act(out, out, AF.Exp, scale=-0.5)

        ts("dve", omuv, muv, -1.0, 1.0, OP.mult, OP.add)
        esink = sb([128, 4], F32, "esink")
        o_s, _ = PCN["sink"]
        act(esink[:], pc[:, o_s:o_s + 4], AF.Exp)
        o_ka, _ = PCN["k_a"]
        hka = sb([128, 4], F32, "hka")
        nhka = sb([128, 4], F32, "nhka")
        ts("dve", hka[:], pc[:, o_ka:o_ka + 4], 0.5, None, OP.mult)
        ts("dve", nhka[:], pc[:, o_ka:o_ka + 4], -0.5, None, OP.mult)
        hw0 = sb([128, 4], F32, "hw0")
        ha0 = sb([128, 4], F32, "ha0")
        ts("dve", hw0[:], pc[:, PCN["w0"][0]:PCN["w0"][0] + 4], 0.5, None, OP.mult)
        ts("dve", ha0[:], pc[:, PCN["a0"][0]:PCN["a0"][0] + 4], 0.5, None, OP.mult)
        mhalf = sb([128, 1], F32, "mhalf")
        memset("pool", mhalf[:], -0.5)
        mone = sb([128, 1], F32, "mone")
        memset("pool", mone[:], -1.0)

        w_in_v = w_in.rearrange("(kt p) c -> p kt c", p=128)
        Wk = sb([128, 8, 640], BF16, "Wk")
        S.dma("pool", Wk[:, :, 0:512], w_in_v[:, :, 512:1024])
        S.dma("pool", Wk[:, :, 512:640], w_in_v[:, :, 1536:1664])
        Wv1 = sb([128, 8, 512], BF16, "Wv1")
        Wv2 = sb([128, 8, 512], BF16, "Wv2")
        xin = SRing(2, 1, [128, D], F32, "xin")
        for ct in range(4):
            stg = xin.get()
            stg3 = stg[:].rearrange("p (k c) -> p k c", k=8)
            S.dma("sp", stg3, w_in_v[:, :, 1024 + ct * 128:1024 + (ct + 1) * 128])
            for kt in range(8):
                tt("dve", Wv1[:, kt, ct * 128:(ct + 1) * 128], stg3[:, kt, :], omuv[:, ct * 128:(ct + 1) * 128], OP.mult)
                tt("dve", Wv2[:, kt, ct * 128:(ct + 1) * 128], stg3[:, kt, :], muv[:, ct * 128:(ct + 1) * 128], OP.mult)
        w2a2 = sb([128, 512], BF16, "w2a2")
        S.dma("pool", w2a2[:], w2a2d)
        wp_v = [w.rearrange("(kt p) c -> p kt c", p=128) for w in (w_pa, w_pb, w_pc)]
        w_o_v = w_o.rearrange("(kt p) c -> p kt c", p=128)
        wring = SRing(2, 6, [128, 8, 128], BF16, "wstream")
        BFV = {}
        CONV = []
        for nm_, src_, rows_, cols_ in (("wbf_in", w_in, D, 7552), ("wbf_skd", w_skd, D, 256), ("wbf_pa", w_pa, 512, D),
                                       ("wbf_pb", w_pb, 512, D), ("wbf_pc", w_pc, 512, D), ("wbf_o", w_o, D, D)):
            nkt, ntl = rows_ // 128, cols_ // 128
            scr = nc.dram_tensor(nm_, [ntl, 128, nkt, 128], BF16, kind="Internal").ap()
            S.barrier_names.add(nm_)
            TCH = 15
            for kt_ in range(nkt):
                for t0_ in range(0, ntl, TCH):
                    nt_ = min(TCH, ntl - t0_)
                    src_ap = src_[kt_ * 128:(kt_ + 1) * 128, t0_ * 128:(t0_ + nt_) * 128].rearrange("p (t c) -> p t c", c=128)
                    dst_ap = scr[t0_:t0_ + nt_, :, kt_, :].rearrange("t p c -> p t c")
                    CONV.append((dst_ap, src_ap))
            BFV[src_.tensor.name] = scr

        wpring = Ring(3, [128, 4, 128], BF16, "wpstream")

        def wtile(src_view, c0, nk=8):
            t_ = wpring.get() if nk == 4 else wring.get()
            bv = BFV.get(src_view.tensor.name)
            if bv is not None:
                S.dma("sp", t_[:, 0:nk, :], bv[c0 // 128])
            else:
                S.dma("pool", t_[:, 0:nk, :], src_view[:, :, c0:c0 + 128])
            return t_

        xnr = Ring(2, [128, D], BF16, "xn")
        col = Ring(8, [128, 1], F32, "col")
        fT = SRing(10, 12, [128, TG], F32, "fT")
        bT = SRing(4, 8, [128, TG], BF16, "bT")
        eT = Ring(2, [128, 512], BF16, "eT")

        def norm_transpose(src_dram_rows, dst, dcol0):
            xt = xin.get()
            S.dma("sp", xt[:], src_dram_rows)
            ss = col.get()
            xn = xnr.get()
            act(xn[:], xt[:], AF.Square, accum=ss[:])
            rs = col.get()
            ts("dve", rs[:], ss[:], 1.0 / D, RMS_EPS, OP.mult, OP.add)
            rstd = col.get()
            S.op("pool", lambda e, o=rstd[:], a=rs[:], b=mhalf[:]: e.tensor_tensor(o, a, b, OP.pow), [rs[:], mhalf[:]], [rstd[:]], cost=1500.0)
            stt(xn[:], xt[:], rstd[:], grow[:], OP.mult, OP.mult)
            ps = PS()
            psb = ps[:].bitcast(BF16).rearrange("p (k t) -> p k t", k=8)
            for kt in range(8):
                tr(psb[:, kt, :], xn[:, kt * 128:(kt + 1) * 128], ident)
            cp("act", dst[:, :, dcol0:dcol0 + 128], psb)

        MG = sb([128, 8, TG], BF16, "MG")
        memT = MG
        for mt in range(2):
            norm_transpose(memd[mt * 128:(mt + 1) * 128, :], memT, mt * 128)
        S.dma("sp", grow[:], growd.partition_broadcast(128))
        KmT = sb([128, 4, 256], BF16, "KmT")
        Vmem = sb([128, 2, 512], BF16, "Vmem")
        w_mkv_v = w_mkv.rearrange("(kt p) c -> p kt c", p=128)
        for hd in range(4):
            wt = wtile(w_mkv_v, hd * 128)
            ps = PS()
            for kt in range(8):
                mm(ps[:, 0:256], wt[:, kt, :], memT[:, kt, :], start=(kt == 0), stop=(kt == 7))
            sq = bT.get()
            act(sq[:, 0:256], ps[:, 0:256], AF.Square)
            ps2 = PS()
            mm(ps2[:, 0:256], ones, sq[:, 0:256])
            ms = fT.get()
            ts("dve", ms[:, 0:256], ps2[:, 0:256], 1.0 / 128, RMS_EPS, OP.mult, OP.add)
            rn = fT.get()
            rsq(rn[:, 0:256], ms[:, 0:256])
            stt(KmT[:, hd, :], ps[:, 0:256], PCc("xkg"), rn[:, 0:256], OP.mult, OP.mult)
        for ct in range(4):
            wt = wtile(w_mkv_v, 512 + ct * 128)
            for mt in range(2):
                ps = PS()
                for kt in range(8):
                    mm(ps[:, 0:128], memT[:, kt, mt * 128:(mt + 1) * 128], wt[:, kt, :], start=(kt == 0), stop=(kt == 7))
                cp(evac_eng(), Vmem[:, mt, ct * 128:(ct + 1) * 128], ps[:, 0:128])

        NCG = max(0, G_OWN0 - 1)
        CPG = (len(CONV) + NCG - 1) // NCG if NCG else 0
        if not NCG:
            for o_, i_ in CONV:
                S.conv_dma(o_, i_)

        NCH = TG // 64
        hTb = [sb([128, 8, TG + 1], BF16, f"hT{i}") for i in range(2)]
        memset("pool", hTb[0][:], 0.0)
        memset("pool", hTb[1][:], 0.0)
        Uk = [sb([128, TG + 1], F32, f"Uk{j}") for j in range(4)]
        Ul = sb([128, TG + 1], F32, "Ul")
        Ur = [sb([128, TG + 1], F32, f"Ur{j}") for j in range(4)]
        for u in Uk + [Ul] + Ur:
            memset("pool", u[:], 0.0)
        lt = sb([128, TG], BF16, "lt")
        def H2(name, dt=BF16):
            return [sb([128, 2, TG], dt, f"{name}{hf}") for hf in range(2)]
        AtT, BtT, KtT, RtT, KhT, BhT = H2("AtT"), H2("BtT"), H2("KtT"), H2("RtT"), H2("KhT"), H2("BhT")
        KP, RX, VT = H2("KP"), H2("RX"), H2("VT")
        gam = [sb([128, 2, NCH], F32, f"gam{hf}") for hf in range(2)]
        TOKA = [[sb([128, 512], BF16, f"TOKA{hf}_{i}") for i in range(NT4)] for hf in range(2)]
        TOKK = [[sb([128, 256], BF16, f"TOKK{hf}_{i}") for i in range(NT4)] for hf in range(2)]
        Vtok = [[sb([128, 256], BF16, f"Vtok{hf}_{i}") for i in range(NT4)] for hf in range(2)]
        N_bt = [[sb([128, 4, 128], BF16, f"Nb{t}_{i}") for i in range(2)] for t in range(NT4)]
        Z_bt = [[sb([128, 4, 128], BF16, f"Zb{t}_{i}") for i in range(2)] for t in range(NT4)]
        W_bt = [[sb([128, 4, 128], BF16, f"Wb{t}_{i}") for i in range(2)] for t in range(NT4)]
        Makt = [sb([128, 4, 128], BF16, f"Mak{t}") for t in range(NT4)]
        Mrbt = [sb([128, 4, 128], BF16, f"Mrb{t}") for t in range(NT4)]
        Mrkt = [sb([128, 4, 128], BF16, f"Mrk{t}") for t in range(NT4)]
        Pm = sb([128, 2, 64], BF16, "Pm")
        Qsb = sb([128, 2, 64], F32, "Qsb")
        Stz = [[sb([128, 2, 2, 64], BF16, f"Stz{hf}_{i}") for i in range(2)] for hf in range(2)]
        for hf in range(2):
            for i in range(2):
                memset("pool", Stz[hf][i][:], 0.0)
        RG = sb([128, 2, 128], BF16, "RG")
        kx_t, sg_t, aa_t, Cs_t, eNC_t, eCp_t, eCL_t, kkn_t, bb_t = [sb([128, TG], F32, f"pre{i}") for i in range(9)]
        YA = sb([128, 4, TG], BF16, "YA")
        YB = sb([128, 4, TG], BF16, "YB")
        YC = sb([128, 4, TG], BF16, "YC")
        Qr = sb([128, 4, TG], BF16, "Qr")
        Kr = sb([128, 2, 128 + TG], BF16, "Kr")
        NV = NT4 + 1
        Vs = [sb([128, 128], BF16, f"Vs{i}") for i in range(NV)]
        for v_ in Vs:
            memset("pool", v_[:], 0.0)
        memset("pool", Kr[:], 0.0)
        cosT = sb([128, TG], F32, "cosT")
        sinT = sb([128, TG], F32, "sinT")
        posi = sb([128, TG], I32, "posi")
        kfi = sb([128, TG], I32, "kfi")
        PEX = {(w_, p_): sb([128, 512], BF16, f"pex{w_}{p_}") for w_ in "cp" for p_ in range(2)}
        cidx = [0, 0]
        w_skd_v = w_skd.rearrange("(kt p) c -> p kt c", p=128)

        def proj(hT, wt_tile, wcols):
            ps = PS()
            for kt in range(8):
                mm(ps[:, 0:TG], wt_tile[:, kt, wcols], hT[:, kt, 1:TG + 1], start=(kt == 0), stop=(kt == 7))
            return ps[:, 0:TG]

        def silu2(ps):
            th = bT.get()
            act(th[:], ps, AF.Tanh, scale=0.5)
            sz = bT.get()
            stt(sz[:], th[:], 1.0, ps, OP.add, OP.mult)
            return sz

        def shift_mix(ps, U, mu_ap, out):
            cp("dve", U[:, 0:1], U[:, TG:TG + 1])
            cp("act", U[:, 1:TG + 1], ps)
            d = fT.get()
            tt("dve", d[:], U[:, 0:TG], U[:, 1:TG + 1], OP.subtract)
            stt(out, d[:], mu_ap, U[:, 1:TG + 1], OP.mult, OP.add)

        def XC(g):
            CUR[0] = "X"
            XOWN[0] = FILL_ALL or g >= G_OWN0
            TAG[0] = "XC" + ("o" if g >= G_OWN0 else "p")
            hT, hTp = hTb[g % 2], hTb[(g - 1) % 2]
            if g > 0:
                cp("dve", hT[:, :, 0:1], hTp[:, :, TG:TG + 1])
            for t4 in range(NT4):
                r0 = g * TG + t4 * 128
                norm_transpose(xw[r0:r0 + 128, :], hT, 1 + t4 * 128)
            if g < NCG:
                for o_, i_ in CONV[g * CPG:(g + 1) * CPG]:
                    S.conv_dma(o_, i_)
            ps = proj(hT, Wk, slice(512, 640))
            lmix = fT.get()
            shift_mix(ps, Ul, PCc("mu_l"), lmix[:])
            act(lt[0:64, :], lmix[0:64, :], AF.Tanh)
            cp("dve", lt[64:128, :], lmix[64:128, :])

        def XH(g, hf):
            CUR[0] = "X"
            XOWN[0] = FILL_ALL or g >= G_OWN0
            TAG[0] = "XH" + ("o" if g >= G_OWN0 else "p")
            own = g >= G_OWN0
            hT = hTb[g % 2]
            for t4 in range(NT4):
                ps = PS()
                vs_ = slice(256 * hf, 256 * hf + 256)
                for kt in range(8):
                    mm(ps[:, 0:256], hT[:, kt, 1 + t4 * 128:1 + (t4 + 1) * 128], Wv1[:, kt, vs_], start=(kt == 0), stop=False)
                for kt in range(8):
                    mm(ps[:, 0:256], hT[:, kt, t4 * 128:(t4 + 1) * 128], Wv2[:, kt, vs_], start=False, stop=(kt == 7))
                cp(evac_eng(), Vtok[hf][t4][:], ps[:, 0:256])
            for jl in range(2):
                j = 2 * hf + jl
                ps = proj(hT, Wk, slice(j * 128, (j + 1) * 128))
                kx = kx_t
                shift_mix(ps, Uk[j], PCc("mu_k", j), kx[:])
                if own or g == G_OWN0 - 1:
                    wt = wtile(w_in_v, j * 128)
                    ps = proj(hT, wt, slice(0, 128))
                    shift_mix(ps, Ur[j], PCc("mu_r", j), RX[hf][:, jl, :])
                psw = PS()
                mm(psw[:, 0:TG], w2a2[0:64, j * 128:(j + 1) * 128], lt[0:64, :])
                thw = sg_t
                act(thw[:], psw[:, 0:TG], AF.Tanh, bias=hw0[:, j:j + 1], scale=0.5)
                psa = PS()
                mm(psa[:, 0:TG], w2a2[64:128, j * 128:(j + 1) * 128], lt[64:128, :])
                tha = aa_t
                act(tha[:], psa[:, 0:TG], AF.Tanh, bias=ha0[:, j:j + 1], scale=0.5)
                Cs = Cs_t
                S.op("dve", lambda e, o=Cs[:], m=rmask, s_=thw[:]: e.tensor_tensor_scan(o, m, s_, 0.0, OP.mult, OP.add),
                     [rmask, thw[:]], [Cs[:]], cost=600.0)
                tt("dve", Cs[:], Cs[:], idx1, OP.add)
                Cp = fT.get()
                stt(Cp[:], thw[:], -1.0, Cs[:], OP.mult, OP.add)
                CL = fT.get()
                Cs3 = Cs[:].rearrange("p (c l) -> p c l", l=64)
                tt("dve", CL[:].rearrange("p (c l) -> p c l", l=64), Cs3[:, :, 63:64].broadcast_to([128, NCH, 64]),
                   Cs3, OP.subtract)
                HC0 = 0.5 * C0
                act(gam[hf][:, jl:jl + 1, :].rearrange("p o c -> p c o"), Cs3[:, :, 63:64], AF.Exp, scale=HC0)
                eNC, eCp, eCL = eNC_t, eCp_t, eCL_t
                act(eNC[:], Cs[:], AF.Exp, scale=-HC0)
                act(eCp[:], Cp[:], AF.Exp, scale=HC0, bias=-HC0)
                act(eCL[:], CL[:], AF.Exp, scale=HC0)
                sq = bT.get()
                act(sq[:], kx[:], AF.Square, scale=PCc("k_k", j))
                pss = PS()
                mm(pss[:, 0:TG], blk64, sq[:])
                mx = fT.get()
                ts("dve", mx[:], pss[:, 0:TG], 1e-24, None, OP.max)
                rn2 = fT.get()
                if POOL_RECIP:
                    S.op("pool", lambda e, o=rn2[:], a=mx[:], b=mone[:, 0:1].broadcast_to([128, TG]): e.tensor_tensor(o, a, b, OP.pow),
                         [mx[:], mone[:]], [rn2[:]], cost=7000.0)
                else:
                    S.op("dve", lambda e, o=rn2[:], i=mx[:]: e.reciprocal(o, i), [mx[:]], [rn2[:]], cost=70 + 8 * TG)
                kkr = kkn_t
                stt(kkr[:], kx[:], PCc("k_k", j), rn2[:], OP.mult, OP.mult)
                t1 = fT.get()
                ts("dve", t1[:], tha[:], hka[:, j:j + 1], nhka[:, j:j + 1], OP.mult, OP.add)
                kp = KP[hf][:, jl, :]
                stt(kp, t1[:], 1.0, kx[:], OP.add, OP.mult)
                bb = bb_t
                stt(bb[:], tha[:], 1.0, kx[:], OP.add, OP.mult)
                tt("dve", KtT[hf][:, jl, :], kp, eNC[:], OP.mult)
                stt(BtT[hf][:, jl, :], bb[:], PCc("k_k", j), eNC[:], OP.mult, OP.mult)
                stt(AtT[hf][:, jl, :], kkr[:], -0.5, eCp[:], OP.mult, OP.mult)
                tt("dve", KhT[hf][:, jl, :], kp, eCL[:], OP.mult)
                stt(BhT[hf][:, jl, :], bb[:], PCc("k_k", j), eCL[:], OP.mult, OP.mult)
                if own:
                    eC = fT.get()
                    act(eC[:], Cs[:], AF.Exp, scale=0.5 * C0)
                    tt("dve", RtT[hf][:, jl, :], RX[hf][:, jl, :], eC[:], OP.mult)
            for t4 in range(NT4):
                cs = slice(t4 * 128, (t4 + 1) * 128)
                ps = PS()
                psb = ps[:].bitcast(BF16)
                for jl in range(2):
                    tr(psb[:, jl * 128:(jl + 1) * 128], AtT[hf][:, jl, cs], ident)
                    tr(psb[:, 256 + jl * 128:256 + (jl + 1) * 128], BhT[hf][:, jl, cs], ident)
                    tr(psb[:, 512 + jl * 128:512 + (jl + 1) * 128], KhT[hf][:, jl, cs], ident)
                    if own:
                        tr(psb[:, 768 + jl * 128:768 + (jl + 1) * 128], Vtok[hf][t4][:, jl * 128:(jl + 1) * 128], ident)
                cp("act", TOKA[hf][t4][:], psb[:, 0:512])
                cp("act", TOKK[hf][t4][:], psb[:, 512:768])
                if own:
                    cp("act", VT[hf][:, :, cs], psb[:, 768:1024].rearrange("p (j t) -> p j t", j=2))

        def YH(g, hf):
            CUR[0] = "Y"
            INYE[0] = False
            YOWN[0] = g >= G_OWN0
            TAG[0] = "YH" + ("o" if g >= G_OWN0 else "p")
            own = g >= G_OWN0
            hT = hTb[g % 2]
            for t4 in range(NT4):
                CUR[0] = f"Y{t4}"
                N_b, Z_b, W_b, Mak, Mrb, Mrk = N_bt[t4], Z_bt[t4], W_bt[t4], Makt[t4], Mrbt[t4], Mrkt[t4]
                cs = slice(t4 * 128, (t4 + 1) * 128)

                def par_mm(lhs, rhs_):
                    banks = []
                    for par in range(2):
                        ps = PS()
                        pv = ps[:, 0:256].rearrange("p (j t) -> p j t", j=2)
                        pr = slice(par * 64, par * 64 + 64)
                        for jl in range(2):
                            mm(pv[:, jl, :], lhs[hf][pr, jl, cs], rhs_[hf][pr, jl, cs])
                        banks.append(pv)
                    return banks

                def evac_par(banks, dst, mask):
                    d4 = dst[:].rearrange("p (j par) t -> p par j t", par=2)
                    for par in range(2):
                        tt("dve", d4[:, par], banks[par], mask.rearrange("p (j t) -> p j t", j=2), OP.mult)

                N0, Z0, W0 = N_b[0], Z_b[0], W_b[0]
                evac_par(par_mm(BtT, AtT), N0, C("msu"))
                evac_par(par_mm(KtT, AtT), Mak, C("msu"))
                evac_par(par_mm(AtT, BtT), Z0, C("msl"))
                if own:
                    evac_par(par_mm(BtT, RtT), Mrb, C("miu"))
                    evac_par(par_mm(KtT, RtT), Mrk, C("miu"))
                if own:
                    cp("dve", W0[:, :, 0:64], TOKA[hf][t4][:, 0:256].rearrange("p (h k) -> p h k", h=4))
                ps = PS()
                pv = ps[:, 0:256].rearrange("p (h v) -> p h v", h=4)
                for hl in range(4):
                    mm(pv[:, hl, :], Mak[:, hl, :], Vtok[hf][t4][:, hl * 64:(hl + 1) * 64])
                cp("act", W0[:, :, 64:128], pv)
                for lv in range(6):
                    Nc, Zc, Wc = N_b[lv % 2], Z_b[lv % 2], W_b[lv % 2]
                    Nn, Zn, Wn = N_b[(lv + 1) % 2], Z_b[(lv + 1) % 2], W_b[(lv + 1) % 2]
                    ps = PS()
                    if own:
                        pv = ps[:].rearrange("p (h t) -> p h t", h=4)
                        if OWN_W_ACC:
                            for hl in range(4):
                                mm(pv[:, hl, :], ident, Wc[:, hl, :], start=True, stop=False)
                                mm(pv[:, hl, :], Nc[:, hl, :], Wc[:, hl, :], start=False, stop=True)
                            cp("act", Wn[:], pv)
                        else:
                            for hl in range(4):
                                mm(pv[:, hl, :], Nc[:, hl, :], Wc[:, hl, :])
                            tt("dve", Wn[:], pv, Wc[:], OP.add)
                    else:
                        pv = ps[:, 0:256].rearrange("p (h k) -> p h k", h=4)
                        Wc_ap = (TOKA[hf][t4][:, 256:512].rearrange("p (h k) -> p h k", h=4) if lv == 0
                                 else Wc[:, :, 0:64])
                        if PREFIX_W_ACC:
                            for hl in range(4):
                                mm(pv[:, hl, :], ident, Wc_ap[:, hl, :], start=True, stop=False)
                                mm(pv[:, hl, :], Zc[:, hl, :], Wc_ap[:, hl, :], start=False, stop=True)
                            cp("act", Wn[:, :, 0:64], pv)
                        else:
                            for hl in range(4):
                                mm(pv[:, hl, :], Zc[:, hl, :], Wc_ap[:, hl, :])
                            tt("dve", Wn[:, :, 0:64], pv, Wc_ap, OP.add)
                    if lv < 5:
                        ps = PS()
                        pv = ps[:].rearrange("p (h t) -> p h t", h=4)
                        for hl in range(4):
                            mm(pv[:, hl, :], Nc[:, hl, :], Zc[:, hl, :])
                        cp("act", Zn[:], pv)
                        ps = PS()
                        pv = ps[:].rearrange("p (h t) -> p h t", h=4)
                        for hl in range(4):
                            mm(pv[:, hl, :], Zc[:, hl, :], Nc[:, hl, :])
                        cp("act", Nn[:], pv)
            CUR[0] = "Y"
            for t4 in range(NT4):
                N_b, Z_b, W_b, Mak, Mrb, Mrk = N_bt[t4], Z_bt[t4], W_bt[t4], Makt[t4], Mrbt[t4], Mrkt[t4]
                cs = slice(t4 * 128, (t4 + 1) * 128)
                Wf = W_b[0]
                if own:
                    ps = PS()
                    pv = ps[:, 0:256].rearrange("p (j t) -> p j t", j=2)
                    for hl in range(4):
                        jl, par = hl // 2, hl % 2
                        mm(pv[par * 64:par * 64 + 64, jl, :], Wf[:, hl, 0:64], Mrb[:, hl, :])
                    tt("dve", RG[:], pv, RtT[hf][:, :, cs], OP.add)
                    pY = PSY_BANK[:, 0:256].rearrange("p (j t) -> p j t", j=2)
                for c in range(2):
                    cr = slice(c * 64, c * 64 + 64)
                    Sc, Sn = Stz[hf][cidx[hf] % 2], Stz[hf][(cidx[hf] + 1) % 2]
                    chunk_in_group = t4 * 2 + c
                    psP = PS()
                    pP = psP[:, 0:128].rearrange("p (j k) -> p j k", j=2)
                    psQ = PS()
                    pQ = psQ[:, 0:128].rearrange("p (j k) -> p j k", j=2)
                    for hl in range(4):
                        jl, par = hl // 2, hl % 2
                        pr = slice(par * 64, par * 64 + 64)
                        if own:
                            Bh_tok = TOKA[hf][t4][cr, 256 + hl * 64:256 + (hl + 1) * 64]
                            mm(pP[pr, jl, :], Wf[cr, hl, 0:64], Bh_tok)
                            mm(pQ[pr, jl, :], Bh_tok, Wf[cr, hl, 64:128], start=True, stop=False)
                        else:
                            mm(pP[pr, jl, :], TOKA[hf][t4][cr, hl * 64:(hl + 1) * 64], Wf[cr, hl, 0:64])
                            mm(pQ[pr, jl, :], Wf[cr, hl, 0:64], Wf[cr, hl, 64:128], start=True, stop=False)
                        mm(pQ[pr, jl, :], TOKK[hf][t4][cr, hl * 64:(hl + 1) * 64], Vtok[hf][t4][cr, hl * 64:(hl + 1) * 64],
                           start=False, stop=True)
                    for jl in range(2):
                        stt(Pm[:, jl, :], C("d0"), gam[hf][:, jl, chunk_in_group:chunk_in_group + 1], pP[:, jl, :],
                            OP.mult, OP.add)
                    cp("act", Qsb[:], pQ)
                    if own:
                        oc = slice(c * 64, c * 64 + 64)
                        for hl in range(4):
                            jl, par = hl // 2, hl % 2
                            pr = slice(par * 64, par * 64 + 64)
                            mm(pY[pr, jl, oc], Sc[:, par, jl, :], RG[:, jl, oc], start=True, stop=False)
                            mm(pY[pr, jl, oc], Wf[:, hl, 64:128], Mrb[:, hl, oc], start=False, stop=False)
                            mm(pY[pr, jl, oc], Vtok[hf][t4][:, hl * 64:(hl + 1) * 64], Mrk[:, hl, oc], start=False, stop=True)
                    psS = PS()
                    pS = psS[:, 0:128].rearrange("p (j k) -> p j k", j=2)
                    for hl in range(4):
                        jl, par = hl // 2, hl % 2
                        mm(pS[par * 64:par * 64 + 64, jl, :], Pm[:, jl, :], Sc[:, par, jl, :])
                    for par in range(2):
                        pr = slice(par * 64, par * 64 + 64)
                        tt("dve", Sn[pr, par], pS[pr], Qsb[pr], OP.add)
                    cidx[hf] += 1
                if own:
                    cp("act", Ybf[:, 2 * hf:2 * hf + 2, cs], pY)
                    act(Ysq[:, 2 * hf:2 * hf + 2, cs], pY, AF.Square)
            if not own:
                return
            for jl in range(2):
                j = 2 * hf + jl
                ps1 = PS()
                mm(ps1[:, 0:TG], blk64, Ybf[:, j, :])
                ps2 = PS()
                mm(ps2[:, 0:TG], blk64, Ysq[:, j, :])
                mean = fT.get()
                act(mean[:], ps1[:, 0:TG], AF.Copy, scale=1.0 / 64)
                msq = fT.get()
                tt("dve", msq[:], mean[:], mean[:], OP.mult)
                var = fT.get()
                stt(var[:], ps2[:, 0:TG], 1.0 / 64, msq[:], OP.mult, OP.subtract)
                ve = fT.get()
                ts("dve", ve[:], var[:], LNX_EPS, None, OP.add)
                rstd = fT.get()
                rsq(rstd[:], ve[:])
                yc = fT.get()
                tt("dve", yc[:], Ybf[:, j, :], mean[:], OP.subtract)
                yn = fT.get()
                tt("dve", yn[:], yc[:], rstd[:], OP.mult)
                yg = fT.get()
                ts("dve", yg[:], yn[:], PCc("lnx_g", j), PCc("lnx_b", j), OP.mult, OP.add)
                rk = bT.get()
                stt(rk[:], RX[hf][:, jl, :], PCc("r_k", j), KP[hf][:, jl, :], OP.mult, OP.mult)
                psb_ = PS()
                mm(psb_[:, 0:TG], blk64, rk[:])
                bv = fT.get()
                tt("dve", bv[:], psb_[:, 0:TG], VT[hf][:, jl, :], OP.mult)
                yb = fT.get()
                tt("dve", yb[:], bv[:], yg[:], OP.add)
                wt = wtile(w_in_v, 1664 + j * 128)
                ps = proj(hT, wt, slice(0, 128))
                sz = silu2(ps)
                stt(YA[:, j, :], yb[:], 0.5, sz[:], OP.mult, OP.mult)

        def YE(g):
            CUR[0] = "Y"
            INYE[0] = True
            YOWN[0] = g >= G_OWN0
            TAG[0] = "YE" + ("o" if g >= G_OWN0 else "p")
            if g < G_OWN0 - 1:
                return
            own = g >= G_OWN0
            og = g - G_OWN0
            hT = hTb[g % 2]
            if own:
                cp("dve", Kr[:, :, 0:128], Kr[:, :, TG:TG + 128])
                S.dma("sp", posi[:], posd[:, 128 + og * TG:128 + (og + 1) * TG].partition_broadcast(128))
            else:
                memset("pool", posi[:], 0)
                S.dma("sp", posi[:, TG - 128:TG], posd[:, 0:128].partition_broadcast(128))
            posf = fT.get()
            cp("dve", posf[:], posi[:])
            ang = fT.get()
            ts("dve", ang[:], posf[:], PCc("invf"), None, OP.mult)
            ts("dve", kfi[:], ang[:], 1.0 / (2 * PI), None, OP.mult)
            kff = fT.get()
            cp("dve", kff[:], kfi[:])
            rr = fT.get()
            stt(rr[:], kff[:], -2 * PI, ang[:], OP.mult, OP.add)
            wa = fT.get()
            ts("dve", wa[:], rr[:], -PI, 2 * PI, OP.is_lt, OP.mult)
            wb = fT.get()
            ts("dve", wb[:], rr[:], PI, -2 * PI, OP.is_gt, OP.mult)
            rw0 = fT.get()
            tt("dve", rw0[:], rr[:], wa[:], OP.add)
            rw = fT.get()
            tt("dve", rw[:], rw0[:], wb[:], OP.add)
            yc_ = fT.get()
            ts("dve", yc_[:], rw[:], PI / 2, None, OP.add)
            wc = fT.get()
            ts("dve", wc[:], yc_[:], PI, -2 * PI, OP.is_gt, OP.mult)
            rc = fT.get()
            tt("dve", rc[:], yc_[:], wc[:], OP.add)
            act(sinT[:], rw[:], AF.Sin)
            act(cosT[:], rc[:], AF.Sin)

            def head_norm_rope(ps, g_ap, dst, nparts_scale, do_rope=True):
                raw = fT.get()
                cp("act", raw[:], ps)
                sq = bT.get()
                act(sq[:], ps, AF.Square)
                pss = PS()
                mm(pss[:, 0:TG], blk64 if nparts_scale == 64 else ones, sq[:])
                ms = fT.get()
                ts("dve", ms[:], pss[:, 0:TG], 1.0 / nparts_scale, RMS_EPS, OP.mult, OP.add)
                rn = fT.get()
                rsq(rn[:], ms[:])
                if not do_rope:
                    stt(dst, raw[:], g_ap, rn[:], OP.mult, OP.mult)
                    return
                qn = bT.get()
                stt(qn[:], raw[:], g_ap, rn[:], OP.mult, OP.mult)
                psr = PS()
                mm(psr[:, 0:TG], C("rot"), qn[:])
                t1 = fT.get()
                tt("dve", t1[:], qn[:], cosT[:], OP.mult)
                t2 = fT.get()
                tt("dve", t2[:], psr[:, 0:TG], sinT[:], OP.mult)
                tt("dve", dst, t1[:], t2[:], OP.add)

            for kvh in range(2):
                wt = wtile(w_skd_v, kvh * 128)
                ps = proj(hT, wt, slice(0, 128))
                head_norm_rope(ps, PCc("kg"), Kr[:, kvh, 128:128 + TG], 64)
            wt = wtile(w_in_v, 2816)
            for t4 in range(NT4):
                if not own and t4 < NT4 - 1:
                    continue
                ps = PS()
                for kt in range(8):
                    mm(ps[:, 0:128], hT[:, kt, 1 + t4 * 128:1 + (t4 + 1) * 128], wt[:, kt, :], start=(kt == 0), stop=(kt == 7))
                vi = (0 if not own else 1 + og * NT4 + t4) % NV
                cp(evac_eng(), Vs[vi][:], ps[:, 0:128])
            if not own:
                return
            for j in range(4):
                wt = wtile(w_in_v, 2176 + j * 128)
                ps = proj(hT, wt, slice(0, 128))
                head_norm_rope(ps, PCc("qg"), Qr[:, j, :], 64)
            for t4 in range(NT4):
                blk = og * NT4 + t4
                qs = slice(t4 * 128, (t4 + 1) * 128)
                kc = slice(128 + t4 * 128, 128 + (t4 + 1) * 128)
                kp_ = slice(t4 * 128, (t4 + 1) * 128)
                Vc, Vp = Vs[(1 + blk) % NV], Vs[blk % NV]
                for par in range(2):
                    pr = slice(par * 64, par * 64 + 64)
                    for which, ksl, msk in (("c", kc, C("mcu")), ("p", kp_, C("mpl"))):
                        ps = PS()
                        pv = ps[:].rearrange("p (j t) -> p j t", j=4)
                        for j in range(4):
                            mm(pv[:, j, :], Kr[pr, j // 2, ksl], Qr[pr, j, qs])
                        et = eT.get()
                        act(et[:], ps[:], AF.Exp, scale=0.125)
                        dst = PEX[(which, par)]
                        if which == "p" and blk == 0:
                            stt(dst[:], et[:], PCc("fm"), msk, OP.mult, OP.mult)
                        else:
                            tt("dve", dst[:], et[:], msk, OP.mult)
                pso = PS()
                po = pso[:].rearrange("p (j t) -> p j t", j=4)
                psd = PS()
                pd = psd[:].rearrange("p (j t) -> p j t", j=4)
                for j in range(4):
                    kvh = j // 2
                    for par in range(2):
                        pr = slice(par * 64, par * 64 + 64)
                        pc_ = PEX[("c", par)][:, j * 128:(j + 1) * 128]
                        pp_ = PEX[("p", par)][:, j * 128:(j + 1) * 128]
                        mm(po[pr, j, :], Vc[:, kvh * 64:(kvh + 1) * 64], pc_, start=True, stop=False)
                        mm(po[pr, j, :], Vp[:, kvh * 64:(kvh + 1) * 64], pp_, start=False, stop=True)
                        mm(pd[pr, j, :], ones[:, 0:64], pc_, start=True, stop=False)
                        mm(pd[pr, j, :], ones[:, 0:64], pp_, start=False, stop=True)
                den = [fT.get(), fT.get()]
                for j in range(4):
                    ts("dve", den[j // 2][:, (j % 2) * 128:(j % 2 + 1) * 128], pd[:, j, :], esink[:, j:j + 1], None, OP.add)
                for hf2 in range(2):
                    rden = fT.get()
                    S.op("dve", lambda e, o=rden[:], i=den[hf2][:]: e.reciprocal(o, i), [den[hf2][:]], [rden[:]])
                    tt("dve", YB[:, 2 * hf2:2 * hf2 + 2, qs], po[:, 2 * hf2:2 * hf2 + 2, :],
                       rden[:].rearrange("p (j t) -> p j t", j=2), OP.mult)
            for j in range(4):
                wt = wtile(w_in_v, 2944 + j * 128)
                ps = proj(hT, wt, slice(0, 128))
                sz = silu2(ps)
                stt(YB[:, j, :], YB[:, j, :], 0.5, sz[:], OP.mult, OP.mult)
            for hd in range(4):
                wt = wtile(w_in_v, 3456 + hd * 128)
                ps = proj(hT, wt, slice(0, 128))
                qx = bT.get()
                head_norm_rope(ps, PCc("xqg"), qx[:], 128, do_rope=False)
                pes = []
                for mt in range(2):
                    pss = PS()
                    mm(pss[:, 0:TG], KmT[:, hd, mt * 128:(mt + 1) * 128], qx[:])
                    pe_ = bT.get()
                    act(pe_[:], pss[:, 0:TG], AF.Exp, scale=float(128 ** -0.5))
                    pes.append(pe_)
                pso = PS()
                psd = PS()
                for mt in range(2):
                    mm(pso[:, 0:TG], Vmem[:, mt, hd * 128:(hd + 1) * 128], pes[mt][:], start=(mt == 0), stop=(mt == 1))
                for mt in range(2):
                    mm(psd[:, 0:TG], ones, pes[mt][:], start=(mt == 0), stop=(mt == 1))
                rden = fT.get()
                S.op("dve", lambda e, o=rden[:], i=psd[:, 0:TG]: e.reciprocal(o, i), [psd[:, 0:TG]], [rden[:]])
                oc_ = fT.get()
                tt("dve", oc_[:], pso[:, 0:TG], rden[:], OP.mult)
                wt = wtile(w_in_v, 3968 + hd * 128)
                ps = proj(hT, wt, slice(0, 128))
                sz = silu2(ps)
                stt(YC[:, hd, :], oc_[:], 0.5, sz[:], OP.mult, OP.mult)
            for dt_ in range(8):
                acc = None
                for br, Yb in enumerate((YA, YB, YC)):
                    wt = wtile(w_in_v, 4480 + br * 1024 + dt_ * 128)
                    psg = proj(hT, wt, slice(0, 128))
                    gt = bT.get()
                    act(gt[:], psg, AF.Tanh, scale=0.5)
                    wpt = wtile(wp_v[br], dt_ * 128, nk=4)
                    psp = PS()
                    for kt in range(4):
                        mm(psp[:, 0:TG], wpt[:, kt, :], Yb[:, kt, :], start=(kt == 0), stop=(kt == 3))
                    tmp = fT.get()
                    stt(tmp[:], gt[:], 1.0, psp[:, 0:TG], OP.add, OP.mult)
                    if acc is None:
                        acc = tmp
                    elif br == 2:
                        tt("dve", MG[:, dt_, :], acc[:], tmp[:], OP.add)
                    else:
                        acc2 = fT.get()
                        tt("dve", acc2[:], acc[:], tmp[:], OP.add)
                        acc = acc2
            for t4 in range(NT4):
                xr = xin.get()
                r0 = og * TG + t4 * 128
                S.dma("sp", xr[:], xown[r0:r0 + 128, :])
                for ct in range(8):
                    wot = wtile(w_o_v, ct * 128)
                    ps = PS()
                    for kt in range(8):
                        mm(ps[:, 0:128], MG[:, kt, t4 * 128:(t4 + 1) * 128], wot[:, kt, :], start=(kt == 0), stop=(kt == 7))
                    stt(xr[:, ct * 128:(ct + 1) * 128], ps[:, 0:128], 0.5, xr[:, ct * 128:(ct + 1) * 128], OP.mult, OP.add)
                S.dma("sp", yout[r0:r0 + 128, :], xr[:], is_output=True)

        def rec(*fns):
            S.rec_start()
            for f in fns:
                f()
            return S.rec_stop()

        S.play(rec(lambda: XC(0), lambda: XH(0, 0)))
        for g in range(n_groups):
            S.fill_on = FILL_ALL or g > G_OWN0
            S.play(S.schedule(rec(lambda: YH(g, 0), lambda: XH(g, 1))))
            S.fill_on = FILL_ALL or g >= G_OWN0
            fns = [lambda: YH(g, 1), lambda: YE(g)]
            if g + 1 < n_groups:
                fns += [lambda: XC(g + 1), lambda: XH(g + 1, 0)]
            S.play(S.schedule(rec(*fns)))

        if dbg:
            print('SBUF bytes/partition', _sbytes[0], 'counts', S.cnt); print('tagcost us', {k: (v[0], round(v[1] / 1000)) for k, v in sorted(S.tagcost.items())}); print('model time us', max(S.etime.values()) / 1000, 'fillers', S.nfill); print('PE cols by tag', {k: (v[0], v[1], round(v[1] / 1.2 / 1000)) for k, v in PEC.items()}); print(sorted(_sblist, key=lambda t: -t[0]))
        S.finish()
        S.emit()
    return nc


def _host_inputs(inputs):
    f = lambda a: np.ascontiguousarray(np.asarray(a))
    x = f(inputs["x"])
    mem = f(inputs["mem"])
    pos = f(inputs["positions"])
    w_in = f(inputs["w_in"][0])
    cm, cf = _consts()
    sk = w_in[:, 2688:2816]
    w_skd = np.ascontiguousarray(np.concatenate([sk[:, 0:64], sk[:, 0:64], sk[:, 64:128], sk[:, 64:128]], axis=1))
    w2a2 = np.ascontiguousarray(np.concatenate([inputs["w2"][0], inputs["a2"][0]], axis=0)).astype(np.float32)

    def c4(v):
        return np.asarray(v, np.float32).reshape(4, 128).T

    def dup(v, n):
        return np.tile(np.asarray(v, np.float32).reshape(-1), n).reshape(128, 1)
    shared = dict(
        mem=None, w_in=w_in, w_skd=w_skd, w_mkv=f(inputs["w_mem_kv"][0]), w_pa=f(inputs["w_proj_a"][0]),
        w_pb=f(inputs["w_proj_b"][0]), w_pc=f(inputs["w_proj_c"][0]), w_o=f(inputs["w_out"][0]), w2a2=w2a2,
        g_row=f(inputs["norm_g"]).reshape(1, D), gm_row=f(inputs["mem_norm_g"]).reshape(1, D),
        muv_row=f(inputs["mu_rkv"][0, 2]).reshape(1, 512), cm=cm, cf=cf)
    invf = (np.float32(10000.0) ** (-(np.arange(32, dtype=np.float32) / np.float32(32)))).astype(np.float32)
    maps = []
    for c in range(NCORES):
        b, q = c // 4, c % 4
        end = OWN * (q + 1)
        xw = np.zeros((T, D), np.float32)
        xw[T - end:] = x[b, :end]
        pp = np.zeros((1, OWN + 128), np.int32)
        s0 = end - OWN - 128
        if s0 >= 0:
            pp[0] = pos[b, s0:end]
        else:
            pp[0, 128:] = pos[b, 0:end]
        pcv = np.zeros((128, NPC), np.float32)

        def put(name, a):
            o, n = PCN[name]
            pcv[:, o:o + n] = a
        put("mu_r", c4(inputs["mu_rkv"][0, 0]))
        put("mu_k", c4(inputs["mu_rkv"][0, 1]))
        put("mu_l", np.concatenate([inputs["mu_wa"][0, 0], inputs["mu_wa"][0, 1]]).reshape(128, 1))
        put("w0", c4(inputs["w0"][0]))
        put("a0", c4(inputs["a0"][0]))
        put("k_k", c4(inputs["k_k"][0]))
        put("k_a", c4(inputs["k_a"][0]))
        put("r_k", c4(np.asarray(inputs["r_k"][0]).reshape(-1)))
        put("lnx_g", c4(inputs["lnx_g"][0]))
        put("lnx_b", c4(inputs["lnx_b"][0]))
        put("qg", dup(inputs["q_norm_g"][0], 2))
        put("kg", dup(inputs["k_norm_g"][0], 2))
        put("sink", np.repeat(np.asarray(inputs["sinks"][0], np.float32).reshape(4, 2).T, 64, axis=0))
        put("xqg", np.asarray(inputs["xq_norm_g"][0], np.float32).reshape(128, 1))
        put("xkg", np.asarray(inputs["xk_norm_g"][0], np.float32).reshape(128, 1))
        put("invf", np.tile(invf, 4).reshape(128, 1))
        put("fm", np.full((128, 1), 0.0 if q == 0 else 1.0, np.float32))
        m = dict(shared)
        m.update(xw=xw, xown=np.ascontiguousarray(x[b, end - OWN:end]), pos=pp, mem=mem[b], pc=pcv)
        maps.append(m)
    return maps


_NC_CACHE = {}


def kernel(**inputs):
    inputs = {k: np.asarray(v) for k, v in inputs.items()}
    if "nc" not in _NC_CACHE:
        _NC_CACHE["nc"] = build_program()
    nc = _NC_CACHE["nc"]
    maps = _host_inputs(inputs)
    res = run_bass_kernel_spmd(nc, maps, core_ids=list(range(NCORES)))
    out = np.zeros((2, T, D), np.float32)
    for c in range(NCORES):
        b, q = c // 4, c % 4
        out[b, q * OWN:(q + 1) * OWN] = res.results[c]["y"]
    return out
```

```python
from contextlib import ExitStack
import numpy as np
import ml_dtypes
import concourse.bass as bass
import concourse.mybir as mybir
from concourse.bass_utils import run_bass_kernel_spmd

F32 = mybir.dt.float32
BF16 = mybir.dt.bfloat16
I32 = mybir.dt.int32
AF = mybir.ActivationFunctionType
OP = mybir.AluOpType

NCORES = 8
D = 1024
T = 8192
OWN = 2048
TG = 256
NT4 = TG // 128
NG = T // TG
G_OWN0 = NG - OWN // TG
RMS_EPS = 1e-6
LNX_EPS = 64e-5
C0 = -float(np.exp(-0.5))
PI = float(np.pi)
FILL_MIN, FILL_CAP, FILL_SCALE = 300.0, 24, 1.3
FILL_DENSITY = 1.3
SCHED_BUCKET = 150.0
FILL_ALL = False
Y_PREFIX_6 = True
YE_6 = True
ENG_LAT = 200.0
SEM_LAT = 300.0
PREFIX_W_ACC = True
OWN_W_ACC = False
POOL_RECIP = False
PE_COLD_GHZ = 1.2
PE_WARM_GHZ = 1.8

PCN = {}
def _pcdef():
    o = 0
    for name, n in [("mu_r", 4), ("mu_k", 4), ("mu_l", 1), ("w0", 4), ("a0", 4), ("k_k", 4), ("k_a", 4), ("r_k", 4),
                    ("lnx_g", 4), ("lnx_b", 4), ("qg", 1), ("kg", 1), ("sink", 4), ("xqg", 1), ("xkg", 1), ("invf", 1),
                    ("fm", 1)]:
        PCN[name] = (o, n)
        o += n
    return o
NPC = _pcdef()

CMN = {}
def _cmdef():
    o = 0
    for name, n in [("ident", 128), ("blk64", 128), ("ones", 128), ("msu", 256), ("miu", 256), ("msl", 256),
                    ("mcu", 512), ("mpl", 512), ("rot", 128), ("d0", 64)]:
        CMN[name] = (o, n)
        o += n
    return o
NCM = _cmdef()


def _consts():
    cm = np.zeros((128, NCM), np.float32)
    def put(name, a):
        o, n = CMN[name]
        assert a.shape == (128, n), (name, a.shape)
        cm[:, o:o + n] = a
    i = np.arange(128)
    r, c = i[:, None], i[None, :]
    same = (r // 64) == (c // 64)
    put("ident", (r == c).astype(np.float32))
    put("blk64", same.astype(np.float32))
    put("ones", np.ones((128, 128), np.float32))
    put("msu", np.tile(((r < c) & same).astype(np.float32), (1, 2)))
    put("miu", np.tile(((r <= c) & same).astype(np.float32), (1, 2)))
    put("msl", np.tile(((r > c) & same).astype(np.float32), (1, 2)))
    put("mcu", np.tile((c >= r).astype(np.float32), (1, 4)))
    put("mpl", np.tile((c < r).astype(np.float32), (1, 4)))
    rot = np.zeros((128, 128), np.float32)
    for cc in range(128):
        if cc % 64 < 32:
            rot[cc + 32, cc] = -1.0
        else:
            rot[cc - 32, cc] = 1.0
    put("rot", rot)
    put("d0", ((r % 64) == np.arange(64)[None, :]).astype(np.float32)[:, :64])
    cf = np.zeros((128, 2 * TG), np.float32)
    rm = np.ones(TG, np.float32)
    rm[::64] = 0.0
    cf[:, 0:TG] = rm[None, :]
    cf[:, TG:2 * TG] = ((np.arange(TG) % 64) + 1).astype(np.float32)[None, :]
    return cm.astype(ml_dtypes.bfloat16), cf


class Sched:
    def __init__(self, nc, stack):
        self.nc, self.stack = nc, stack
        self.names = ["pe", "act", "dve", "pool", "sp"]
        self.prog = {e: [] for e in self.names}
        self.cnt = {e: 0 for e in self.names}
        self.seg = 12000
        self.sems = {e: [] for e in self.names}
        self.seen = {e: {} for e in self.names}
        self.lastw, self.readers = {}, {}
        self.ND = 16
        self.dnext_pool = 0
        self.barrier_names = set()
        self.barrier_done = set()
        self.convsem = stack.enter_context(nc.semaphore("convsem"))
        self.nconv = 0
        self.dsems = [stack.enter_context(nc.semaphore(f"dsem{i}")) for i in range(self.ND)]
        self.dcnt = [0] * self.ND
        self.dlast = [None] * self.ND
        self.dnext = 0
        self.out_tokens = []
        self.rec = None
        self.filler = None
        self.fill_on = False
        self.cur_tset = "A"
        self.tagcost = {}
        self.curtag = ['?']
        self.nfill = 0
        self.fill_ns = 0.0
        self.etime = {e: 0.0 for e in self.names}

    def rec_start(self):
        self.rec = []

    def rec_stop(self):
        r, self.rec = self.rec, None
        return r

    def play(self, lst):
        for it in lst:
            if it[0] == "op":
                self.op(*it[1:5])
            elif it[0] == "conv":
                self._conv_dma(it[1], it[2])
            elif it[0] == "fill":
                for _ in range(it[1]):
                    self.prog["pe"].append(([], self.filler, None, 0))
                self.nfill += it[1]
            else:
                self.dma(*it[1:5])

    def schedule(self, lst):
        n = len(lst)
        lastw, readers = {}, {}
        preds = [set() for _ in range(n)]
        eng, cost, lat = [None] * n, [0.0] * n, [0.0] * n
        tset = [None] * n
        for i, it in enumerate(lst):
            if it[0] == "op":
                _, e, fn, ins, outs, c = it
                ins = [a for a in ins if a is not None and not isinstance(a, (int, float))]
                fs = 1
                for d_ in outs[0].shape[1:]:
                    fs *= int(d_)
                if c is None:
                    if e == "pe":
                        c = max(64, fs) / (PE_WARM_GHZ if self.fill_on else PE_COLD_GHZ) + 8
                    elif e == "act":
                        c = 200 + fs / 1.2
                    elif e == "dve":
                        c = 70 + fs * 1.04
                    else:
                        c = 600 + fs
                eng[i], cost[i], lat[i] = e, c, c + (200 if e == "pe" else ENG_LAT)
                tset[i] = getattr(fn, "_tset", None)
                tg_ = getattr(fn, "_tag", "?")
                acc_ = self.tagcost.setdefault((tg_, e), [0, 0.0])
                acc_[0] += 1
                acc_[1] += c
            elif it[0] == "conv":
                ins, outs = [], []
                eng[i], cost[i], lat[i] = "pool", 1000.0, 1000.0
            else:
                _, q, out, in_, _io = it
                ins, outs = [in_], [out]
                eng[i], cost[i], lat[i] = q, 80.0, 2500.0
            for a in ins:
                k = self.key(a)
                if k in lastw:
                    preds[i].add(lastw[k])
            for a in outs:
                k = self.key(a)
                if k in lastw:
                    preds[i].add(lastw[k])
                preds[i].update(readers.get(k, ()))
            for a in ins:
                readers.setdefault(self.key(a), set()).add(i)
            for a in outs:
                k = self.key(a)
                lastw[k] = i
                readers[k] = set()
            preds[i].discard(i)
        succs = [[] for _ in range(n)]
        npred = [len(p_) for p_ in preds]
        for i in range(n):
            for p_ in preds[i]:
                succs[p_].append(i)
        blev = [0.0] * n
        for i in range(n - 1, -1, -1):
            m_ = 0.0
            for s_ in succs[i]:
                if blev[s_] > m_:
                    m_ = blev[s_]
            blev[i] = lat[i] + m_
        etime = dict(self.etime)
        t0 = max(etime.values())
        for e in etime:
            etime[e] = max(etime[e], t0 - 2000.0)
        fin = [0.0] * n
        ready = [0.0] * n
        avail = [i for i in range(n) if npred[i] == 0]
        order = []
        while avail:
            best, bi = None, None
            for i in avail:
                st = max(etime[eng[i]], ready[i])
                if tset[i] is not None and self.cur_tset not in tset[i]:
                    st += 1300.0
                key_ = (st // SCHED_BUCKET, -blev[i], i) if SCHED_BUCKET else (st, i)
                if best is None or key_ < best:
                    best, bi = key_, i
            avail.remove(bi)
            st = max(etime[eng[bi]], ready[bi])
            if tset[bi] is not None and self.cur_tset not in tset[bi]:
                self.cur_tset = tset[bi][0]
            if eng[bi] == "pe" and self.filler is not None and self.fill_cap > 0 and self.fill_on:
                gap = (st - etime["pe"]) * self.fill_scale
                if gap > self.fill_min:
                    nf = min(self.fill_cap, int(gap * FILL_DENSITY / 512.0 + 0.5))
                    if nf > 0:
                        order.append(("fill", nf))
            etime[eng[bi]] = st + cost[bi]
            fin[bi] = st + lat[bi]
            order.append(bi)
            for s_ in succs[bi]:
                r = fin[bi] + (SEM_LAT if eng[s_] != eng[bi] else 0.0)
                if r > ready[s_]:
                    ready[s_] = r
                npred[s_] -= 1
                if npred[s_] == 0:
                    avail.append(s_)
        if getattr(self, "dbgwin", 0) > 0:
            self.dbgwin -= 1
            bus = {}
            for i in range(n):
                bus[eng[i]] = bus.get(eng[i], 0.0) + cost[i]
            print("window n=%d span=%.1fus busy=%s" % (n, (max(etime.values()) - t0) / 1000, {k: round(v / 1000, 1) for k, v in bus.items()}))
        self.etime = etime
        return [x if isinstance(x, tuple) else lst[x] for x in order]

    @staticmethod
    def merge(a, b):
        out, i, j = [], 0, 0
        na, nb = len(a), len(b)
        while i < na or j < nb:
            if j >= nb or (i < na and i * nb <= j * na):
                out.append(a[i]); i += 1
            else:
                out.append(b[j]); j += 1
        return out

    def _semfor(self, e, k):
        si = (k - 1) // self.seg
        while len(self.sems[e]) <= si:
            self.sems[e].append(self.stack.enter_context(self.nc.semaphore(f"s_{e}_{len(self.sems[e])}")))
        return self.sems[e][si], (k - 1) % self.seg + 1, si

    @staticmethod
    def key(ap):
        return ap.tensor.name

    def _waits(self, e, toks):
        best = {}
        for t in toks:
            if t is None:
                continue
            if t[0] == "c":
                _, f, k = t
                if f == e and e == "pe":
                    continue
                sem, val, si = self._semfor(f, k)
                kk = ("c", f)
                cur = best.get(kk)
                if cur is None or (si, val) > (cur[0], cur[1]):
                    best[kk] = (si, val, sem)
            else:
                _, i, val = t
                kk = ("d", i)
                cur = best.get(kk)
                if cur is None or val > cur[1]:
                    best[kk] = (0, val, self.dsems[i])
        wl = []
        for kk, (si, val, sem) in best.items():
            if self.seen[e].get(kk, (-1, 0)) >= (si, val):
                continue
            self.seen[e][kk] = (si, val)
            wl.append((sem, val))
        return wl

    def _deps(self, ins, outs):
        toks = []
        for a in ins:
            toks.append(self.lastw.get(self.key(a)))
        for a in outs:
            k = self.key(a)
            toks.append(self.lastw.get(k))
            toks.extend(self.readers.get(k, {}).values())
        return toks

    def _commit(self, tok, rid, ins, outs):
        for a in ins:
            self.readers.setdefault(self.key(a), {})[rid] = tok
        for a in outs:
            k = self.key(a)
            self.lastw[k] = tok
            self.readers[k] = {}

    def op(self, e, fn, ins, outs, cost=None):
        if self.rec is not None:
            try:
                fn._tag = self.curtag[0]
            except Exception:
                pass
            self.rec.append(("op", e, fn, ins, outs, cost))
            return
        ins = [a for a in ins if a is not None and not isinstance(a, (int, float))]
        wl = self._waits(e, self._deps(ins, outs))
        k = self.cnt[e] + 1
        self.cnt[e] = k
        sem, _, _ = self._semfor(e, k)
        self.prog[e].append((wl, fn, sem, 1))
        self._commit(("c", e, k), e, ins, outs)

    def dma(self, q, out, in_, is_output=False):
        if self.rec is not None:
            self.rec.append(("dma", q, out, in_, is_output))
            return
        half = self.ND // 2
        if q == "pool":
            i = half + self.dnext_pool
            self.dnext_pool = (self.dnext_pool + 1) % (self.ND - half)
        else:
            i = self.dnext
            self.dnext = (self.dnext + 1) % half
        toks = self._deps([in_], [out]) + [self.dlast[i]]
        wl = self._waits(q, toks)
        if self.key(in_) in self.barrier_names and q not in self.barrier_done:
            wl = wl + [(self.convsem, 16 * self.nconv)]
            self.barrier_done.add(q)
        self.dcnt[i] += 1
        tok = ("d", i, 16 * self.dcnt[i])
        self.dlast[i] = tok
        self.prog[q].append((wl, lambda eng, o=out, a=in_: eng.dma_start(out=o, in_=a), self.dsems[i], 16))
        self._commit(tok, ("d", i), [in_], [out])
        if is_output:
            self.out_tokens.append(tok)

    def conv_dma(self, out, in_):
        if self.rec is not None:
            self.rec.append(("conv", out, in_))
            return
        self._conv_dma(out, in_)

    def _conv_dma(self, out, in_):
        self.nconv += 1
        self.prog["pool"].append(([], lambda eng, o=out, a=in_: eng.dma_start(out=o, in_=a), self.convsem, 16))

    def finish(self):
        wl = self._waits("sp", self.out_tokens + [t for t in self.dlast if t is not None])
        self.prog["sp"].append((wl, None, None, 0))

    def emit(self):
        nc = self.nc
        engs = {"pe": "tensor", "act": "scalar", "dve": "vector", "pool": "gpsimd", "sp": "sync"}
        with nc.Block() as block:
            for e, bn in engs.items():
                prog = self.prog[e]

                def body(eng, prog=prog):
                    for wl, fn, sem, inc in prog:
                        for ws, wv in wl:
                            eng.wait_ge(ws, wv)
                        if fn is not None:
                            ins_ = fn(eng)
                            if sem is not None:
                                ins_.then_inc(sem, inc)
                getattr(block, bn)(body)


def build_program(n_groups=NG, g_own0=G_OWN0, dbg=False):
    G_OWN0 = g_own0
    nc = bass.Bass("TRN2", target_bir_lowering=False)
    st = ExitStack()
    with st:
        S = Sched(nc, st)

        def dram(name, shape, dt, kind="ExternalInput"):
            return nc.dram_tensor(name, list(shape), dt, kind=kind).ap()
        xw = dram("xw", [T, D], F32)
        xown = dram("xown", [OWN, D], F32)
        posd = dram("pos", [1, OWN + 128], I32)
        memd = dram("mem", [256, D], F32)
        w_in = dram("w_in", [D, 7552], F32)
        w_skd = dram("w_skd", [D, 256], F32)
        w_mkv = dram("w_mkv", [D, 1024], F32)
        w_pa = dram("w_pa", [512, D], F32)
        w_pb = dram("w_pb", [512, D], F32)
        w_pc = dram("w_pc", [512, D], F32)
        w_o = dram("w_o", [D, D], F32)
        w2a2d = dram("w2a2", [128, 512], F32)
        pcd = dram("pc", [128, NPC], F32)
        growd = dram("g_row", [1, D], F32)
        gmrowd = dram("gm_row", [1, D], F32)
        muvd = dram("muv_row", [1, 512], F32)
        cmd = dram("cm", [128, NCM], BF16)
        cfd = dram("cf", [128, 2 * TG], F32)
        yout = dram("y", [OWN, D], F32, kind="ExternalOutput")

        _n = [0]

        _sbytes = [0]
        _sblist = []

        def sb(shape, dt, name=None):
            _n[0] += 1
            _sbytes[0] += int(np.prod(shape[1:])) * (2 if dt == BF16 else 4)
            _sblist.append((int(np.prod(shape[1:])) * (2 if dt == BF16 else 4), name))
            return st.enter_context(nc.sbuf_tensor("sb_" + (name or f"t{_n[0]}"), list(shape), dt))

        class Ring:
            def __init__(self, n, shape, dt, name):
                self.t = [sb(shape, dt, f"{name}{i}") for i in range(n)]
                self.i = 0

            def get(self):
                t = self.t[self.i % len(self.t)]
                self.i += 1
                return t

        psum = [st.enter_context(nc.psum_tensor(f"ps{i}", [128, 512], F32)) for i in range(8)]
        CUR = ["Y"]
        PSB = {"X": [6, 7], "Y": [0, 1, 2, 3, 4], "Y0": [0, 1, 2], "Y1": [3, 4, 5]}
        PSI = {"X": 0, "Y": 0, "Y0": 0, "Y1": 0}
        PSY_BANK = psum[5]
        JUNK = psum[7]
        S.fill_min, S.fill_cap, S.fill_scale = FILL_MIN, FILL_CAP, FILL_SCALE

        XOWN = [False]
        YOWN = [False]
        INYE = [False]

        def PS():
            c = CUR[0]
            if c == "X" and XOWN[0] and FILL_CAP > 0:
                return psum[6]
            banks = PSB[c]
            if c == "Y" and (not YOWN[0] or (INYE[0] and YE_6)) and Y_PREFIX_6:
                banks = [0, 1, 2, 3, 4, 5]
            t = psum[banks[PSI[c] % len(banks)]]
            PSI[c] += 1
            return t

        class SRing:
            def __init__(self, nx, ny, shape, dt, name):
                self.r = {"X": Ring(nx, shape, dt, name + "x") if nx else None,
                          "Y": Ring(ny, shape, dt, name + "y") if ny else None}

            def get(self):
                return self.r[CUR[0][0]].get()

        PEC = {}
        TAG = S.curtag
        TAG[0] = "setup"

        def _pec(out):
            fs = 1
            for d_ in out.shape[1:]:
                fs *= int(d_)
            k = TAG[0]
            c = PEC.setdefault(k, [0, 0])
            c[0] += 1
            c[1] += max(64, fs)

        def mm(out, lhsT, rhs, start=True, stop=True):
            _pec(out)
            S.op("pe", lambda e: e.matmul(out, lhsT, rhs, start=start, stop=stop), [lhsT, rhs], [out])

        def tr(out, in_, ident):
            _pec(out)
            S.op("pe", lambda e: e.transpose(out, in_, ident), [in_, ident], [out])

        TSET = {AF.Exp: ("A", "B"), AF.Tanh: ("A",), AF.Ln: ("B",), AF.Sin: ("C",)}

        def act(out, in_, func, bias=None, scale=None, accum=None):
            kw = {}
            if bias is not None:
                kw["bias"] = bias
            if scale is not None:
                kw["scale"] = scale
            if accum is not None:
                kw["accum_out"] = accum
            outs = [out] + ([accum] if accum is not None else [])
            fn_ = lambda e: e.activation(out, in_, func, **kw)
            fn_._tset = TSET.get(func)
            S.op("act", fn_, [in_, bias, scale], outs)

        def tt(eng, out, a, b, op):
            S.op(eng, lambda e: e.tensor_tensor(out, a, b, op), [a, b], [out])

        def ts(eng, out, a, s1, s2, op0, op1=None):
            if op1 is None:
                S.op(eng, lambda e: e.tensor_scalar(out, a, s1, None, op0), [a, s1], [out])
            else:
                S.op(eng, lambda e: e.tensor_scalar(out, a, s1, s2, op0, op1), [a, s1, s2], [out])

        def stt(out, a, s, b, op0, op1):
            S.op("dve", lambda e: e.scalar_tensor_tensor(out, a, s, b, op0, op1), [a, s, b], [out])

        def cp(eng, out, in_):
            if eng == "act":
                act(out, in_, AF.Copy)
            else:
                S.op(eng, lambda e: e.tensor_copy(out, in_), [in_], [out])

        def memset(eng, out, val):
            S.op(eng, lambda e: e.memset(out, val), [], [out])

        _rr = [0]

        def evac_eng():
            _rr[0] += 1
            return "act" if _rr[0] % 2 else "dve"

        cm = sb([128, NCM], BF16, "cm")
        cf = sb([128, 2 * TG], F32, "cf")
        pc = sb([128, NPC], F32, "pc")
        grow = sb([128, D], F32, "grow")
        Ybf = sb([128, 4, TG], BF16, "Ybf")
        Ysq = sb([128, 4, TG], BF16, "Ysq")
        muv = Ybf[:].bitcast(F32).rearrange("p j t -> p (j t)")
        omuv = Ysq[:].bitcast(F32).rearrange("p j t -> p (j t)")
        S.dma("sp", cm[:], cmd)
        S.dma("sp", cf[:], cfd)
        S.dma("sp", pc[:], pcd)
        S.dma("sp", grow[:], gmrowd.partition_broadcast(128))
        S.dma("sp", muv, muvd.partition_broadcast(128))

        def C(name):
            o, n = CMN[name]
            return cm[:, o:o + n]

        def PCc(name, j=0):
            o, n = PCN[name]
            return pc[:, o + j:o + j + 1]
        ident = C("ident")
        S.filler = lambda e: e.matmul(JUNK[:, 0:512], ident, cm[:, 0:512], start=True, stop=True)
        blk64 = C("blk64")
        ones = C("ones")
        rmask = cf[:, 0:TG]
        idx1 = cf[:, TG:2 * TG]

        def rsq(out, in_):
            act(out, in_, AF.Ln)
            act(out, out, AF.Exp, scale=-0.5)

        ts("dve", omuv, muv, -1.0, 1.0, OP.mult, OP.add)
        esink = sb([128, 4], F32, "esink")
        o_s, _ = PCN["sink"]
        act(esink[:], pc[:, o_s:o_s + 4], AF.Exp)
        o_ka, _ = PCN["k_a"]
        hka = sb([128, 4], F32, "hka")
        nhka = sb([128, 4], F32, "nhka")
        ts("dve", hka[:], pc[:, o_ka:o_ka + 4], 0.5, None, OP.mult)
        ts("dve", nhka[:], pc[:, o_ka:o_ka + 4], -0.5, None, OP.mult)
        hw0 = sb([128, 4], F32, "hw0")
        ha0 = sb([128, 4], F32, "ha0")
        ts("dve", hw0[:], pc[:, PCN["w0"][0]:PCN["w0"][0] + 4], 0.5, None, OP.mult)
        ts("dve", ha0[:], pc[:, PCN["a0"][0]:PCN["a0"][0] + 4], 0.5, None, OP.mult)
        mhalf = sb([128, 1], F32, "mhalf")
        memset("pool", mhalf[:], -0.5)
        mone = sb([128, 1], F32, "mone")
        memset("pool", mone[:], -1.0)

        w_in_v = w_in.rearrange("(kt p) c -> p kt c", p=128)
        Wk = sb([128, 8, 640], BF16, "Wk")
        S.dma("pool", Wk[:, :, 0:512], w_in_v[:, :, 512:1024])
        S.dma("pool", Wk[:, :, 512:640], w_in_v[:, :, 1536:1664])
        Wv1 = sb([128, 8, 512], BF16, "Wv1")
        Wv2 = sb([128, 8, 512], BF16, "Wv2")
        xin = SRing(2, 1, [128, D], F32, "xin")
        for ct in range(4):
            stg = xin.get()
            stg3 = stg[:].rearrange("p (k c) -> p k c", k=8)
            S.dma("sp", stg3, w_in_v[:, :, 1024 + ct * 128:1024 + (ct + 1) * 128])
            for kt in range(8):
                tt("dve", Wv1[:, kt, ct * 128:(ct + 1) * 128], stg3[:, kt, :], omuv[:, ct * 128:(ct + 1) * 128], OP.mult)
                tt("dve", Wv2[:, kt, ct * 128:(ct + 1) * 128], stg3[:, kt, :], muv[:, ct * 128:(ct + 1) * 128], OP.mult)
        w2a2 = sb([128, 512], BF16, "w2a2")
        S.dma("pool", w2a2[:], w2a2d)
        wp_v = [w.rearrange("(kt p) c -> p kt c", p=128) for w in (w_pa, w_pb, w_pc)]
        w_o_v = w_o.rearrange("(kt p) c -> p kt c", p=128)
        wring = SRing(2, 6, [128, 8, 128], BF16, "wstream")
        BFV = {}
        CONV = []
        for nm_, src_, rows_, cols_ in (("wbf_in", w_in, D, 7552), ("wbf_skd", w_skd, D, 256), ("wbf_pa", w_pa, 512, D),
                                       ("wbf_pb", w_pb, 512, D), ("wbf_pc", w_pc, 512, D), ("wbf_o", w_o, D, D)):
            nkt, ntl = rows_ // 128, cols_ // 128
            scr = nc.dram_tensor(nm_, [ntl, 128, nkt, 128], BF16, kind="Internal").ap()
            S.barrier_names.add(nm_)
            TCH = 15
            for kt_ in range(nkt):
                for t0_ in range(0, ntl, TCH):
                    nt_ = min(TCH, ntl - t0_)
                    src_ap = src_[kt_ * 128:(kt_ + 1) * 128, t0_ * 128:(t0_ + nt_) * 128].rearrange("p (t c) -> p t c", c=128)
                    dst_ap = scr[t0_:t0_ + nt_, :, kt_, :].rearrange("t p c -> p t c")
                    CONV.append((dst_ap, src_ap))
            BFV[src_.tensor.name] = scr

        wpring = Ring(3, [128, 4, 128], BF16, "wpstream")

        def wtile(src_view, c0, nk=8):
            t_ = wpring.get() if nk == 4 else wring.get()
            bv = BFV.get(src_view.tensor.name)
            if bv is not None:
                S.dma("sp", t_[:, 0:nk, :], bv[c0 // 128])
            else:
                S.dma("pool", t_[:, 0:nk, :], src_view[:, :, c0:c0 + 128])
            return t_

        xnr = Ring(2, [128, D], BF16, "xn")
        col = Ring(8, [128, 1], F32, "col")
        fT = SRing(10, 12, [128, TG], F32, "fT")
        bT = SRing(4, 8, [128, TG], BF16, "bT")
        eT = Ring(2, [128, 512], BF16, "eT")

        def norm_transpose(src_dram_rows, dst, dcol0):
            xt = xin.get()
            S.dma("sp", xt[:], src_dram_rows)
            ss = col.get()
            xn = xnr.get()
            act(xn[:], xt[:], AF.Square, accum=ss[:])
            rs = col.get()
            ts("dve", rs[:], ss[:], 1.0 / D, RMS_EPS, OP.mult, OP.add)
            rstd = col.get()
            S.op("pool", lambda e, o=rstd[:], a=rs[:], b=mhalf[:]: e.tensor_tensor(o, a, b, OP.pow), [rs[:], mhalf[:]], [rstd[:]], cost=1500.0)
            stt(xn[:], xt[:], rstd[:], grow[:], OP.mult, OP.mult)
            ps = PS()
            psb = ps[:].bitcast(BF16).rearrange("p (k t) -> p k t", k=8)
            for kt in range(8):
                tr(psb[:, kt, :], xn[:, kt * 128:(kt + 1) * 128], ident)
            cp("act", dst[:, :, dcol0:dcol0 + 128], psb)

        MG = sb([128, 8, TG], BF16, "MG")
        memT = MG
        for mt in range(2):
            norm_transpose(memd[mt * 128:(mt + 1) * 128, :], memT, mt * 128)
        S.dma("sp", grow[:], growd.partition_broadcast(128))
        KmT = sb([128, 4, 256], BF16, "KmT")
        Vmem = sb([128, 2, 512], BF16, "Vmem")
        w_mkv_v = w_mkv.rearrange("(kt p) c -> p kt c", p=128)
        for hd in range(4):
            wt = wtile(w_mkv_v, hd * 128)
            ps = PS()
            for kt in range(8):
                mm(ps[:, 0:256], wt[:, kt, :], memT[:, kt, :], start=(kt == 0), stop=(kt == 7))
            sq = bT.get()
            act(sq[:, 0:256], ps[:, 0:256], AF.Square)
            ps2 = PS()
            mm(ps2[:, 0:256], ones, sq[:, 0:256])
            ms = fT.get()
            ts("dve", ms[:, 0:256], ps2[:, 0:256], 1.0 / 128, RMS_EPS, OP.mult, OP.add)
            rn = fT.get()
            rsq(rn[:, 0:256], ms[:, 0:256])
            stt(KmT[:, hd, :], ps[:, 0:256], PCc("xkg"), rn[:, 0:256], OP.mult, OP.mult)
        for ct in range(4):
            wt = wtile(w_mkv_v, 512 + ct * 128)
            for mt in range(2):
                ps = PS()
                for kt in range(8):
                    mm(ps[:, 0:128], memT[:, kt, mt * 128:(mt + 1) * 128], wt[:, kt, :], start=(kt == 0), stop=(kt == 7))
                cp(evac_eng(), Vmem[:, mt, ct * 128:(ct + 1) * 128], ps[:, 0:128])

        NCG = max(0, G_OWN0 - 1)
        CPG = (len(CONV) + NCG - 1) // NCG if NCG else 0
        if not NCG:
            for o_, i_ in CONV:
                S.conv_dma(o_, i_)

        NCH = TG // 64
        hTb = [sb([128, 8, TG + 1], BF16, f"hT{i}") for i in range(2)]
        memset("pool", hTb[0][:], 0.0)
        memset("pool", hTb[1][:], 0.0)
        Uk = [sb([128, TG + 1], F32, f"Uk{j}") for j in range(4)]
        Ul = sb([128, TG + 1], F32, "Ul")
        Ur = [sb([128, TG + 1], F32, f"Ur{j}") for j in range(4)]
        for u in Uk + [Ul] + Ur:
            memset("pool", u[:], 0.0)
        lt = sb([128, TG], BF16, "lt")
        def H2(name, dt=BF16):
            return [sb([128, 2, TG], dt, f"{name}{hf}") for hf in range(2)]
        AtT, BtT, KtT, RtT, KhT, BhT = H2("AtT"), H2("BtT"), H2("KtT"), H2("RtT"), H2("KhT"), H2("BhT")
        KP, RX, VT = H2("KP"), H2("RX"), H2("VT")
        gam = [sb([128, 2, NCH], F32, f"gam{hf}") for hf in range(2)]
        TOKA = [[sb([128, 512], BF16, f"TOKA{hf}_{i}") for i in range(NT4)] for hf in range(2)]
        TOKK = [[sb([128, 256], BF16, f"TOKK{hf}_{i}") for i in range(NT4)] for hf in range(2)]
        Vtok = [[sb([128, 256], BF16, f"Vtok{hf}_{i}") for i in range(NT4)] for hf in range(2)]
        N_bt = [[sb([128, 4, 128], BF16, f"Nb{t}_{i}") for i in range(2)] for t in range(NT4)]
        Z_bt = [[sb([128, 4, 128], BF16, f"Zb{t}_{i}") for i in range(2)] for t in range(NT4)]
        W_bt = [[sb([128, 4, 128], BF16, f"Wb{t}_{i}") for i in range(2)] for t in range(NT4)]
        Makt = [sb([128, 4, 128], BF16, f"Mak{t}") for t in range(NT4)]
        Mrbt = [sb([128, 4, 128], BF16, f"Mrb{t}") for t in range(NT4)]
        Mrkt = [sb([128, 4, 128], BF16, f"Mrk{t}") for t in range(NT4)]
        Pm = sb([128, 2, 64], BF16, "Pm")
        Qsb = sb([128, 2, 64], F32, "Qsb")
        Stz = [[sb([128, 2, 2, 64], BF16, f"Stz{hf}_{i}") for i in range(2)] for hf in range(2)]
        for hf in range(2):
            for i in range(2):
                memset("pool", Stz[hf][i][:], 0.0)
        RG = sb([128, 2, 128], BF16, "RG")
        kx_t, sg_t, aa_t, Cs_t, eNC_t, eCp_t, eCL_t, kkn_t, bb_t = [sb([128, TG], F32, f"pre{i}") for i in range(9)]
        YA = sb([128, 4, TG], BF16, "YA")
        YB = sb([128, 4, TG], BF16, "YB")
        YC = sb([128, 4, TG], BF16, "YC")
        Qr = sb([128, 4, TG], BF16, "Qr")
        Kr = sb([128, 2, 128 + TG], BF16, "Kr")
        NV = NT4 + 1
        Vs = [sb([128, 128], BF16, f"Vs{i}") for i in range(NV)]
        for v_ in Vs:
            memset("pool", v_[:], 0.0)
        memset("pool", Kr[:], 0.0)
        cosT = sb([128, TG], F32, "cosT")
        sinT = sb([128, TG], F32, "sinT")
        posi = sb([128, TG], I32, "posi")
        kfi = sb([128, TG], I32, "kfi")
        PEX = {(w_, p_): sb([128, 512], BF16, f"pex{w_}{p_}") for w_ in "cp" for p_ in range(2)}
        cidx = [0, 0]
        w_skd_v = w_skd.rearrange("(kt p) c -> p kt c", p=128)

        def proj(hT, wt_tile, wcols):
            ps = PS()
            for kt in range(8):
                mm(ps[:, 0:TG], wt_tile[:, kt, wcols], hT[:, kt, 1:TG + 1], start=(kt == 0), stop=(kt == 7))
            return ps[:, 0:TG]

        def silu2(ps):
            th = bT.get()
            act(th[:], ps, AF.Tanh, scale=0.5)
            sz = bT.get()
            stt(sz[:], th[:], 1.0, ps, OP.add, OP.mult)
            return sz

        def shift_mix(ps, U, mu_ap, out):
            cp("dve", U[:, 0:1], U[:, TG:TG + 1])
            cp("act", U[:, 1:TG + 1], ps)
            d = fT.get()
            tt("dve", d[:], U[:, 0:TG], U[:, 1:TG + 1], OP.subtract)
            stt(out, d[:], mu_ap, U[:, 1:TG + 1], OP.mult, OP.add)

        def XC(g):
            CUR[0] = "X"
            XOWN[0] = FILL_ALL or g >= G_OWN0
            TAG[0] = "XC" + ("o" if g >= G_OWN0 else "p")
            hT, hTp = hTb[g % 2], hTb[(g - 1) % 2]
            if g > 0:
                cp("dve", hT[:, :, 0:1], hTp[:, :, TG:TG + 1])
            for t4 in range(NT4):
                r0 = g * TG + t4 * 128
                norm_transpose(xw[r0:r0 + 128, :], hT, 1 + t4 * 128)
            if g < NCG:
                for o_, i_ in CONV[g * CPG:(g + 1) * CPG]:
                    S.conv_dma(o_, i_)
            ps = proj(hT, Wk, slice(512, 640))
            lmix = fT.get()
            shift_mix(ps, Ul, PCc("mu_l"), lmix[:])
            act(lt[0:64, :], lmix[0:64, :], AF.Tanh)
            cp("dve", lt[64:128, :], lmix[64:128, :])

        def XH(g, hf):
            CUR[0] = "X"
            XOWN[0] = FILL_ALL or g >= G_OWN0
            TAG[0] = "XH" + ("o" if g >= G_OWN0 else "p")
            own = g >= G_OWN0
            hT = hTb[g % 2]
            for t4 in range(NT4):
                ps = PS()
                vs_ = slice(256 * hf, 256 * hf + 256)
                for kt in range(8):
                    mm(ps[:, 0:256], hT[:, kt, 1 + t4 * 128:1 + (t4 + 1) * 128], Wv1[:, kt, vs_], start=(kt == 0), stop=False)
                for kt in range(8):
                    mm(ps[:, 0:256], hT[:, kt, t4 * 128:(t4 + 1) * 128], Wv2[:, kt, vs_], start=False, stop=(kt == 7))
                cp(evac_eng(), Vtok[hf][t4][:], ps[:, 0:256])
            for jl in range(2):
                j = 2 * hf + jl
                ps = proj(hT, Wk, slice(j * 128, (j + 1) * 128))
                kx = kx_t
                shift_mix(ps, Uk[j], PCc("mu_k", j), kx[:])
                if own or g == G_OWN0 - 1:
                    wt = wtile(w_in_v, j * 128)
                    ps = proj(hT, wt, slice(0, 128))
                    shift_mix(ps, Ur[j], PCc("mu_r", j), RX[hf][:, jl, :])
                psw = PS()
                mm(psw[:, 0:TG], w2a2[0:64, j * 128:(j + 1) * 128], lt[0:64, :])
                thw = sg_t
                act(thw[:], psw[:, 0:TG], AF.Tanh, bias=hw0[:, j:j + 1], scale=0.5)
                psa = PS()
                mm(psa[:, 0:TG], w2a2[64:128, j * 128:(j + 1) * 128], lt[64:128, :])
                tha = aa_t
                act(tha[:], psa[:, 0:TG], AF.Tanh, bias=ha0[:, j:j + 1], scale=0.5)
                Cs = Cs_t
                S.op("dve", lambda e, o=Cs[:], m=rmask, s_=thw[:]: e.tensor_tensor_scan(o, m, s_, 0.0, OP.mult, OP.add),
                     [rmask, thw[:]], [Cs[:]], cost=600.0)
                tt("dve", Cs[:], Cs[:], idx1, OP.add)
                Cp = fT.get()
                stt(Cp[:], thw[:], -1.0, Cs[:], OP.mult, OP.add)
                CL = fT.get()
                Cs3 = Cs[:].rearrange("p (c l) -> p c l", l=64)
                tt("dve", CL[:].rearrange("p (c l) -> p c l", l=64), Cs3[:, :, 63:64].broadcast_to([128, NCH, 64]),
                   Cs3, OP.subtract)
                HC0 = 0.5 * C0
                act(gam[hf][:, jl:jl + 1, :].rearrange("p o c -> p c o"), Cs3[:, :, 63:64], AF.Exp, scale=HC0)
                eNC, eCp, eCL = eNC_t, eCp_t, eCL_t
                act(eNC[:], Cs[:], AF.Exp, scale=-HC0)
                act(eCp[:], Cp[:], AF.Exp, scale=HC0, bias=-HC0)
                act(eCL[:], CL[:], AF.Exp, scale=HC0)
                sq = bT.get()
                act(sq[:], kx[:], AF.Square, scale=PCc("k_k", j))
                pss = PS()
                mm(pss[:, 0:TG], blk64, sq[:])
                mx = fT.get()
                ts("dve", mx[:], pss[:, 0:TG], 1e-24, None, OP.max)
                rn2 = fT.get()
                if POOL_RECIP:
                    S.op("pool", lambda e, o=rn2[:], a=mx[:], b=mone[:, 0:1].broadcast_to([128, TG]): e.tensor_tensor(o, a, b, OP.pow),
                         [mx[:], mone[:]], [rn2[:]], cost=7000.0)
                else:
                    S.op("dve", lambda e, o=rn2[:], i=mx[:]: e.reciprocal(o, i), [mx[:]], [rn2[:]], cost=70 + 8 * TG)
                kkr = kkn_t
                stt(kkr[:], kx[:], PCc("k_k", j), rn2[:], OP.mult, OP.mult)
                t1 = fT.get()
                ts("dve", t1[:], tha[:], hka[:, j:j + 1], nhka[:, j:j + 1], OP.mult, OP.add)
                kp = KP[hf][:, jl, :]
                stt(kp, t1[:], 1.0, kx[:], OP.add, OP.mult)
                bb = bb_t
                stt(bb[:], tha[:], 1.0, kx[:], OP.add, OP.mult)
                tt("dve", KtT[hf][:, jl, :], kp, eNC[:], OP.mult)
                stt(BtT[hf][:, jl, :], bb[:], PCc("k_k", j), eNC[:], OP.mult, OP.mult)
                stt(AtT[hf][:, jl, :], kkr[:], -0.5, eCp[:], OP.mult, OP.mult)
                tt("dve", KhT[hf][:, jl, :], kp, eCL[:], OP.mult)
                stt(BhT[hf][:, jl, :], bb[:], PCc("k_k", j), eCL[:], OP.mult, OP.mult)
                if own:
                    eC = fT.get()
                    act(eC[:], Cs[:], AF.Exp, scale=0.5 * C0)
                    tt("dve", RtT[hf][:, jl, :], RX[hf][:, jl, :], eC[:], OP.mult)
            for t4 in range(NT4):
                cs = slice(t4 * 128, (t4 + 1) * 128)
                ps = PS()
                psb = ps[:].bitcast(BF16)
                for jl in range(2):
                    tr(psb[:, jl * 128:(jl + 1) * 128], AtT[hf][:, jl, cs], ident)
                    tr(psb[:, 256 + jl * 128:256 + (jl + 1) * 128], BhT[hf][:, jl, cs], ident)
                    tr(psb[:, 512 + jl * 128:512 + (jl + 1) * 128], KhT[hf][:, jl, cs], ident)
                    if own:
                        tr(psb[:, 768 + jl * 128:768 + (jl + 1) * 128], Vtok[hf][t4][:, jl * 128:(jl + 1) * 128], ident)
                cp("act", TOKA[hf][t4][:], psb[:, 0:512])
                cp("act", TOKK[hf][t4][:], psb[:, 512:768])
                if own:
                    cp("act", VT[hf][:, :, cs], psb[:, 768:1024].rearrange("p (j t) -> p j t", j=2))

        def YH(g, hf):
            CUR[0] = "Y"
            INYE[0] = False
            YOWN[0] = g >= G_OWN0
            TAG[0] = "YH" + ("o" if g >= G_OWN0 else "p")
            own = g >= G_OWN0
            hT = hTb[g % 2]
            for t4 in range(NT4):
                CUR[0] = f"Y{t4}"
                N_b, Z_b, W_b, Mak, Mrb, Mrk = N_bt[t4], Z_bt[t4], W_bt[t4], Makt[t4], Mrbt[t4], Mrkt[t4]
                cs = slice(t4 * 128, (t4 + 1) * 128)

                def par_mm(lhs, rhs_):
                    banks = []
                    for par in range(2):
                        ps = PS()
                        pv = ps[:, 0:256].rearrange("p (j t) -> p j t", j=2)
                        pr = slice(par * 64, par * 64 + 64)
                        for jl in range(2):
                            mm(pv[:, jl, :], lhs[hf][pr, jl, cs], rhs_[hf][pr, jl, cs])
                        banks.append(pv)
                    return banks

                def evac_par(banks, dst, mask):
                    d4 = dst[:].rearrange("p (j par) t -> p par j t", par=2)
                    for par in range(2):
                        tt("dve", d4[:, par], banks[par], mask.rearrange("p (j t) -> p j t", j=2), OP.mult)

                N0, Z0, W0 = N_b[0], Z_b[0], W_b[0]
                evac_par(par_mm(BtT, AtT), N0, C("msu"))
                evac_par(par_mm(KtT, AtT), Mak, C("msu"))
                evac_par(par_mm(AtT, BtT), Z0, C("msl"))
                if own:
                    evac_par(par_mm(BtT, RtT), Mrb, C("miu"))
                    evac_par(par_mm(KtT, RtT), Mrk, C("miu"))
                if own:
                    cp("dve", W0[:, :, 0:64], TOKA[hf][t4][:, 0:256].rearrange("p (h k) -> p h k", h=4))
                ps = PS()
                pv = ps[:, 0:256].rearrange("p (h v) -> p h v", h=4)
                for hl in range(4):
                    mm(pv[:, hl, :], Mak[:, hl, :], Vtok[hf][t4][:, hl * 64:(hl + 1) * 64])
                cp("act", W0[:, :, 64:128], pv)
                for lv in range(6):
                    Nc, Zc, Wc = N_b[lv % 2], Z_b[lv % 2], W_b[lv % 2]
                    Nn, Zn, Wn = N_b[(lv + 1) % 2], Z_b[(lv + 1) % 2], W_b[(lv + 1) % 2]
                    ps = PS()
                    if own:
                        pv = ps[:].rearrange("p (h t) -> p h t", h=4)
                        if OWN_W_ACC:
                            for hl in range(4):
                                mm(pv[:, hl, :], ident, Wc[:, hl, :], start=True, stop=False)
                                mm(pv[:, hl, :], Nc[:, hl, :], Wc[:, hl, :], start=False, stop=True)
                            cp("act", Wn[:], pv)
                        else:
                            for hl in range(4):
                                mm(pv[:, hl, :], Nc[:, hl, :], Wc[:, hl, :])
                            tt("dve", Wn[:], pv, Wc[:], OP.add)
                    else:
                        pv = ps[:, 0:256].rearrange("p (h k) -> p h k", h=4)
                        Wc_ap = (TOKA[hf][t4][:, 256:512].rearrange("p (h k) -> p h k", h=4) if lv == 0
                                 else Wc[:, :, 0:64])
                        if PREFIX_W_ACC:
                            for hl in range(4):
                                mm(pv[:, hl, :], ident, Wc_ap[:, hl, :], start=True, stop=False)
                                mm(pv[:, hl, :], Zc[:, hl, :], Wc_ap[:, hl, :], start=False, stop=True)
                            cp("act", Wn[:, :, 0:64], pv)
                        else:
                            for hl in range(4):
                                mm(pv[:, hl, :], Zc[:, hl, :], Wc_ap[:, hl, :])
                            tt("dve", Wn[:, :, 0:64], pv, Wc_ap, OP.add)
                    if lv < 5:
                        ps = PS()
                        pv = ps[:].rearrange("p (h t) -> p h t", h=4)
                        for hl in range(4):
                            mm(pv[:, hl, :], Nc[:, hl, :], Zc[:, hl, :])
                        cp("act", Zn[:], pv)
                        ps = PS()
                        pv = ps[:].rearrange("p (h t) -> p h t", h=4)
                        for hl in range(4):
                            mm(pv[:, hl, :], Zc[:, hl, :], Nc[:, hl, :])
                        cp("act", Nn[:], pv)
            CUR[0] = "Y"
            for t4 in range(NT4):
                N_b, Z_b, W_b, Mak, Mrb, Mrk = N_bt[t4], Z_bt[t4], W_bt[t4], Makt[t4], Mrbt[t4], Mrkt[t4]
                cs = slice(t4 * 128, (t4 + 1) * 128)
                Wf = W_b[0]
                if own:
                    ps = PS()
                    pv = ps[:, 0:256].rearrange("p (j t) -> p j t", j=2)
                    for hl in range(4):
                        jl, par = hl // 2, hl % 2
                        mm(pv[par * 64:par * 64 + 64, jl, :], Wf[:, hl, 0:64], Mrb[:, hl, :])
                    tt("dve", RG[:], pv, RtT[hf][:, :, cs], OP.add)
                    pY = PSY_BANK[:, 0:256].rearrange("p (j t) -> p j t", j=2)
                for c in range(2):
                    cr = slice(c * 64, c * 64 + 64)
                    Sc, Sn = Stz[hf][cidx[hf] % 2], Stz[hf][(cidx[hf] + 1) % 2]
                    chunk_in_group = t4 * 2 + c
                    psP = PS()
                    pP = psP[:, 0:128].rearrange("p (j k) -> p j k", j=2)
                    psQ = PS()
                    pQ = psQ[:, 0:128].rearrange("p (j k) -> p j k", j=2)
                    for hl in range(4):
                        jl, par = hl // 2, hl % 2
                        pr = slice(par * 64, par * 64 + 64)
                        if own:
                            Bh_tok = TOKA[hf][t4][cr, 256 + hl * 64:256 + (hl + 1) * 64]
                            mm(pP[pr, jl, :], Wf[cr, hl, 0:64], Bh_tok)
                            mm(pQ[pr, jl, :], Bh_tok, Wf[cr, hl, 64:128], start=True, stop=False)
                        else:
                            mm(pP[pr, jl, :], TOKA[hf][t4][cr, hl * 64:(hl + 1) * 64], Wf[cr, hl, 0:64])
                            mm(pQ[pr, jl, :], Wf[cr, hl, 0:64], Wf[cr, hl, 64:128], start=True, stop=False)
                        mm(pQ[pr, jl, :], TOKK[hf][t4][cr, hl * 64:(hl + 1) * 64], Vtok[hf][t4][cr, hl * 64:(hl + 1) * 64],
                           start=False, stop=True)
                    for jl in range(2):
                        stt(Pm[:, jl, :], C("d0"), gam[hf][:, jl, chunk_in_group:chunk_in_group + 1], pP[:, jl, :],
                            OP.mult, OP.add)
                    cp("act", Qsb[:], pQ)
                    if own:
                        oc = slice(c * 64, c * 64 + 64)
                        for hl in range(4):
                            jl, par = hl // 2, hl % 2
                            pr = slice(par * 64, par * 64 + 64)
                            mm(pY[pr, jl, oc], Sc[:, par, jl, :], RG[:, jl, oc], start=True, stop=False)
                            mm(pY[pr, jl, oc], Wf[:, hl, 64:128], Mrb[:, hl, oc], start=False, stop=False)
                            mm(pY[pr, jl, oc], Vtok[hf][t4][:, hl * 64:(hl + 1) * 64], Mrk[:, hl, oc], start=False, stop=True)
                    psS = PS()
                    pS = psS[:, 0:128].rearrange("p (j k) -> p j k", j=2)
                    for hl in range(4):
                        jl, par = hl // 2, hl % 2
                        mm(pS[par * 64:par * 64 + 64, jl, :], Pm[:, jl, :], Sc[:, par, jl, :])
                    for par in range(2):
                        pr = slice(par * 64, par * 64 + 64)
                        tt("dve", Sn[pr, par], pS[pr], Qsb[pr], OP.add)
                    cidx[hf] += 1
                if own:
                    cp("act", Ybf[:, 2 * hf:2 * hf + 2, cs], pY)
                    act(Ysq[:, 2 * hf:2 * hf + 2, cs], pY, AF.Square)
            if not own:
                return
            for jl in range(2):
                j = 2 * hf + jl
                ps1 = PS()
                mm(ps1[:, 0:TG], blk64, Ybf[:, j, :])
                ps2 = PS()
                mm(ps2[:, 0:TG], blk64, Ysq[:, j, :])
                mean = fT.get()
                act(mean[:], ps1[:, 0:TG], AF.Copy, scale=1.0 / 64)
                msq = fT.get()
                tt("dve", msq[:], mean[:], mean[:], OP.mult)
                var = fT.get()
                stt(var[:], ps2[:, 0:TG], 1.0 / 64, msq[:], OP.mult, OP.subtract)
                ve = fT.get()
                ts("dve", ve[:], var[:], LNX_EPS, None, OP.add)
                rstd = fT.get()
                rsq(rstd[:], ve[:])
                yc = fT.get()
                tt("dve", yc[:], Ybf[:, j, :], mean[:], OP.subtract)
                yn = fT.get()
                tt("dve", yn[:], yc[:], rstd[:], OP.mult)
                yg = fT.get()
                ts("dve", yg[:], yn[:], PCc("lnx_g", j), PCc("lnx_b", j), OP.mult, OP.add)
                rk = bT.get()
                stt(rk[:], RX[hf][:, jl, :], PCc("r_k", j), KP[hf][:, jl, :], OP.mult, OP.mult)
                psb_ = PS()
                mm(psb_[:, 0:TG], blk64, rk[:])
                bv = fT.get()
                tt("dve", bv[:], psb_[:, 0:TG], VT[hf][:, jl, :], OP.mult)
                yb = fT.get()
                tt("dve", yb[:], bv[:], yg[:], OP.add)
                wt = wtile(w_in_v, 1664 + j * 128)
                ps = proj(hT, wt, slice(0, 128))
                sz = silu2(ps)
                stt(YA[:, j, :], yb[:], 0.5, sz[:], OP.mult, OP.mult)

        def YE(g):
            CUR[0] = "Y"
            INYE[0] = True
            YOWN[0] = g >= G_OWN0
            TAG[0] = "YE" + ("o" if g >= G_OWN0 else "p")
            if g < G_OWN0 - 1:
                return
            own = g >= G_OWN0
            og = g - G_OWN0
            hT = hTb[g % 2]
            if own:
                cp("dve", Kr[:, :, 0:128], Kr[:, :, TG:TG + 128])
                S.dma("sp", posi[:], posd[:, 128 + og * TG:128 + (og + 1) * TG].partition_broadcast(128))
            else:
                memset("pool", posi[:], 0)
                S.dma("sp", posi[:, TG - 128:TG], posd[:, 0:128].partition_broadcast(128))
            posf = fT.get()
            cp("dve", posf[:], posi[:])
            ang = fT.get()
            ts("dve", ang[:], posf[:], PCc("invf"), None, OP.mult)
            ts("dve", kfi[:], ang[:], 1.0 / (2 * PI), None, OP.mult)
            kff = fT.get()
            cp("dve", kff[:], kfi[:])
            rr = fT.get()
            stt(rr[:], kff[:], -2 * PI, ang[:], OP.mult, OP.add)
            wa = fT.get()
            ts("dve", wa[:], rr[:], -PI, 2 * PI, OP.is_lt, OP.mult)
            wb = fT.get()
            ts("dve", wb[:], rr[:], PI, -2 * PI, OP.is_gt, OP.mult)
            rw0 = fT.get()
            tt("dve", rw0[:], rr[:], wa[:], OP.add)
            rw = fT.get()
            tt("dve", rw[:], rw0[:], wb[:], OP.add)
            yc_ = fT.get()
            ts("dve", yc_[:], rw[:], PI / 2, None, OP.add)
            wc = fT.get()
            ts("dve", wc[:], yc_[:], PI, -2 * PI, OP.is_gt, OP.mult)
            rc = fT.get()
            tt("dve", rc[:], yc_[:], wc[:], OP.add)
            act(sinT[:], rw[:], AF.Sin)
            act(cosT[:], rc[:], AF.Sin)

            def head_norm_rope(ps, g_ap, dst, nparts_scale, do_rope=True):
                raw = fT.get()
                cp("act", raw[:], ps)
                sq = bT.get()
                act(sq[:], ps, AF.Square)
                pss = PS()
                mm(pss[:, 0:TG], blk64 if nparts_scale == 64 else ones, sq[:])
                ms = fT.get()
                ts("dve", ms[:], pss[:, 0:TG], 1.0 / nparts_scale, RMS_EPS, OP.mult, OP.add)
                rn = fT.get()
                rsq(rn[:], ms[:])
                if not do_rope:
                    stt(dst, raw[:], g_ap, rn[:], OP.mult, OP.mult)
                    return
                qn = bT.get()
                stt(qn[:], raw[:], g_ap, rn[:], OP.mult, OP.mult)
                psr = PS()
                mm(psr[:, 0:TG], C("rot"), qn[:])
                t1 = fT.get()
                tt("dve", t1[:], qn[:], cosT[:], OP.mult)
                t2 = fT.get()
                tt("dve", t2[:], psr[:, 0:TG], sinT[:], OP.mult)
                tt("dve", dst, t1[:], t2[:], OP.add)

            for kvh in range(2):
                wt = wtile(w_skd_v, kvh * 128)
                ps = proj(hT, wt, slice(0, 128))
                head_norm_rope(ps, PCc("kg"), Kr[:, kvh, 128:128 + TG], 64)
            wt = wtile(w_in_v, 2816)
            for t4 in range(NT4):
                if not own and t4 < NT4 - 1:
                    continue
                ps = PS()
                for kt in range(8):
                    mm(ps[:, 0:128], hT[:, kt, 1 + t4 * 128:1 + (t4 + 1) * 128], wt[:, kt, :], start=(kt == 0), stop=(kt == 7))
                vi = (0 if not own else 1 + og * NT4 + t4) % NV
                cp(evac_eng(), Vs[vi][:], ps[:, 0:128])
            if not own:
                return
            for j in range(4):
                wt = wtile(w_in_v, 2176 + j * 128)
                ps = proj(hT, wt, slice(0, 128))
                head_norm_rope(ps, PCc("qg"), Qr[:, j, :], 64)
            for t4 in range(NT4):
                blk = og * NT4 + t4
                qs = slice(t4 * 128, (t4 + 1) * 128)
                kc = slice(128 + t4 * 128, 128 + (t4 + 1) * 128)
                kp_ = slice(t4 * 128, (t4 + 1) * 128)
                Vc, Vp = Vs[(1 + blk) % NV], Vs[blk % NV]
                for par in range(2):
                    pr = slice(par * 64, par * 64 + 64)
                    for which, ksl, msk in (("c", kc, C("mcu")), ("p", kp_, C("mpl"))):
                        ps = PS()
                        pv = ps[:].rearrange("p (j t) -> p j t", j=4)
                        for j in range(4):
                            mm(pv[:, j, :], Kr[pr, j // 2, ksl], Qr[pr, j, qs])
                        et = eT.get()
                        act(et[:], ps[:], AF.Exp, scale=0.125)
                        dst = PEX[(which, par)]
                        if which == "p" and blk == 0:
                            stt(dst[:], et[:], PCc("fm"), msk, OP.mult, OP.mult)
                        else:
                            tt("dve", dst[:], et[:], msk, OP.mult)
                pso = PS()
                po = pso[:].rearrange("p (j t) -> p j t", j=4)
                psd = PS()
                pd = psd[:].rearrange("p (j t) -> p j t", j=4)
                for j in range(4):
                    kvh = j // 2
                    for par in range(2):
                        pr = slice(par * 64, par * 64 + 64)
                        pc_ = PEX[("c", par)][:, j * 128:(j + 1) * 128]
                        pp_ = PEX[("p", par)][:, j * 128:(j + 1) * 128]
                        mm(po[pr, j, :], Vc[:, kvh * 64:(kvh + 1) * 64], pc_, start=True, stop=False)
                        mm(po[pr, j, :], Vp[:, kvh * 64:(kvh + 1) * 64], pp_, start=False, stop=True)
                        mm(pd[pr, j, :], ones[:, 0:64], pc_, start=True, stop=False)
                        mm(pd[pr, j, :], ones[:, 0:64], pp_, start=False, stop=True)
                den = [fT.get(), fT.get()]
                for j in range(4):
                    ts("dve", den[j // 2][:, (j % 2) * 128:(j % 2 + 1) * 128], pd[:, j, :], esink[:, j:j + 1], None, OP.add)
                for hf2 in range(2):
                    rden = fT.get()
                    S.op("dve", lambda e, o=rden[:], i=den[hf2][:]: e.reciprocal(o, i), [den[hf2][:]], [rden[:]])
                    tt("dve", YB[:, 2 * hf2:2 * hf2 + 2, qs], po[:, 2 * hf2:2 * hf2 + 2, :],
                       rden[:].rearrange("p (j t) -> p j t", j=2), OP.mult)
            for j in range(4):
                wt = wtile(w_in_v, 2944 + j * 128)
                ps = proj(hT, wt, slice(0, 128))
                sz = silu2(ps)
                stt(YB[:, j, :], YB[:, j, :], 0.5, sz[:], OP.mult, OP.mult)
            for hd in range(4):
                wt = wtile(w_in_v, 3456 + hd * 128)
                ps = proj(hT, wt, slice(0, 128))
                qx = bT.get()
                head_norm_rope(ps, PCc("xqg"), qx[:], 128, do_rope=False)
                pes = []
                for mt in range(2):
                    pss = PS()
                    mm(pss[:, 0:TG], KmT[:, hd, mt * 128:(mt + 1) * 128], qx[:])
                    pe_ = bT.get()
                    act(pe_[:], pss[:, 0:TG], AF.Exp, scale=float(128 ** -0.5))
                    pes.append(pe_)
                pso = PS()
                psd = PS()
                for mt in range(2):
                    mm(pso[:, 0:TG], Vmem[:, mt, hd * 128:(hd + 1) * 128], pes[mt][:], start=(mt == 0), stop=(mt == 1))
                for mt in range(2):
                    mm(psd[:, 0:TG], ones, pes[mt][:], start=(mt == 0), stop=(mt == 1))
                rden = fT.get()
                S.op("dve", lambda e, o=rden[:], i=psd[:, 0:TG]: e.reciprocal(o, i), [psd[:, 0:TG]], [rden[:]])
                oc_ = fT.get()
                tt("dve", oc_[:], pso[:, 0:TG], rden[:], OP.mult)
                wt = wtile(w_in_v, 3968 + hd * 128)
                ps = proj(hT, wt, slice(0, 128))
                sz = silu2(ps)
                stt(YC[:, hd, :], oc_[:], 0.5, sz[:], OP.mult, OP.mult)
            for dt_ in range(8):
                acc = None
                for br, Yb in enumerate((YA, YB, YC)):
                    wt = wtile(w_in_v, 4480 + br * 1024 + dt_ * 128)
                    psg = proj(hT, wt, slice(0, 128))
                    gt = bT.get()
                    act(gt[:], psg, AF.Tanh, scale=0.5)
                    wpt = wtile(wp_v[br], dt_ * 128, nk=4)
                    psp = PS()
                    for kt in range(4):
                        mm(psp[:, 0:TG], wpt[:, kt, :], Yb[:, kt, :], start=(kt == 0), stop=(kt == 3))
                    tmp = fT.get()
                    stt(tmp[:], gt[:], 1.0, psp[:, 0:TG], OP.add, OP.mult)
                    if acc is None:
                        acc = tmp
                    elif br == 2:
                        tt("dve", MG[:, dt_, :], acc[:], tmp[:], OP.add)
                    else:
                        acc2 = fT.get()
                        tt("dve", acc2[:], acc[:], tmp[:], OP.add)
                        acc = acc2
            for t4 in range(NT4):
                xr = xin.get()
                r0 = og * TG + t4 * 128
                S.dma("sp", xr[:], xown[r0:r0 + 128, :])
                for ct in range(8):
                    wot = wtile(w_o_v, ct * 128)
                    ps = PS()
                    for kt in range(8):
                        mm(ps[:, 0:128], MG[:, kt, t4 * 128:(t4 + 1) * 128], wot[:, kt, :], start=(kt == 0), stop=(kt == 7))
                    stt(xr[:, ct * 128:(ct + 1) * 128], ps[:, 0:128], 0.5, xr[:, ct * 128:(ct + 1) * 128], OP.mult, OP.add)
                S.dma("sp", yout[r0:r0 + 128, :], xr[:], is_output=True)

        def rec(*fns):
            S.rec_start()
            for f in fns:
                f()
            return S.rec_stop()

        S.play(rec(lambda: XC(0), lambda: XH(0, 0)))
        for g in range(n_groups):
            S.fill_on = FILL_ALL or g > G_OWN0
            S.play(S.schedule(rec(lambda: YH(g, 0), lambda: XH(g, 1))))
            S.fill_on = FILL_ALL or g >= G_OWN0
            fns = [lambda: YH(g, 1), lambda: YE(g)]
            if g + 1 < n_groups:
                fns += [lambda: XC(g + 1), lambda: XH(g + 1, 0)]
            S.play(S.schedule(rec(*fns)))

        if dbg:
            print('SBUF bytes/partition', _sbytes[0], 'counts', S.cnt); print('tagcost us', {k: (v[0], round(v[1] / 1000)) for k, v in sorted(S.tagcost.items())}); print('model time us', max(S.etime.values()) / 1000, 'fillers', S.nfill); print('PE cols by tag', {k: (v[0], v[1], round(v[1] / 1.2 / 1000)) for k, v in PEC.items()}); print(sorted(_sblist, key=lambda t: -t[0]))
        S.finish()
        S.emit()
    return nc


def _host_inputs(inputs):
    f = lambda a: np.ascontiguousarray(np.asarray(a))
    x = f(inputs["x"])
    mem = f(inputs["mem"])
    pos = f(inputs["positions"])
    w_in = f(inputs["w_in"][0])
    cm, cf = _consts()
    sk = w_in[:, 2688:2816]
    w_skd = np.ascontiguousarray(np.concatenate([sk[:, 0:64], sk[:, 0:64], sk[:, 64:128], sk[:, 64:128]], axis=1))
    w2a2 = np.ascontiguousarray(np.concatenate([inputs["w2"][0], inputs["a2"][0]], axis=0)).astype(np.float32)

    def c4(v):
        return np.asarray(v, np.float32).reshape(4, 128).T

    def dup(v, n):
        return np.tile(np.asarray(v, np.float32).reshape(-1), n).reshape(128, 1)
    shared = dict(
        mem=None, w_in=w_in, w_skd=w_skd, w_mkv=f(inputs["w_mem_kv"][0]), w_pa=f(inputs["w_proj_a"][0]),
        w_pb=f(inputs["w_proj_b"][0]), w_pc=f(inputs["w_proj_c"][0]), w_o=f(inputs["w_out"][0]), w2a2=w2a2,
        g_row=f(inputs["norm_g"]).reshape(1, D), gm_row=f(inputs["mem_norm_g"]).reshape(1, D),
        muv_row=f(inputs["mu_rkv"][0, 2]).reshape(1, 512), cm=cm, cf=cf)
    invf = (np.float32(10000.0) ** (-(np.arange(32, dtype=np.float32) / np.float32(32)))).astype(np.float32)
    maps = []
    for c in range(NCORES):
        b, q = c // 4, c % 4
        end = OWN * (q + 1)
        xw = np.zeros((T, D), np.float32)
        xw[T - end:] = x[b, :end]
        pp = np.zeros((1, OWN + 128), np.int32)
        s0 = end - OWN - 128
        if s0 >= 0:
            pp[0] = pos[b, s0:end]
        else:
            pp[0, 128:] = pos[b, 0:end]
        pcv = np.zeros((128, NPC), np.float32)

        def put(name, a):
            o, n = PCN[name]
            pcv[:, o:o + n] = a
        put("mu_r", c4(inputs["mu_rkv"][0, 0]))
        put("mu_k", c4(inputs["mu_rkv"][0, 1]))
        put("mu_l", np.concatenate([inputs["mu_wa"][0, 0], inputs["mu_wa"][0, 1]]).reshape(128, 1))
        put("w0", c4(inputs["w0"][0]))
        put("a0", c4(inputs["a0"][0]))
        put("k_k", c4(inputs["k_k"][0]))
        put("k_a", c4(inputs["k_a"][0]))
        put("r_k", c4(np.asarray(inputs["r_k"][0]).reshape(-1)))
        put("lnx_g", c4(inputs["lnx_g"][0]))
        put("lnx_b", c4(inputs["lnx_b"][0]))
        put("qg", dup(inputs["q_norm_g"][0], 2))
        put("kg", dup(inputs["k_norm_g"][0], 2))
        put("sink", np.repeat(np.asarray(inputs["sinks"][0], np.float32).reshape(4, 2).T, 64, axis=0))
        put("xqg", np.asarray(inputs["xq_norm_g"][0], np.float32).reshape(128, 1))
        put("xkg", np.asarray(inputs["xk_norm_g"][0], np.float32).reshape(128, 1))
        put("invf", np.tile(invf, 4).reshape(128, 1))
        put("fm", np.full((128, 1), 0.0 if q == 0 else 1.0, np.float32))
        m = dict(shared)
        m.update(xw=xw, xown=np.ascontiguousarray(x[b, end - OWN:end]), pos=pp, mem=mem[b], pc=pcv)
        maps.append(m)
    return maps


_NC_CACHE = {}


def kernel(**inputs):
    inputs = {k: np.asarray(v) for k, v in inputs.items()}
    if "nc" not in _NC_CACHE:
        _NC_CACHE["nc"] = build_program()
    nc = _NC_CACHE["nc"]
    maps = _host_inputs(inputs)
    res = run_bass_kernel_spmd(nc, maps, core_ids=list(range(NCORES)))
    out = np.zeros((2, T, D), np.float32)
    for c in range(NCORES):
        b, q = c // 4, c % 4
        out[b, q * OWN:(q + 1) * OWN] = res.results[c]["y"]
    return out
```
